# Optimizing a Trainium2 kernel written in Bass

```python
import math
import jax, jax.numpy as jnp
from jax import lax
import numpy as np

D_MODEL = 1024
BATCH = 8
SEQ = 2048
DEPTH = 4
DEC_BATCH = 128
DEC_SEQ = 4
PAST_LEN = 16384
PAGE_SIZE = 128

DN_HEADS = 4
DN_DK = 128
DN_DV = 128
DN_KEY_WIDTH = DN_HEADS * DN_DK
DN_VAL_WIDTH = DN_HEADS * DN_DV
DN_QKV = 2 * DN_KEY_WIDTH + DN_VAL_WIDTH
DN_CONV = 4
DN_CHUNK = 64
LRU_WIDTH = D_MODEL // 2
LRU_BLOCKS = 4
LRU_BW = LRU_WIDTH // LRU_BLOCKS
LRU_CONV = 4
LRU_C = 8.0
D_FF = 3 * D_MODEL
FFN_CONV = 3
NORM_EPS = 1e-6
SPLIT_SIZES = (DN_QKV, DN_VAL_WIDTH, DN_HEADS, DN_HEADS, LRU_WIDTH, LRU_WIDTH, D_MODEL, D_MODEL)
IN_COLS = sum(SPLIT_SIZES)
SPLIT_POINTS = [int(v) for v in np.cumsum(SPLIT_SIZES)[:-1]]

kernel_name = "hybrid_gdn_rglru_convffn_step"


def rmsnorm(x, w):
    xf = x.astype(jnp.float32)
    y = xf * lax.rsqrt(jnp.mean(xf * xf, axis=-1, keepdims=True) + NORM_EPS)
    return (y * w.astype(jnp.float32)).astype(x.dtype)


def l2norm(x):
    return x * lax.rsqrt(jnp.sum(x * x, axis=-1, keepdims=True) + NORM_EPS)


def causal_dwconv(x, buf, w):
    width = w.shape[0]
    seq = x.shape[1]
    xp = jnp.concatenate([buf.astype(x.dtype), x], axis=1)
    y = xp[:, 0:seq] * w[0]
    for j in range(1, width):
        y = y + xp[:, j:j + seq] * w[j]
    return y, xp[:, xp.shape[1] - (width - 1):]


def gated_delta_rule(q, k, v, g, beta, s0):
    bsz, seq, nh, dk = q.shape
    dv = v.shape[-1]
    c = min(DN_CHUNK, seq)
    n = -(-seq // c)
    pad = n * c - seq

    def blocks(t):
        t = jnp.pad(t, [(0, 0), (0, pad)] + [(0, 0)] * (t.ndim - 2))
        t = t.reshape((bsz, n, c) + t.shape[2:])
        return jnp.swapaxes(t, 2, 3)

    q, k, v, g, beta = (blocks(t) for t in (q * (dk ** -0.5), k, v, g, beta))
    gc = jnp.cumsum(g, axis=-1)
    idx = jnp.arange(c)
    incl = idx[:, None] >= idx[None, :]
    strict = idx[:, None] > idx[None, :]
    diff = gc[..., :, None] - gc[..., None, :]
    decay = jnp.where(incl, jnp.exp(jnp.where(incl, diff, 0.0)), 0.0)
    kb = k * beta[..., None]
    vb = v * beta[..., None]
    m = jnp.where(strict, jnp.einsum('bnhid,bnhjd->bnhij', kb, k) * decay, 0.0)
    eye = jnp.eye(c, dtype=m.dtype)
    rhs = jnp.concatenate([vb, kb * jnp.exp(gc)[..., None]], axis=-1)
    sol = lax.linalg.triangular_solve(m + eye, rhs, left_side=True, lower=True, unit_diagonal=True)
    u, w = sol[..., :dv], sol[..., dv:]
    attn = jnp.einsum('bnhid,bnhjd->bnhij', q, k) * decay
    qg = q * jnp.exp(gc)[..., None]
    kd = k * jnp.exp(gc[..., -1:] - gc)[..., None]
    glast = jnp.exp(gc[..., -1])

    def step(s, xs):
        u_i, w_i, qg_i, attn_i, kd_i, gl_i = xs
        v_new = u_i - jnp.einsum('bhck,bhkv->bhcv', w_i, s)
        o_i = jnp.einsum('bhck,bhkv->bhcv', qg_i, s) + jnp.einsum('bhij,bhjv->bhiv', attn_i, v_new)
        s = s * gl_i[..., None, None] + jnp.einsum('bhck,bhcv->bhkv', kd_i, v_new)
        return s, o_i

    xs = tuple(jnp.moveaxis(t, 1, 0) for t in (u, w, qg, attn, kd, glast))
    s_final, o = lax.scan(step, s0, xs)
    o = jnp.transpose(o, (1, 0, 3, 2, 4)).reshape(bsz, n * c, nh, dv)[:, :seq]
    return o, s_final


def _linear_combine(left, right):
    a_l, b_l = left
    a_r, b_r = right
    return a_l * a_r, a_r * b_l + b_r


def rg_lru(xb, h0, wa, ba, wx, bx, lam, start_pos):
    f32 = jnp.float32
    bsz, seq, width = xb.shape
    xf = xb.astype(f32)
    xblk = xf.reshape(bsz, seq, LRU_BLOCKS, LRU_BW)
    r = jax.nn.sigmoid(jnp.einsum('blnc,ncd->blnd', xblk, wa.astype(f32)).reshape(bsz, seq, width) + ba.astype(f32))
    i = jax.nn.sigmoid(jnp.einsum('blnc,ncd->blnd', xblk, wx.astype(f32)).reshape(bsz, seq, width) + bx.astype(f32))
    log_a = -LRU_C * r * jax.nn.softplus(-lam.astype(f32))
    a = jnp.exp(log_a)
    mult = jnp.sqrt(-jnp.expm1(2.0 * log_a))
    pos = start_pos + jnp.arange(seq)
    mult = jnp.where((pos == 0)[None, :, None], 1.0, mult)
    b = mult * i * xf
    b = jnp.concatenate([b[:, :1] + a[:, :1] * h0.astype(f32)[:, None], b[:, 1:]], axis=1)
    _, h = lax.associative_scan(_linear_combine, (a, b), axis=1)
    return h, h[:, -1]


def hybrid_layer(x, dn_buf, dn_s, lru_buf, lru_h, ffn_buf, lw, start_pos):
    (norm1_w, w_in, dn_conv_w, dn_A_log, dn_dt_bias, dn_norm_w, lru_conv_w, lru_conv_b,
     lru_wa, lru_ba, lru_wx, lru_bx, lru_lambda, w_branch_a, w_branch_b, w_o,
     norm2_w, ffn_w_in, ffn_conv_w, ffn_w_down) = lw
    f32 = jnp.float32
    bsz, seq, _ = x.shape
    h = rmsnorm(x, norm1_w)
    proj = jnp.einsum('bld,dc->blc', h, w_in)
    qkv, z, a_in, b_in, lru_x, lru_y, gate_a, gate_b = jnp.split(proj, SPLIT_POINTS, axis=-1)

    qkv, new_dn_buf = causal_dwconv(qkv, dn_buf, dn_conv_w)
    qkv = jax.nn.silu(qkv).astype(f32)
    q, k, v = jnp.split(qkv, [DN_KEY_WIDTH, 2 * DN_KEY_WIDTH], axis=-1)
    q = l2norm(q.reshape(bsz, seq, DN_HEADS, DN_DK))
    k = l2norm(k.reshape(bsz, seq, DN_HEADS, DN_DK))
    v = v.reshape(bsz, seq, DN_HEADS, DN_DV)
    beta = jax.nn.sigmoid(b_in.astype(f32))
    g = -jnp.exp(dn_A_log.astype(f32)) * jax.nn.softplus(a_in.astype(f32) + dn_dt_bias.astype(f32))
    o, new_s = gated_delta_rule(q, k, v, g, beta, dn_s.astype(f32))
    zf = z.astype(f32).reshape(bsz, seq, DN_HEADS, DN_DV)
    o = rmsnorm(o, dn_norm_w) * jax.nn.silu(zf)
    o_a = o.reshape(bsz, seq, DN_VAL_WIDTH).astype(x.dtype)

    xc, new_lru_buf = causal_dwconv(lru_x, lru_buf, lru_conv_w)
    xc = xc + lru_conv_b
    hseq, new_h = rg_lru(xc, lru_h, lru_wa, lru_ba, lru_wx, lru_bx, lru_lambda, start_pos)
    o_b = (hseq * jax.nn.gelu(lru_y.astype(f32), approximate=True)).astype(x.dtype)

    br_a = jnp.einsum('blc,cd->bld', o_a, w_branch_a)
    br_b = jnp.einsum('blc,cd->bld', o_b, w_branch_b)
    merged = jax.nn.sigmoid(gate_a) * br_a + jax.nn.sigmoid(gate_b) * br_b
    x = x + jnp.einsum('bld,de->ble', merged, w_o)

    h2 = rmsnorm(x, norm2_w)
    gu = jnp.einsum('bld,df->blf', h2, ffn_w_in)
    gate, up = jnp.split(gu, [D_FF], axis=-1)
    gate, new_ffn_buf = causal_dwconv(gate, ffn_buf, ffn_conv_w)
    x = x + jnp.einsum('blf,fd->bld', jax.nn.gelu(gate, approximate=True) * up, ffn_w_down)
    return (x, new_dn_buf, new_s.astype(dn_s.dtype), new_lru_buf, new_h.astype(lru_h.dtype), new_ffn_buf)


def run_trunk(x, dn_buf, dn_s, lru_buf, lru_h, ffn_buf, weights, final_norm_w, start_pos):
    outs = ([], [], [], [], [])
    for l in range(DEPTH):
        lw = tuple(w[l] for w in weights)
        x, *new = hybrid_layer(x, dn_buf[l], dn_s[l], lru_buf[l], lru_h[l], ffn_buf[l], lw, start_pos)
        for lst, s in zip(outs, new):
            lst.append(s)
    y = rmsnorm(x, final_norm_w)
    return (y, jnp.stack(outs[0]), jnp.stack(outs[1]), jnp.stack(outs[2]), jnp.stack(outs[3]), jnp.stack(outs[4]))


def setup_inputs(seed: int = 0) -> dict:
    key = jax.random.key(seed)
    ks = jax.random.split(key, 32)
    f32 = jnp.float32

    def nrm(k, shape, scale):
        return jax.random.normal(k, shape, f32) * scale

    dt = jnp.exp(jax.random.uniform(ks[11], (DEPTH, DN_HEADS), f32, math.log(1e-3), math.log(1e-1)))
    u = jax.random.uniform(ks[19], (DEPTH, LRU_WIDTH), f32, 0.9, 0.999)
    s = u ** (1.0 / LRU_C)
    return {
        'x_prompt': nrm(ks[0], (BATCH, SEQ, D_MODEL), 1.0),
        'x_sample': nrm(ks[1], (DEC_BATCH, DEC_SEQ, D_MODEL), 1.0),
        'state_dn_conv': nrm(ks[2], (DEPTH, DEC_BATCH, DN_CONV - 1, DN_QKV), 1.0),
        'state_dn': nrm(ks[3], (DEPTH, DEC_BATCH, DN_HEADS, DN_DK, DN_DV), 0.3),
        'state_lru_conv': nrm(ks[4], (DEPTH, DEC_BATCH, LRU_CONV - 1, LRU_WIDTH), 1.0),
        'state_lru': nrm(ks[5], (DEPTH, DEC_BATCH, LRU_WIDTH), 1.0),
        'state_ffn_conv': nrm(ks[6], (DEPTH, DEC_BATCH, FFN_CONV - 1, D_FF), 1.0),
        'norm1_w': 1.0 + nrm(ks[7], (DEPTH, D_MODEL), 0.02),
        'w_in': nrm(ks[8], (DEPTH, D_MODEL, IN_COLS), D_MODEL ** -0.5),
        'dn_conv_w': nrm(ks[9], (DEPTH, DN_CONV, DN_QKV), DN_CONV ** -0.5),
        'dn_A_log': jnp.log(jax.random.uniform(ks[10], (DEPTH, DN_HEADS), f32, 1.0, 16.0)),
        'dn_dt_bias': dt + jnp.log(-jnp.expm1(-dt)),
        'dn_norm_w': 1.0 + nrm(ks[12], (DEPTH, DN_DV), 0.02),
        'lru_conv_w': nrm(ks[13], (DEPTH, LRU_CONV, LRU_WIDTH), LRU_CONV ** -0.5),
        'lru_conv_b': nrm(ks[14], (DEPTH, LRU_WIDTH), 0.02),
        'lru_wa': nrm(ks[15], (DEPTH, LRU_BLOCKS, LRU_BW, LRU_BW), LRU_BW ** -0.5),
        'lru_ba': nrm(ks[16], (DEPTH, LRU_WIDTH), 0.02),
        'lru_wx': nrm(ks[17], (DEPTH, LRU_BLOCKS, LRU_BW, LRU_BW), LRU_BW ** -0.5),
        'lru_bx': nrm(ks[18], (DEPTH, LRU_WIDTH), 0.02),
        'lru_lambda': jnp.log(s) - jnp.log1p(-s),
        'w_branch_a': nrm(ks[20], (DEPTH, DN_VAL_WIDTH, D_MODEL), DN_VAL_WIDTH ** -0.5),
        'w_branch_b': nrm(ks[21], (DEPTH, LRU_WIDTH, D_MODEL), LRU_WIDTH ** -0.5),
        'w_o': nrm(ks[22], (DEPTH, D_MODEL, D_MODEL), D_MODEL ** -0.5),
        'norm2_w': 1.0 + nrm(ks[23], (DEPTH, D_MODEL), 0.02),
        'ffn_w_in': nrm(ks[24], (DEPTH, D_MODEL, 2 * D_FF), D_MODEL ** -0.5),
        'ffn_conv_w': nrm(ks[25], (DEPTH, FFN_CONV, D_FF), FFN_CONV ** -0.5),
        'ffn_w_down': nrm(ks[26], (DEPTH, D_FF, D_MODEL), D_FF ** -0.5),
        'final_norm_w': 1.0 + nrm(ks[27], (D_MODEL,), 0.02),
    }


def reference(x_prompt, x_sample, state_dn_conv, state_dn, state_lru_conv, state_lru, state_ffn_conv,
              norm1_w, w_in, dn_conv_w, dn_A_log, dn_dt_bias, dn_norm_w, lru_conv_w, lru_conv_b,
              lru_wa, lru_ba, lru_wx, lru_bx, lru_lambda, w_branch_a, w_branch_b, w_o,
              norm2_w, ffn_w_in, ffn_conv_w, ffn_w_down, final_norm_w):
    weights = (norm1_w, w_in, dn_conv_w, dn_A_log, dn_dt_bias, dn_norm_w, lru_conv_w, lru_conv_b,
               lru_wa, lru_ba, lru_wx, lru_bx, lru_lambda, w_branch_a, w_branch_b, w_o,
               norm2_w, ffn_w_in, ffn_conv_w, ffn_w_down)
    lead = (DEPTH, BATCH)
    z_dn_conv = jnp.zeros(lead + state_dn_conv.shape[2:], state_dn_conv.dtype)
    z_dn = jnp.zeros(lead + state_dn.shape[2:], state_dn.dtype)
    z_lru_conv = jnp.zeros(lead + state_lru_conv.shape[2:], state_lru_conv.dtype)
    z_lru = jnp.zeros(lead + state_lru.shape[2:], state_lru.dtype)
    z_ffn_conv = jnp.zeros(lead + state_ffn_conv.shape[2:], state_ffn_conv.dtype)
    y_prompt, p_dn_conv, p_dn, p_lru_conv, p_lru, p_ffn_conv = run_trunk(
        x_prompt, z_dn_conv, z_dn, z_lru_conv, z_lru, z_ffn_conv, weights, final_norm_w, 0)
    y_sample, s_dn_conv, s_dn, s_lru_conv, s_lru, s_ffn_conv = run_trunk(
        x_sample, state_dn_conv, state_dn, state_lru_conv, state_lru, state_ffn_conv, weights, final_norm_w, PAST_LEN)
    return (y_prompt, y_sample, p_dn_conv, p_dn, p_lru_conv, p_lru, p_ffn_conv,
            s_dn_conv, s_dn, s_lru_conv, s_lru, s_ffn_conv)
```

```python
import contextlib
import os
import numpy as np
import concourse.bass as bass
import concourse.mybir as mybir
from concourse.bass_utils import run_bass_kernel_spmd

F32 = mybir.dt.float32
BF16 = mybir.dt.bfloat16
AF = mybir.ActivationFunctionType
ALU = mybir.AluOpType

ENGS = ("pe", "act", "dve", "pool", "sp")

DEPTH = 4
D = 1024
T = 2112
TP = 2048
NS = 16
INC = 5128
DFF = 3072
EPS = 1e-6
import os
KSTOP = int(os.environ.get("KSTOP", "99"))
OQ = os.environ.get("KOQ", "sp")
CPL = 169
CP_FIN = 4 * CPL
CP_AB = CP_FIN + 8
CP_N = CP_AB + 32
CM_ID = 0
CM_MU = 128
CM_PL = 256
CM_UC = 384
CM_MUS = 512
CM_PLS = 576
CM_UCS = 640
CM_OBS = 704
CM_RM = 768
CM_SEL = 784
CM_N = 800

PASS_TILES = [[(0, 512), (512, 512)], [(1024, 512), (1536, 512), (2048, 64)]]
PASS_BASE = [0, 1024]
ALL_TILES = [(0, 512), (512, 512), (1024, 512), (1536, 512), (2048, 64)]
HT = 1088


class Op:
    __slots__ = ("eng", "fn", "waits", "flag", "idx", "dma", "dsem", "dval")

    def __init__(self, eng, fn):
        self.eng = eng
        self.fn = fn
        self.waits = []
        self.flag = False
        self.idx = -1
        self.dma = False
        self.dsem = None
        self.dval = 0


class Prog:
    def __init__(self, nc, n_dma_sems=int(os.environ.get("KNS", "32")), dry=False):
        self.nc = nc
        self.dry = dry
        self.streams = {e: [] for e in ENGS}
        if not dry:
            self.esem = {e: nc.alloc_semaphore(name="s_" + e) for e in ENGS}
            self.dsems = [nc.alloc_semaphore(name="d%d" % i) for i in range(n_dma_sems)]
        else:
            self.esem = {e: None for e in ENGS}
            self.dsems = [i for i in range(n_dma_sems)]
        self.dlast = [None] * n_dma_sems
        self.dvals = [0] * n_dma_sems
        self.dnext = 0
        self.last_w = {}
        self.readers = {}
        self.waited = {}
        self.dwaited = {}
        self.nops = 0

    def _add_wait(self, op, dep):
        f = op.eng
        if dep.dma:
            k = (f, id(dep.dsem) if not self.dry else dep.dsem)
            if self.dwaited.get(k, 0) >= dep.dval:
                return
            self.dwaited[k] = dep.dval
            op.waits.append(dep)
        else:
            if dep.eng == f and False:
                return
            k = (f, dep.eng)
            if self.waited.get(k, -1) >= dep.idx:
                return
            self.waited[k] = dep.idx
            dep.flag = True
            op.waits.append(dep)

    def _deps(self, op, reads, writes):
        for k in reads:
            w = self.last_w.get(k)
            if w is not None and (w.dma or w.eng != op.eng or op.eng != "pe"):
                self._add_wait(op, w)
            if k.startswith("ps"):
                for r in self.readers.get(k, ()):
                    if r.eng != op.eng:
                        self._add_wait(op, r)
        for k in writes:
            w = self.last_w.get(k)
            if w is not None and (w.dma or w.eng != op.eng or op.eng == "pool"):
                self._add_wait(op, w)
            for r in self.readers.get(k, ()):
                if r is not op and (r.dma or r.eng != op.eng or op.eng == "pool"):
                    self._add_wait(op, r)
        for k in reads:
            self.readers.setdefault(k, []).append(op)
        for k in writes:
            self.last_w[k] = op
            self.readers[k] = []

    def op(self, eng, fn, reads=(), writes=()):
        o = Op(eng, fn)
        o.idx = len(self.streams[eng])
        self._deps(o, reads, writes)
        self.streams[eng].append(o)
        self.nops += 1
        return o

    def dma(self, eng, out, in_, reads=(), writes=()):
        o = Op(eng, None)
        o.dma = True
        o.idx = len(self.streams[eng])
        si = self.dnext
        self.dnext = (self.dnext + 1) % len(self.dsems)
        prev = self.dlast[si]
        if prev is not None:
            self._add_wait(o, prev)
        self._deps(o, reads, writes)
        o.dsem = self.dsems[si]
        self.dvals[si] += 16
        o.dval = self.dvals[si]
        self.dlast[si] = o
        o.fn = (out, in_)
        self.streams[eng].append(o)
        self.nops += 1
        return o

    def fence(self):
        lasts = {}
        for en in ENGS:
            for o in reversed(self.streams[en]):
                if not o.dma and o.fn is not None:
                    lasts[en] = o
                    break
        dl = [d for d in self.dlast if d is not None]
        for en in ENGS:
            o = Op(en, None)
            o.idx = len(self.streams[en])
            for en2, l in lasts.items():
                if en2 != en:
                    self._add_wait(o, l)
            for d in dl:
                self._add_wait(o, d)
            self.streams[en].append(o)
        self.last_w = {}
        self.readers = {}

    def final_wait_all(self, ename="sp"):
        o = Op(ename, None)
        o.idx = len(self.streams[ename])
        for prev in self.dlast:
            if prev is not None:
                self._add_wait(o, prev)
        for en in ENGS:
            if en != ename:
                for l in reversed(self.streams[en]):
                    if not l.dma and l.fn is not None:
                        self._add_wait(o, l)
                        break
        self.streams[ename].append(o)

    def prepass(self):
        for ename in ENGS:
            cnt = 0
            for o in self.streams[ename]:
                if not o.dma and o.flag:
                    cnt += 1
                    o.dval = cnt

    def emit(self, ename, e):
        for o in self.streams[ename]:
            for d in o.waits:
                if d.dma:
                    e.wait_ge(d.dsem, d.dval)
                else:
                    e.wait_ge(self.esem[d.eng], d.dval)
            if o.dma:
                out, in_ = o.fn
                e.dma_start(out=out, in_=in_).then_inc(o.dsem, 16)
            elif o.fn is not None:
                ins = o.fn(e)
                if o.flag:
                    ins.then_inc(self.esem[ename], 1)
            else:
                assert not o.flag

    def run(self):
        nc = self.nc
        self.prepass()
        with nc.Block() as block:
            @block.tensor
            def _(e):
                self.emit("pe", e)

            @block.scalar
            def _(e):
                self.emit("act", e)

            @block.vector
            def _(e):
                self.emit("dve", e)

            @block.gpsimd
            def _(e):
                self.emit("pool", e)

            @block.sync
            def _(e):
                self.emit("sp", e)


class V:
    __slots__ = ("ap", "key")

    def __init__(self, ap, key):
        self.ap = ap
        self.key = key

    def __getitem__(self, idx):
        return V(self.ap[idx], self.key)

    def k(self, key):
        return V(self.ap, key)


class Builder:
    def __init__(self, nc, P, dram, tens, plan, depth):
        self.nc = nc
        self.P = P
        self.d = dram
        self.t = tens
        self.depth = depth
        self.plan = plan
        self.rec = []
        self.wpos = 0
        self.wissued = 0
        self.dma_next = 0
        self.cast_next = 0
        self.uid = 0
        self.big_i = 0
        self.sm_i = 0
        self.scr_off = 0
        self.scr_phase = 0

    def _rk(self, *vs):
        out = []
        for v in vs:
            if isinstance(v, V):
                if isinstance(v.key, tuple):
                    out.extend(v.key)
                else:
                    out.append(v.key)
        return out

    def _wk(self, *vs):
        return self._rk(*vs)

    def mm(self, out, lhsT, rhs, start=True, stop=True):
        self.P.op("pe", lambda e: e.matmul(out.ap, lhsT=lhsT.ap, rhs=rhs.ap, start=start, stop=stop),
                  reads=self._rk(lhsT, rhs), writes=self._wk(out))

    def tr(self, out, in_, np_):
        idn = self.t["cm"][0:np_, CM_ID:CM_ID + np_]
        shp = tuple(in_.ap.shape)
        if len(shp) == 2 and shp[0] == 128 and shp[1] == 128:
            self.P.op("pe", lambda e: e.transpose(out.ap, in_.ap, idn.ap),
                      reads=self._rk(in_, idn), writes=self._wk(out))
        else:
            self.P.op("pe", lambda e: e.matmul(out.ap, lhsT=in_.ap, rhs=idn.ap, start=True, stop=True),
                      reads=self._rk(in_, idn), writes=self._wk(out))

    def act(self, out, in_, func, bias=None, scale=None, accum=None):
        kw = {}
        rd = [in_]
        if bias is not None:
            kw["bias"] = bias.ap if isinstance(bias, V) else bias
            rd.append(bias)
        if scale is not None:
            kw["scale"] = scale.ap if isinstance(scale, V) else scale
            rd.append(scale)
        wr = self._wk(out)
        if accum is not None:
            kw["accum_out"] = accum.ap
            wr = wr + self._wk(accum)
        self.P.op("act", lambda e: e.activation(out=out.ap, in_=in_.ap, func=func, **kw),
                  reads=self._rk(*rd), writes=wr)

    def ts(self, eng, out, in0, s1, op0, s2=None, op1=None):
        a1 = s1.ap if isinstance(s1, V) else s1
        a2 = s2.ap if isinstance(s2, V) else s2
        if op1 is None:
            fn = lambda e: e.tensor_scalar(out=out.ap, in0=in0.ap, scalar1=a1, scalar2=None, op0=op0)
        else:
            fn = lambda e: e.tensor_scalar(out=out.ap, in0=in0.ap, scalar1=a1, scalar2=a2, op0=op0, op1=op1)
        self.P.op(eng, fn, reads=self._rk(in0, s1, s2), writes=self._wk(out))

    def stt(self, out, in0, scalar, in1, op0, op1):
        a = scalar.ap if isinstance(scalar, V) else scalar
        self.P.op("dve", lambda e: e.scalar_tensor_tensor(out=out.ap, in0=in0.ap, scalar=a, in1=in1.ap,
                                                          op0=op0, op1=op1),
                  reads=self._rk(in0, scalar, in1), writes=self._wk(out))

    def tt(self, eng, out, in0, in1, op):
        self.P.op(eng, lambda e: e.tensor_tensor(out=out.ap, in0=in0.ap, in1=in1.ap, op=op),
                  reads=self._rk(in0, in1), writes=self._wk(out))

    def cp(self, eng, out, in_):
        if eng == "act":
            self.act(out, in_, AF.Copy)
        else:
            self.P.op(eng, lambda e: e.tensor_copy(out=out.ap, in_=in_.ap), reads=self._rk(in_), writes=self._wk(out))

    def memset(self, eng, out, val):
        self.P.op(eng, lambda e: e.memset(out.ap, val), writes=self._wk(out))

    def recip(self, out, in_):
        self.P.op("dve", lambda e: e.reciprocal(out=out.ap, in_=in_.ap), reads=self._rk(in_), writes=self._wk(out))

    def scan(self, out, d0, d1, init):
        a = init.ap if isinstance(init, V) else init
        self.P.op("dve", lambda e: e.tensor_tensor_scan(out=out.ap, data0=d0.ap, data1=d1.ap, initial=a,
                                                        op0=ALU.mult, op1=ALU.add),
                  reads=self._rk(d0, d1, init), writes=self._wk(out))

    def dma(self, eng, out_ap, in_ap, reads=(), writes=()):
        def fl(ks):
            o = []
            for k in ks:
                if isinstance(k, tuple):
                    o.extend(k)
                else:
                    o.append(k)
            return o
        self.P.dma(eng, out_ap, in_ap, reads=fl(reads), writes=fl(writes))

    def psbig(self):
        i = self.big_i
        self.big_i = (i + 1) % 4
        return V(self.t["psb"][i][:, :], "psb%d" % i)

    def pssm(self):
        i = self.sm_i
        self.sm_i = (i + 1) % 16
        q, b = divmod(i, 4)
        return V(self.t["pss"][b][:, q * 128:(q + 1) * 128], "pss%d" % b)

    def new_phase(self, segs=("SCR",)):
        self.P.fence()
        self.scr_off = 0
        self.scr_phase += 1
        self.segs = []
        o = 0
        for nm in segs:
            ap = self.t[nm]
            n = ap.shape[1]
            self.segs.append((o, n, ap))
            o += n
        self.scr_total = o

    def scr(self, cols, dt=F32, parts=128):
        n32 = cols if dt == F32 else (cols + 1) // 2
        n32 = (n32 + 7) // 8 * 8
        off = self.scr_off
        if n32 >= 128:
            n32 = (n32 + 127) // 128 * 128
            off = (off + 127) // 128 * 128
        seg = None
        for (o, n, ap) in self.segs:
            if off < o:
                off = o
            if off >= o and off + n32 <= o + n:
                seg = (o, n, ap)
                break
        assert seg is not None, ("scratch overflow", self.scr_off, n32, self.scr_total)
        self.scr_off = off + n32
        self.scr_peak = max(getattr(self, "scr_peak", 0), self.scr_off)
        o, n, sap = seg
        ap = sap[0:parts, off - o:off - o + n32]
        if dt != F32:
            ap = ap.bitcast(BF16)[:, 0:cols]
        else:
            ap = ap[:, 0:cols]
        keys = tuple("scrB%d" % b for b in range(off // 128, (off + n32 - 1) // 128 + 1))
        return V(ap, keys)

    def piece(self, desc):
        if self.plan is None:
            self.rec.append(desc)
            i = len(self.rec) - 1
            kind = desc[0]
            if kind == "cast":
                return V(self.t["wbf"][i % 6][:, :], "wbf%d" % (i % 6))
            return V(self.t["stg"][i % 3][:, :], "stg%d" % (i % 3))
        i = self.wpos
        self.wpos += 1
        last = len(self.plan) - 1
        progressed = True
        while progressed:
            progressed = False
            if self.cast_next <= min(i + 1, last):
                j = self.cast_next
                if self.plan[j][0] != "cast":
                    self.cast_next += 1
                    progressed = True
                elif self.dma_next > j:
                    self._issue_cast(j)
                    self.cast_next += 1
                    progressed = True
            if self.dma_next <= min(i + 3, last):
                j = self.dma_next
                k = j - 3
                ok = k < 0 or (self.plan[k][0] == "cast" and self.cast_next > k) or \
                    (self.plan[k][0] != "cast" and i >= k + 1)
                if ok:
                    self._issue_dma(j)
                    self.dma_next += 1
                    progressed = True
        assert self.dma_next > i and self.cast_next > i, (i, self.dma_next, self.cast_next)
        kind = self.plan[i][0]
        if kind == "cast":
            return V(self.t["wbf"][self.castidx[i] % 6][:, :], "wbf%d" % (self.castidx[i] % 6))
        return V(self.t["stg"][i % 3][:, :], "stg%d" % (i % 3))

    def prep_plan(self):
        self.castidx = {}
        c = 0
        for i, d in enumerate(self.plan):
            if d[0] == "cast":
                self.castidx[i] = c
                c += 1

    def _issue_dma(self, j):
        kind, srcs, ncols = self.plan[j]
        stg = self.t["stg"][j % 3]
        skey = "stg%d" % (j % 3)
        for (sl, dram_ap) in srcs:
            self.P.dma("sp", sl(stg), dram_ap, writes=[skey])

    def _issue_cast(self, j):
        kind, srcs, ncols = self.plan[j]
        stg = self.t["stg"][j % 3]
        skey = "stg%d" % (j % 3)
        if kind == "cast":
            ci = self.castidx[j] % 6
            wb = self.t["wbf"][ci]
            ceng = os.environ.get("KCAST", "act")
            if ceng == "act":
                self.P.op("act", lambda e: e.activation(out=wb[:, 0:ncols], in_=stg[:, 0:ncols], func=AF.Copy),
                          reads=[skey], writes=["wbf%d" % ci])
            else:
                self.P.op(ceng, lambda e: e.tensor_copy(out=wb[:, 0:ncols], in_=stg[:, 0:ncols]),
                          reads=[skey], writes=["wbf%d" % ci])

    def pc_cols(self, w_ap, c0, nk=8, ncol=128):
        src = w_ap[:, c0:c0 + ncol].rearrange("(k p) c -> p k c", p=128)
        sl = lambda stg: stg[:, 0:nk * ncol].rearrange("p (k c) -> p k c", c=ncol)
        return ("cast", [(sl, src)], nk * ncol)

    def pc_rows(self, w_ap, r0):
        src = w_ap[r0:r0 + 128, :]
        sl = lambda stg: stg[:, 0:1024]
        return ("cast", [(sl, src)], 1024)

    def pc_raw(self, dram_ap, rows, cols):
        sl = lambda stg: stg[0:rows, 0:cols]
        return ("raw", [(sl, dram_ap)], cols)

    def load_x(self):
        t = self.t
        X = t["X"]
        for i in range(17):
            if i >= int(os.environ.get("KNX", "17")) and i < 16:
                continue
            if i == 16 and os.environ.get("KNOXS"):
                continue
            rows = 128 if i < 16 else 64
            src = self.d["xp"][i * 128:(i + 1) * 128, :] if i < 16 else self.d["xs"][:, :]
            st = self.piece(self.pc_raw(src, rows, 1024))
            ti = min(i // 4, 4)
            for b in range(2):
                ps = self.psbig()
                for q in range(4):
                    k = 4 * b + q
                    self.tr(ps[:, q * rows:(q + 1) * rows], st[0:rows, k * 128:(k + 1) * 128], rows)
                for q in range(4):
                    k = 4 * b + q
                    dst = V(X[:, k, i * 128:i * 128 + rows], "X%d_%d" % (k, ti))
                    kcp = os.environ.get("KCP", "bank")
                    if kcp == "dve":
                        eng_ = "dve"
                    elif kcp == "bank":
                        eng_ = "dve" if b == 0 else "act"
                    else:
                        eng_ = "dve" if q % 2 == 0 else "act"
                    self.cp(eng_, dst, ps[:, q * rows:(q + 1) * rows])

    def rmsnorm_fm(self, tiles, wcol0, dst_fn, scratch=None):
        t = self.t
        X = t["X"]
        cp = t["cp"]
        if scratch is None:
            so_ = self.scr_off
            sq = [self.scr(512, BF16) for _ in range(2)]
            rs = [self.scr(512) for _ in range(2)]
            self.scr_off = so_
        else:
            sq, rs = scratch
        for ti_i, (t0, n) in enumerate(tiles):
            ti = ALL_TILES.index((t0, n))
            ps = self.psbig()
            for k in range(8):
                s = sq[k % 2]
                xin = V(X[:, k, t0:t0 + n], "X%d_%d" % (k, ti))
                self.act(s[:, 0:n], xin, AF.Square)
                self.mm(ps[:, 0:n], t["onesb"], s[:, 0:n], start=(k == 0), stop=(k == 7))
            r = rs[ti_i % 2]
            self.act(r[:, 0:n], ps[:, 0:n], AF.Ln, bias=t["epsc"], scale=1.0 / D)
            self.act(r[:, 0:n], r[:, 0:n], AF.Exp, scale=-0.5)
            for k in range(8):
                xin = V(X[:, k, t0:t0 + n], "X%d_%d" % (k, ti))
                self.stt(dst_fn(k, t0, n), xin, cp[:, wcol0 + k:wcol0 + k + 1], r[:, 0:n], ALU.mult, ALU.mult)

    def load_hist(self, l):
        t = self.t
        specs = [
            ("sdc", 48, 1536, t["HISTD"], 12),
            ("slc", 48, 512, t["HISTL"], 4),
            ("slh", 16, 512, t["H0"], 4),
            ("sfc", 32, 3072, t["HISTF"], 24),
        ]
        for name, rows, cols, dst, nch in specs:
            c0 = 0
            while c0 < cols:
                cc = min(1024, cols - c0)
                st = self.piece(self.pc_raw(self.d[name][l, :, c0:c0 + cc], rows, cc))
                nb = cc // 128
                for b0 in range(0, nb, 4):
                    ps = self.psbig()
                    qn = min(4, nb - b0)
                    for q in range(qn):
                        self.tr(ps[:, q * rows:(q + 1) * rows], st[0:rows, (b0 + q) * 128:(b0 + q + 1) * 128], rows)
                    ch0 = c0 // 128 + b0
                    dstv = V(dst[:, ch0:ch0 + qn, :], name + "h")
                    self.cp("dve", dstv, V(ps.ap[:, 0:qn * rows].rearrange("p (q r) -> p q r", r=rows), ps.key))
                c0 += cc

    def out_states(self, l, src, nch, ncolp, ncols, out_p, out_s, name):
        nr = ncolp + ncols
        ost = self.scr(nch * 128, parts=nr)
        for b0 in range(0, nch, 4):
            ps = self.psbig()
            qn = min(4, nch - b0)
            for q in range(qn):
                self.tr(ps[0:nr, q * 128:(q + 1) * 128], V(src[:, b0 + q, :], name), 128)
            self.cp("act", ost[:, b0 * 128:(b0 + qn) * 128], ps[0:nr, 0:qn * 128])
        self.dma(OQ, out_p, ost.ap[0:ncolp, :], reads=[ost.key])
        self.dma(OQ, out_s, ost.ap[ncolp:nr, :], reads=[ost.key])

    def layer(self, l):
        t = self.t
        d = self.d
        cp = t["cp"]
        cm = t["cm"]
        X = t["X"]
        H = t["H"]
        OAB = t["OAB"]
        MG = t["MG"]
        cb = l * CPL
        w_in = d["w_in"][l]

        self.new_phase()
        self.memset("pool", V(t["CSTD"][:, :, :], "cstd"), 0.0)
        self.memset("pool", V(t["CSTL"][:, :, :], "cstl"), 0.0)
        self.memset("pool", V(t["CSTF"][:, :, :], "cstf"), 0.0)
        self.memset("pool", V(t["HL"][:, :, :], "hl"), 0.0)
        self.memset("pool", V(t["SP"][:, :, :], "sprompt"), 0.0)
        self.load_hist(l)
        nea = V(t["LC"][:, 0:4], "lc_nea")
        self.act(nea, cp[:, CP_AB + l * 8:CP_AB + l * 8 + 4], AF.Exp)
        self.ts("dve", nea, nea, -1.0, ALU.mult)
        lc = V(t["LC"][:, 4:8], "lc_c")
        lc2 = V(t["LC"][:, 8:12], "lc_c2")
        self.act(lc, cp[:, cb + 92:cb + 96], AF.Exp, scale=-1.0)
        self.act(lc, lc, AF.Ln, bias=1.0)
        self.ts("dve", lc2, lc, -16.0, ALU.mult)
        self.ts("dve", lc, lc, -8.0, ALU.mult)
        srcs = [(lambda stg: stg[:, 0:1024].rearrange("p (n d) -> p n d", d=128),
                 d["lruw"][l].rearrange("n c d -> c n d"))]
        wv = self.piece(("cast", srcs, 1024))
        LW = V(t["LW"][:, :], "lw")
        self.cp("dve", LW, wv)

        if KSTOP <= 1:
            return
        for p in range(2):
            self.mixer_pass(l, p)
            if KSTOP <= 6:
                return

        self.ffn(l)

    def mixer_pass(self, l, p):
        t = self.t
        d = self.d
        cp = t["cp"]
        cm = t["cm"]
        X = t["X"]
        H = t["H"]
        OAB = t["OAB"]
        MG = t["MG"]
        cb = l * CPL
        w_in = d["w_in"][l]
        tiles = PASS_TILES[p]
        base = PASS_BASE[p]
        H = t["Hm"]
        self.new_phase(("SCR", "SCR2", "MG32"))

        def hv(k, t0, n):
            return V(H[:, k, t0 - base:t0 - base + n], "H%d_%d" % (k, t0))

        self.rmsnorm_fm(tiles, cb + 0, hv)

        chunks = []
        for (t0, n) in tiles:
            if n == 512:
                for c in range(4):
                    chunks.append((t0 + c * 128, 128, t0))
            else:
                chunks.append((t0, 64, t0))
        COLS = t["COLS"]
        wab = self.piece(self.pc_cols(w_in, 2048, 8, 8))
        wabv = V(wab.ap[:, 0:64].rearrange("p (k c) -> p k c", c=8), wab.key)
        tmp4 = [self.scr(4) for _ in range(4)]
        for ci, (c0, C, t0) in enumerate(chunks):
            smp = (C == 64)
            ps = self.pssm()
            for k in range(8):
                self.mm(ps[0:C, 0:8], V(H[:, k, c0 - base:c0 - base + C], "H%d_%d" % (k, t0)), wabv[:, k, :],
                        start=(k == 0), stop=(k == 7))
            col = lambda q: V(COLS[0:C, ci, q * 4:(q + 1) * 4], "cols%d" % ci)
            g = tmp4[ci % 2]
            self.tt("dve", g[0:C, :], ps[0:C, 0:4], cp[0:C, CP_AB + l * 8 + 4:CP_AB + l * 8 + 8], ALU.add)
            self.act(g[0:C, :], g[0:C, :], AF.Exp)
            self.act(g[0:C, :], g[0:C, :], AF.Ln, bias=1.0)
            self.tt("dve", g[0:C, :], g[0:C, :], V(t["LC"][0:C, 0:4], "lc_nea"), ALU.mult)
            self.act(col(1), ps[0:C, 4:8], AF.Sigmoid)
            ps2 = self.pssm()
            uc = cm[0:C, CM_UCS:CM_UCS + C] if smp else cm[0:C, CM_UC:CM_UC + C]
            ob = cm[0:C, CM_OBS:CM_OBS + C] if smp else t["ones32"][0:C, 0:C]
            self.mm(ps2[0:C, 0:4], uc, g[0:C, :])
            self.mm(ps2[0:C, 4:8], ob, g[0:C, :])
            self.cp("dve", col(0), ps2[0:C, 0:4])
            self.ts("dve", col(2), col(1), -1.0, ALU.mult)
            self.act(col(3), ps2[0:C, 0:4], AF.Exp)
            self.tt("dve", col(4), col(1), col(3), ALU.mult)
            d4 = tmp4[2 + ci % 2]
            self.tt("dve", d4[0:C, :], ps2[0:C, 4:8], col(0), ALU.subtract)
            self.act(col(5), d4[0:C, :], AF.Exp)
            self.act(col(6), ps2[0:C, 4:8], AF.Exp)

        if KSTOP <= 2:
            return
        for hd in range(4):
            self.dn_head(l, p, hd, tiles, chunks)
            if KSTOP <= 3:
                return
        if KSTOP <= 4:
            return

        for j in range(4):
            self.lru_chunk(l, p, j, tiles)

        if KSTOP <= 5:
            return
        self.P.fence()
        mt = [self.scr(512) for _ in range(4)]
        mi = 0
        for m in range(8):
            wga = self.piece(self.pc_cols(w_in, 3080 + m * 128))
            wgb = self.piece(self.pc_cols(w_in, 4104 + m * 128))
            wba = self.piece(self.pc_cols(d["wbr"][l, 0], m * 128, 4))
            wbb = self.piece(self.pc_cols(d["wbr"][l, 1], m * 128, 4))
            v3 = lambda w, nk: V(w.ap[:, 0:nk * 128].rearrange("p (k c) -> p k c", c=128), w.key)
            wga3, wgb3, wba3, wbb3 = v3(wga, 8), v3(wgb, 8), v3(wba, 4), v3(wbb, 4)
            for (t0, n) in tiles:
                lo = t0 - base
                pga, pgb, pba, pbb = self.psbig(), self.psbig(), self.psbig(), self.psbig()
                for k in range(8):
                    self.mm(pga[:, 0:n], wga3[:, k, :], hv(k, t0, n), start=(k == 0), stop=(k == 7))
                for k in range(8):
                    self.mm(pgb[:, 0:n], wgb3[:, k, :], hv(k, t0, n), start=(k == 0), stop=(k == 7))
                for k in range(4):
                    self.mm(pba[:, 0:n], wba3[:, k, :], V(OAB[:, k, lo:lo + n], "OAB%d_%d" % (k, t0)),
                            start=(k == 0), stop=(k == 3))
                for k in range(4):
                    self.mm(pbb[:, 0:n], wbb3[:, k, :], V(OAB[:, 4 + k, lo:lo + n], "OAB%d_%d" % (4 + k, t0)),
                            start=(k == 0), stop=(k == 3))
                sa = mt[mi % 4]
                sb = mt[(mi + 1) % 4]
                mi += 2
                self.act(sa[:, 0:n], pga[:, 0:n], AF.Sigmoid)
                self.act(sb[:, 0:n], pgb[:, 0:n], AF.Sigmoid)
                self.tt("dve", sa[:, 0:n], sa[:, 0:n], pba[:, 0:n], ALU.mult)
                self.tt("dve", sb[:, 0:n], sb[:, 0:n], pbb[:, 0:n], ALU.mult)
                self.tt("dve", V(MG[:, m, lo:lo + n], "MG%d_%d" % (m, t0)), sa[:, 0:n], sb[:, 0:n], ALU.add)

        for e in range(8):
            wo = self.piece(self.pc_cols(d["w_o"][l], e * 128))
            wo3 = V(wo.ap.rearrange("p (k c) -> p k c", c=128), wo.key)
            for (t0, n) in tiles:
                lo = t0 - base
                ti = ALL_TILES.index((t0, n))
                ps = self.psbig()
                for k in range(8):
                    self.mm(ps[:, 0:n], wo3[:, k, :], V(MG[:, k, lo:lo + n], "MG%d_%d" % (k, t0)),
                            start=(k == 0), stop=(k == 7))
                xv = V(X[:, e, t0:t0 + n], "X%d_%d" % (e, ti))
                self.tt("dve", xv, xv, ps[:, 0:n], ALU.add)

        if p == 1:
            self.new_phase()
            o = self.d
            self.out_states(l, t["CSTD"], 12, 3, 48, o["o_pdc"][l], o["o_sdc"][l], "cstd")
            self.out_states(l, t["CSTL"], 4, 3, 48, o["o_plc"][l], o["o_slc"][l], "cstl")
            self.out_states(l, t["HL"], 4, 1, 16, o["o_pl"][l], o["o_sl"][l], "hl")
            for hd in range(4):
                self.dma(OQ, o["o_pd"][l, hd], t["SP"][:, hd, :], reads=["sprompt"])

    def dn_head(self, l, p, hd, tiles, chunks):
        t = self.t
        d = self.d
        cp = t["cp"]
        cm = t["cm"]
        H = t["Hm"]
        OAB = t["OAB"]
        COLS = t["COLS"]
        cb = l * CPL
        base = PASS_BASE[p]
        w_in = d["w_in"][l]
        CSTD = t["CSTD"]
        so = self.scr_off
        self.scr_off = so
        wq = self.piece(self.pc_cols(w_in, hd * 128))
        wk = self.piece(self.pc_cols(w_in, 512 + hd * 128))
        wv = self.piece(self.pc_cols(w_in, 1024 + hd * 128))
        wz = self.piece(self.pc_cols(w_in, 1536 + hd * 128))
        v3 = lambda w: V(w.ap.rearrange("p (k c) -> p k c", c=128), w.key)
        w3 = [v3(wq), v3(wk), v3(wv), v3(wz)]
        chs = [hd, 4 + hd, 8 + hd]
        cvs = [[self.scr(512) for _ in range(3)] for _ in range(2)]
        zss = [self.scr(512) for _ in range(2)]
        xp = [self.scr(520) for _ in range(3)]
        sq = [self.scr(512) for _ in range(2)]
        mark = self.scr_off
        S = V(t["SP"][:, hd, :], "sprompt")

        def proj(t0_, n_):
            lo_ = t0_ - base
            pss_ = [self.psbig() for _ in range(4)]

            def mk(i):
                def f():
                    for k in range(8):
                        self.mm(pss_[i][:, 0:n_], w3[i][:, k, :], V(H[:, k, lo_:lo_ + n_], "H%d_%d" % (k, t0_)),
                                start=(k == 0), stop=(k == 7))
                return f
            return pss_, [mk(i) for i in range(4)]

        def pre_steps(t0, n, pss, cv, zs):
            smp = (n == 64)
            steps = []
            steps.append(lambda: self.act(zs[:, 0:n], pss[3][:, 0:n], AF.Silu))
            views = {}

            def evac(i):
                ch = chs[i]
                if not smp:
                    carry = V(CSTD[:, ch, 0:3], "cstd")
                    self.cp("act", xp[i][:, 0:3], carry)
                    self.cp("act", xp[i][:, 3:3 + n], pss[i][:, 0:n])
                    self.cp("act", carry, xp[i][:, n:n + 3])
                else:
                    x3 = V(xp[i].ap[:, 0:112].rearrange("p (s c) -> p s c", c=7), xp[i].key)
                    self.cp("act", x3[:, :, 0:3], V(t["HISTD"][:, ch, :].rearrange("p (s r) -> p s r", r=3), "sdch"))
                    self.cp("act", x3[:, :, 3:7], V(pss[i].ap[:, 0:64].rearrange("p (s c) -> p s c", c=4), pss[i].key))
                    self.cp("act", V(CSTD[:, ch, 3:51].rearrange("p (s r) -> p s r", r=3), "cstd"), x3[:, :, 4:7])

            def conv(i):
                ch = chs[i]
                cwc = cb + 16
                if not smp:
                    src = lambda j: xp[i][:, j:j + n]
                    dst = cv[i][:, 0:n]
                else:
                    x3 = V(xp[i].ap[:, 0:112].rearrange("p (s c) -> p s c", c=7), xp[i].key)
                    src = lambda j: x3[:, :, j:j + 4]
                    dst = V(cv[i].ap[:, 0:64].rearrange("p (s c) -> p s c", c=4), cv[i].key)
                wc = lambda j: cp[:, cwc + j * 12 + ch:cwc + j * 12 + ch + 1]
                self.ts("dve", dst, src(0), wc(0), ALU.mult)
                for j in range(1, 4):
                    self.stt(dst, src(j), wc(j), dst, ALU.mult, ALU.add)
                self.act(cv[i][:, 0:n], cv[i][:, 0:n], AF.Silu)

            def l2a(i):
                sqb = V(sq[i].ap.bitcast(BF16)[:, 0:512], sq[i].key)
                self.act(sqb[:, 0:n], cv[i][:, 0:n], AF.Square)
                ps = self.psbig()
                self.mm(ps[:, 0:n], t["onesb"], sqb[:, 0:n])
                self.act(sq[i][:, 0:n], ps[:, 0:n], AF.Ln, bias=t["epsc"], scale=1.0)

            def l2b(i):
                self.act(sq[i][:, 0:n], sq[i][:, 0:n], AF.Exp, scale=-0.5)
                if i == 0:
                    self.stt(cv[i][:, 0:n], cv[i][:, 0:n], 128.0 ** -0.5, sq[i][:, 0:n], ALU.mult, ALU.mult)
                else:
                    self.tt("dve", cv[i][:, 0:n], cv[i][:, 0:n], sq[i][:, 0:n], ALU.mult)

            for i in range(3):
                steps.append(lambda i=i: evac(i))
            for i in range(3):
                steps.append(lambda i=i: conv(i))
            for i in range(2):
                steps.append(lambda i=i: l2a(i))
            for i in range(2):
                steps.append(lambda i=i: l2b(i))
            return steps

        pss, cl = proj(*tiles[0])
        for f_ in cl:
            f_()
        for st_ in pre_steps(tiles[0][0], tiles[0][1], pss, cvs[0], zss[0]):
            st_()
        ci = 0
        for tidx, (t0, n) in enumerate(tiles):
            lo = t0 - base
            smp = (n == 64)
            cv, zs = cvs[tidx % 2], zss[tidx % 2]
            side = []
            if tidx + 1 < len(tiles):
                nt0, nn = tiles[tidx + 1]
                npss, ncl = proj(nt0, nn)
                side = ncl + pre_steps(nt0, nn, npss, cvs[(tidx + 1) % 2], zss[(tidx + 1) % 2])
            self.scr_off = mark
            if smp:
                for st_ in side:
                    st_()
                self.delta_chunk(l, p, hd, ci, 64, True, cv[0][:, 0:64], cv[1][:, 0:64], cv[2][:, 0:64], zs[:, 0:64], S,
                                 V(OAB[:, hd, lo:lo + 64], "OAB%d_%d" % (hd, t0)))
                ci += 1
            else:
                ctxs = []
                for c in range(4):
                    a = c * 128
                    ctxs.append(dict(ci=ci, qn=cv[0][:, a:a + 128], kn=cv[1][:, a:a + 128], vs=cv[2][:, a:a + 128],
                                     zs=zs[:, a:a + 128],
                                     dst=V(OAB[:, hd, lo + a:lo + a + 128], "OAB%d_%d" % (hd, t0))))
                    ci += 1
                self.delta_group(l, hd, ctxs, S, side, zs_all=zs[:, 0:512],
                                 dst_all=V(OAB[:, hd, lo:lo + 512], "OAB%d_%d" % (hd, t0)))
        self.scr_off = so

    def delta_chunk(self, l, p, hd, ci, C, smp, qn, kn, vs, zs, S, oab_dst):
        t = self.t
        d = self.d
        cp = t["cp"]
        cm = t["cm"]
        COLS = t["COLS"]
        cb = l * CPL
        so = self.scr_off
        col = lambda q: V(COLS[0:C, ci, q * 4 + hd:q * 4 + hd + 1], "cols%d" % ci)
        gc, beta, nbeta, egc, bge, edl, egl = [col(q) for q in range(7)]
        if smp:
            mU = cm[0:C, CM_MUS:CM_MUS + C]
            pL = cm[0:C, CM_PLS:CM_PLS + C]
        else:
            mU = cm[0:C, CM_MU:CM_MU + C]
            pL = cm[0:C, CM_PL:CM_PL + C]
        idn = cm[0:C, CM_ID:CM_ID + C]
        sc = lambda n=128, parts=128: self.scr(n, parts=parts)
        pk = self.pssm()
        pv = self.pssm()
        self.tr(pk[0:C, :], kn, 128)
        self.tr(pv[0:C, :], vs, 128)
        kbg, kd, vb = sc(), sc(), sc()
        self.act(kbg[0:C, :], pk[0:C, :], AF.Copy, scale=bge)
        self.act(kd[0:C, :], pk[0:C, :], AF.Copy, scale=edl)
        self.ts("dve", vb[0:C, :], pv[0:C, :], beta, ALU.mult)
        gcb = sc()
        self.ts("dve", gcb[0:C, :], t["ones32"][0:C, :], gc, ALU.mult)
        pg = self.pssm()
        self.tr(pg[:, 0:C], gcb[0:C, :], C)
        eT, eL = sc(), sc()
        self.stt(eT[0:C, 0:C], pg[0:C, 0:C], gc, mU, ALU.subtract, ALU.add)
        self.act(eT[0:C, 0:C], eT[0:C, 0:C], AF.Exp)
        self.stt(eL[0:C, 0:C], pg[0:C, 0:C], gc, pL, ALU.subtract, ALU.add)
        self.act(eL[0:C, 0:C], eL[0:C, 0:C], AF.Exp, scale=-1.0)
        pkk = self.pssm()
        self.mm(pkk[0:C, 0:C], kn, kn)
        N = sc()
        self.stt(N[0:C, 0:C], pkk[0:C, 0:C], nbeta, eL[0:C, 0:C], ALU.mult, ALU.mult)
        pqk = self.pssm()
        self.mm(pqk[0:C, 0:C], kn, qn)
        attnT = sc()
        self.tt("dve", attnT[0:C, 0:C], pqk[0:C, 0:C], eT[0:C, 0:C], ALU.mult)
        pn = self.pssm()
        self.tr(pn[0:C, 0:C], N[0:C, 0:C], C)
        Nt = sc()
        self.cp("act", Nt[0:C, 0:C], pn[0:C, 0:C])
        Pt = sc()
        self.tt("dve", Pt[0:C, 0:C], pn[0:C, 0:C], idn, ALU.add)
        L = 2 if smp else 7
        Ncur, Ntcur, Ptcur = N, Nt, Pt
        for k in range(1, L):
            p1 = self.pssm()
            self.mm(p1[0:C, 0:C], Ntcur[0:C, 0:C], Ncur[0:C, 0:C])
            Nn = sc()
            self.cp("act", Nn[0:C, 0:C], p1[0:C, 0:C])
            Ntn = None
            if k < L - 1:
                p2 = self.pssm()
                self.mm(p2[0:C, 0:C], Ncur[0:C, 0:C], Ntcur[0:C, 0:C])
                Ntn = sc()
                self.cp("dve", Ntn[0:C, 0:C], p2[0:C, 0:C])
            p3 = self.pssm()
            self.mm(p3[0:C, 0:C], Nn[0:C, 0:C], Ptcur[0:C, 0:C])
            Ptn = sc()
            self.tt("dve", Ptn[0:C, 0:C], p3[0:C, 0:C], Ptcur[0:C, 0:C], ALU.add)
            Ncur, Ntcur, Ptcur = Nn, Ntn, Ptn
        Pt = Ptcur
        pw = self.pssm()
        self.mm(pw[:, 0:C], kbg[0:C, :], Pt[0:C, 0:C])
        nwT = sc()
        self.act(nwT[:, 0:C], pw[:, 0:C], AF.Copy, scale=-1.0)
        o_sb = sc()
        if not smp:
            pvn = self.pssm()
            self.mm(pvn[0:C, :], Pt[0:C, 0:C], vb[0:C, :], start=True, stop=False)
            self.mm(pvn[0:C, :], nwT[:, 0:C], S, start=False, stop=True)
            vn = sc()
            self.cp("act", vn[0:C, :], pvn[0:C, :])
            pqs = self.pssm()
            self.mm(pqs[0:C, :], qn, S)
            tq = sc()
            self.act(tq[0:C, :], pqs[0:C, :], AF.Copy, scale=egc)
            pav = self.pssm()
            self.mm(pav[0:C, :], attnT[0:C, 0:C], vn[0:C, :])
            self.tt("dve", o_sb[0:C, :], tq[0:C, :], pav[0:C, :], ALU.add)
            pds = self.pssm()
            self.mm(pds[:, :], kd[0:C, :], vn[0:C, :])
            self.stt(S, S, egl, pds[:, :], ALU.mult, ALU.add)
        else:
            SS = [self.scr(128) for _ in range(NS)]
            for s in range(NS):
                self.dma("sp", SS[s].ap, d["sd"][l, s, hd], writes=[SS[s].key])
            nwm = self.scr(1088)
            qm = self.scr(1088)
            self.memset("pool", nwm, 0.0)
            self.memset("pool", qm, 0.0)
            dv = lambda b: V(b.ap.rearrange("p (s c) -> p s c", c=68)[:, :, 0:4], b.key)
            sv = lambda b: V(b.ap[:, 0:64].rearrange("p (s c) -> p s c", c=4), b.key)
            self.cp("pool", dv(nwm), sv(nwT))
            self.cp("pool", dv(qm), sv(qn))
            pvn = self.pssm()
            self.mm(pvn[0:C, :], Pt[0:C, 0:C], vb[0:C, :], start=True, stop=False)
            for s in range(NS):
                self.mm(pvn[0:C, :], nwm[:, s * 68 - s * 4:s * 68 - s * 4 + 64], SS[s], start=False, stop=(s == NS - 1))
            vn = sc()
            self.cp("act", vn[0:C, :], pvn[0:C, :])
            pqs = self.pssm()
            for s in range(NS):
                self.mm(pqs[0:C, :], qm[:, s * 64:s * 64 + 64], SS[s], start=(s == 0), stop=(s == NS - 1))
            tq = sc()
            self.act(tq[0:C, :], pqs[0:C, :], AF.Copy, scale=egc)
            pav = self.pssm()
            self.mm(pav[0:C, :], attnT[0:C, 0:C], vn[0:C, :])
            self.tt("dve", o_sb[0:C, :], tq[0:C, :], pav[0:C, :], ALU.add)
            eglb = sc()
            self.ts("dve", eglb[0:C, :], t["ones32"][0:C, :], egl, ALU.mult)
            pe_ = self.pssm()
            self.mm(pe_[:, 0:NS], eglb[0:C, :], cm[0:C, CM_SEL:CM_SEL + NS])
            egs = sc(16)
            self.cp("act", egs[:, 0:NS], pe_[:, 0:NS])
            kdm = [self.scr(128) for _ in range(2)]
            for s in range(NS):
                km = kdm[s % 2]
                self.act(km[0:C, :], kd[0:C, :], AF.Copy, scale=cm[0:C, CM_RM + s:CM_RM + s + 1])
                pds = self.pssm()
                self.mm(pds[:, :], km[0:C, :], vn[0:C, :])
                self.stt(SS[s], SS[s], egs[:, s:s + 1], pds[:, :], ALU.mult, ALU.add)
                self.dma(OQ, d["o_sd"][l, s, hd], SS[s].ap, reads=[SS[s].key])
        junk = sc()
        ss = sc(8)
        self.act(junk[0:C, :], o_sb[0:C, :], AF.Square, accum=ss[0:C, 0:1])
        self.act(ss[0:C, 0:1], ss[0:C, 0:1], AF.Ln, bias=t["epsc"][0:C, :], scale=1.0 / 128)
        self.act(ss[0:C, 0:1], ss[0:C, 0:1], AF.Exp, scale=-0.5)
        self.ts("dve", o_sb[0:C, :], o_sb[0:C, :], ss[0:C, 0:1], ALU.mult)
        po = self.pssm()
        self.tr(po[:, 0:C], o_sb[0:C, :], C)
        self.stt(oab_dst, po[:, 0:C], cp[:, cb + 168:cb + 169], zs, ALU.mult, ALU.mult)
        self.scr_off = so

    def psq4(self):
        b = self.q4_i = (getattr(self, "q4_i", -1) + 1) % 4
        key = "pss%d" % b
        bank = V(self.t["pss"][b][:, :], key)
        return bank, [V(self.t["pss"][b][:, q * 128:(q + 1) * 128], key) for q in range(4)]

    def delta_group(self, l, hd, ctxs, S, side=(), zs_all=None, dst_all=None):
        t = self.t
        cp = t["cp"]
        cm = t["cm"]
        COLS = t["COLS"]
        cb = l * CPL
        so = self.scr_off
        C = 128
        nc4 = len(ctxs)
        assert nc4 == 4
        mU = cm[:, CM_MU:CM_MU + C]
        pL = cm[:, CM_PL:CM_PL + C]
        idn = cm[:, CM_ID:CM_ID + C]
        allb = {}
        for nm in ("kbg", "kd", "vb", "gcb", "eT", "eL", "Nt", "Pt", "N2", "Nt2", "Pt2"):
            a = self.scr(128 * nc4)
            allb[nm] = a
            for c, cx in enumerate(ctxs):
                cx[nm] = V(a.ap[:, c * 128:(c + 1) * 128], (a.key[c],))
        for cx in ctxs:
            ci = cx["ci"]
            col = lambda q, ci=ci: V(COLS[:, ci, q * 4 + hd:q * 4 + hd + 1], "cols%d" % ci)
            cx["c"] = [col(q) for q in range(7)]
            cx["nwT"] = cx["Nt2"]
            cx["vn"] = cx["N2"]
            cx["osb"] = cx["Pt2"]
            cx["junk"] = cx["Nt"]
        ssall = self.scr(8 * nc4)
        for ii, cx in enumerate(ctxs):
            cx["ss"] = ssall[:, ii * 8:(ii + 1) * 8]
        ss4 = V(ssall.ap.rearrange("p (c e) -> p c e", e=8)[:, :, 0], ssall.key)
        side = list(side)

        def sidestep(k=1):
            for _ in range(k):
                if side:
                    side.pop(0)()
        sidestep(4)
        bk, qk_ = self.psq4()
        for c, cx in enumerate(ctxs):
            self.tr(qk_[c], cx["kn"], 128)
        bv, qv_ = self.psq4()
        for c, cx in enumerate(ctxs):
            self.tr(qv_[c], cx["vs"], 128)
        for c, cx in enumerate(ctxs):
            gc, beta, nbeta, egc, bge, edl, egl = cx["c"]
            self.act(cx["kbg"], qk_[c], AF.Copy, scale=bge)
            self.act(cx["kd"], qk_[c], AF.Copy, scale=edl)
            self.ts("dve", cx["vb"], qv_[c], beta, ALU.mult)
            self.ts("dve", cx["gcb"], t["ones32"], gc, ALU.mult)
        sidestep()
        bg, qg_ = self.psq4()
        for c, cx in enumerate(ctxs):
            self.tr(qg_[c], cx["gcb"], 128)
        for c, cx in enumerate(ctxs):
            gc = cx["c"][0]
            self.stt(cx["eT"], qg_[c], gc, mU, ALU.subtract, ALU.add)
            self.stt(cx["eL"], qg_[c], gc, pL, ALU.subtract, ALU.add)
        self.act(allb["eT"], allb["eT"], AF.Exp)
        self.act(allb["eL"], allb["eL"], AF.Exp, scale=-1.0)
        sidestep()
        bkk, qkk = self.psq4()
        for c, cx in enumerate(ctxs):
            self.mm(qkk[c], cx["kn"], cx["kn"])
        bqk, qqk = self.psq4()
        for c, cx in enumerate(ctxs):
            self.mm(qqk[c], cx["kn"], cx["qn"])
        for c, cx in enumerate(ctxs):
            self.stt(cx["eL"], qkk[c], cx["c"][2], cx["eL"], ALU.mult, ALU.mult)
            cx["N"] = cx["eL"]
            cx["attnT"] = cx["eT"]
        self.tt("dve", allb["eT"], bqk, allb["eT"], ALU.mult)
        sidestep()
        bn, qn_ = self.psq4()
        for c, cx in enumerate(ctxs):
            self.tr(qn_[c], cx["N"], 128)
        self.cp("act", allb["Nt"], bn)
        for c, cx in enumerate(ctxs):
            self.tt("dve", cx["Pt"], qn_[c], idn, ALU.add)
        L = 7
        cur = ("eL", "Nt", "Pt")
        alt = ("N2", "Nt2", "Pt2")
        for k in range(1, L):
            sidestep()
            b1, q1 = self.psq4()
            for c, cx in enumerate(ctxs):
                self.mm(q1[c], cx[cur[1]], cx[cur[0]])
            self.cp("act", allb[alt[0]], b1)
            b3, q3 = self.psq4()
            for c, cx in enumerate(ctxs):
                self.mm(q3[c], cx[alt[0]], cx[cur[2]])
            if k < L - 1:
                b2, q2 = self.psq4()
                for c, cx in enumerate(ctxs):
                    self.tr(q2[c], cx[alt[0]], 128)
                self.cp("dve", allb[alt[1]], b2)
            self.tt("dve", allb[alt[2]], b3, allb[cur[2]], ALU.add)
            cur, alt = alt, cur
        assert cur == ("eL", "Nt", "Pt")
        bw, qw = self.psq4()
        for c, cx in enumerate(ctxs):
            self.mm(qw[c], cx["kbg"], cx["Pt"])
        self.act(allb["Nt2"], bw, AF.Copy, scale=-1.0)
        while side:
            side.pop(0)()
        for cx in ctxs:
            gc, beta, nbeta, egc, bge, edl, egl = cx["c"]
            Pt = cx["Pt"]
            pvn = self.pssm()
            self.mm(pvn, Pt, cx["vb"], start=True, stop=False)
            self.mm(pvn, cx["nwT"], S, start=False, stop=True)
            pqs = self.pssm()
            self.mm(pqs, cx["qn"], S)
            self.cp("act", cx["vn"], pvn)
            tq = cx["gcb"]
            self.act(tq, pqs, AF.Copy, scale=egc)
            pav = self.pssm()
            self.mm(pav, cx["attnT"], cx["vn"])
            pds = self.pssm()
            self.mm(pds, cx["kd"], cx["vn"])
            self.stt(S, S, egl, pds, ALU.mult, ALU.add)
            self.tt("dve", cx["osb"], tq, pav, ALU.add)
        for cx in ctxs:
            self.act(cx["junk"], cx["osb"], AF.Square, accum=cx["ss"][:, 0:1])
        self.act(ss4, ss4, AF.Ln, bias=t["epsc"], scale=1.0 / 128)
        self.act(ss4, ss4, AF.Exp, scale=-0.5)
        for cx in ctxs:
            self.ts("dve", cx["osb"], cx["osb"], cx["ss"][:, 0:1], ALU.mult)
        bo, qo = self.psq4()
        for c, cx in enumerate(ctxs):
            self.tr(qo[c], cx["osb"], 128)
        self.stt(dst_all, bo, cp[:, cb + 168:cb + 169], zs_all, ALU.mult, ALU.mult)
        self.scr_off = so

    def lru_chunk(self, l, p, j, tiles):
        t = self.t
        d = self.d
        cp = t["cp"]
        H = t["Hm"]
        OAB = t["OAB"]
        cb = l * CPL
        base = PASS_BASE[p]
        w_in = d["w_in"][l]
        CSTL = t["CSTL"]
        HL = t["HL"]
        so = self.scr_off
        wx = self.piece(self.pc_cols(w_in, 2056 + j * 128))
        wy = self.piece(self.pc_cols(w_in, 2568 + j * 128))
        v3 = lambda w: V(w.ap.rearrange("p (k c) -> p k c", c=128), w.key)
        wx3, wy3 = v3(wx), v3(wy)
        LW3 = V(t["LW"][:, :].rearrange("p (n d) -> p n d", d=128), "lw")
        xp = self.scr(520)
        xc = self.scr(512)
        xcb = self.scr(512, BF16)
        r_, i_, a_, m_, gy = [self.scr(512) for _ in range(5)]
        lc = V(t["LC"][:, 4 + j:5 + j], "lc_c")
        lc2 = V(t["LC"][:, 8 + j:9 + j], "lc_c2")
        for (t0, n) in tiles:
            lo = t0 - base
            smp = (n == 64)
            px, py = self.psbig(), self.psbig()
            for k in range(8):
                self.mm(px[:, 0:n], wx3[:, k, :], V(H[:, k, lo:lo + n], "H%d_%d" % (k, t0)), start=(k == 0), stop=(k == 7))
            for k in range(8):
                self.mm(py[:, 0:n], wy3[:, k, :], V(H[:, k, lo:lo + n], "H%d_%d" % (k, t0)), start=(k == 0), stop=(k == 7))
            self.act(gy[:, 0:n], py[:, 0:n], AF.Gelu_apprx_tanh)
            if not smp:
                carry = V(CSTL[:, j, 0:3], "cstl")
                self.cp("act", xp[:, 0:3], carry)
                self.cp("act", xp[:, 3:3 + n], px[:, 0:n])
                self.cp("act", carry, xp[:, n:n + 3])
                src = lambda jj: xp[:, jj:jj + n]
                dst = xc[:, 0:n]
            else:
                x3 = V(xp.ap[:, 0:112].rearrange("p (s c) -> p s c", c=7), xp.key)
                self.cp("act", x3[:, :, 0:3], V(t["HISTL"][:, j, :].rearrange("p (s r) -> p s r", r=3), "slch"))
                self.cp("act", x3[:, :, 3:7], V(px.ap[:, 0:64].rearrange("p (s c) -> p s c", c=4), px.key))
                self.cp("act", V(CSTL[:, j, 3:51].rearrange("p (s r) -> p s r", r=3), "cstl"), x3[:, :, 4:7])
                src = lambda jj: x3[:, :, jj:jj + 4]
                dst = V(xc.ap[:, 0:64].rearrange("p (s c) -> p s c", c=4), xc.key)
            wc = lambda jj: cp[:, cb + 64 + jj * 4 + j:cb + 64 + jj * 4 + j + 1]
            self.ts("dve", dst, src(0), wc(0), ALU.mult, cp[:, cb + 80 + j:cb + 81 + j], ALU.add)
            for jj in range(1, 4):
                self.stt(dst, src(jj), wc(jj), dst, ALU.mult, ALU.add)
            self.cp("dve", xcb[:, 0:n], xc[:, 0:n])
            pr, pi = self.psbig(), self.psbig()
            self.mm(pr[:, 0:n], LW3[:, j, :], xcb[:, 0:n])
            self.mm(pi[:, 0:n], LW3[:, 4 + j, :], xcb[:, 0:n])
            self.act(r_[:, 0:n], pr[:, 0:n], AF.Sigmoid, bias=cp[:, cb + 84 + j:cb + 85 + j])
            self.act(i_[:, 0:n], pi[:, 0:n], AF.Sigmoid, bias=cp[:, cb + 88 + j:cb + 89 + j])
            self.act(a_[:, 0:n], r_[:, 0:n], AF.Exp, scale=lc)
            self.act(m_[:, 0:n], r_[:, 0:n], AF.Exp, scale=lc2)
            self.act(m_[:, 0:n], m_[:, 0:n], AF.Ln, bias=1.0, scale=-1.0)
            self.act(m_[:, 0:n], m_[:, 0:n], AF.Exp, scale=0.5)
            if t0 == 0:
                self.memset("dve", m_[:, 0:1], 1.0)
            self.tt("dve", i_[:, 0:n], i_[:, 0:n], xc[:, 0:n], ALU.mult)
            self.tt("dve", i_[:, 0:n], i_[:, 0:n], m_[:, 0:n], ALU.mult)
            hl = V(HL[:, j, 0:1], "hl")
            if not smp:
                self.scan(r_[:, 0:n], a_[:, 0:n], i_[:, 0:n], hl)
                self.cp("dve", hl, r_[:, n - 1:n])
            else:
                for s in range(NS):
                    self.scan(r_[:, s * 4:s * 4 + 4], a_[:, s * 4:s * 4 + 4], i_[:, s * 4:s * 4 + 4],
                              V(t["H0"][:, j, s:s + 1], "slhh"))
                self.cp("dve", V(HL[:, j, 1:17], "hl"),
                        V(r_.ap[:, 0:64].rearrange("p (s c) -> p s c", c=4)[:, :, 3], r_.key))
            self.tt("dve", V(OAB[:, 4 + j, lo:lo + n], "OAB%d_%d" % (4 + j, t0)), r_[:, 0:n], gy[:, 0:n], ALU.mult)
        self.scr_off = so

    def ffn(self, l):
        t = self.t
        d = self.d
        cp = t["cp"]
        X = t["X"]
        H = t["H"]
        ACTB = t["ACTB"]
        CSTF = t["CSTF"]
        cb = l * CPL
        self.new_phase()

        def hv(k, t0, n):
            return V(H[:, k, t0:t0 + n], "H%d_%d" % (k, t0))

        self.rmsnorm_fm(ALL_TILES, cb + 8, hv)
        xp = [self.scr(520) for _ in range(2)]
        gc_ = [self.scr(512) for _ in range(3)]
        ub_ = [self.scr(512, BF16) for _ in range(3)]
        it = 0
        pend = None
        for g in range(6):
            for jj in range(4):
                f = g * 4 + jj
                wg = self.piece(self.pc_cols(d["ffn_w_in"][l], f * 128))
                wu = self.piece(self.pc_cols(d["ffn_w_in"][l], DFF + f * 128))
                v3 = lambda w: V(w.ap.rearrange("p (k c) -> p k c", c=128), w.key)
                wg3, wu3 = v3(wg), v3(wu)
                for (t0, n) in ALL_TILES:
                    smp = (n == 64)
                    pg, pu = self.psbig(), self.psbig()
                    for k in range(8):
                        self.mm(pg[:, 0:n], wg3[:, k, :], hv(k, t0, n), start=(k == 0), stop=(k == 7))
                    for k in range(8):
                        self.mm(pu[:, 0:n], wu3[:, k, :], hv(k, t0, n), start=(k == 0), stop=(k == 7))
                    x_ = xp[it % 2]
                    c_ = gc_[it % 3]
                    ub = ub_[it % 3]
                    it += 1
                    self.cp("act", ub[:, 0:n], pu[:, 0:n])
                    if not smp:
                        carry = V(CSTF[:, f, 0:2], "cstf")
                        self.cp("act", x_[:, 0:2], carry)
                        self.cp("act", x_[:, 2:2 + n], pg[:, 0:n])
                        self.cp("act", carry, x_[:, n:n + 2])
                        src = lambda j: x_[:, j:j + n]
                        dst = c_[:, 0:n]
                    else:
                        x3 = V(x_.ap[:, 0:96].rearrange("p (s c) -> p s c", c=6), x_.key)
                        self.cp("act", x3[:, :, 0:2], V(t["HISTF"][:, f, :].rearrange("p (s r) -> p s r", r=2), "sfch"))
                        self.cp("act", x3[:, :, 2:6], V(pg.ap[:, 0:64].rearrange("p (s c) -> p s c", c=4), pg.key))
                        self.cp("act", V(CSTF[:, f, 2:34].rearrange("p (s r) -> p s r", r=2), "cstf"), x3[:, :, 4:6])
                        src = lambda j: x3[:, :, j:j + 4]
                        dst = V(c_.ap[:, 0:64].rearrange("p (s c) -> p s c", c=4), c_.key)
                    wc = lambda j: cp[:, cb + 96 + j * 24 + f:cb + 96 + j * 24 + f + 1]
                    self.ts("dve", dst, src(0), wc(0), ALU.mult)
                    for j in range(1, 3):
                        self.stt(dst, src(j), wc(j), dst, ALU.mult, ALU.add)
                    if pend is not None:
                        pend()

                    def _fin(c_=c_, ub=ub, n=n, jj=jj, t0=t0):
                        self.act(c_[:, 0:n], c_[:, 0:n], AF.Gelu_apprx_tanh)
                        self.tt("dve", V(ACTB[:, jj, t0:t0 + n], "ACTB%d_%d" % (jj, t0)), c_[:, 0:n], ub[:, 0:n], ALU.mult)
                    pend = _fin
            if pend is not None:
                pend()
                pend = None
            wd = [self.piece(self.pc_rows(d["ffn_w_down"][l], (g * 4 + jj) * 128)) for jj in range(4)]
            for e in range(8):
                for (t0, n) in ALL_TILES:
                    ti = ALL_TILES.index((t0, n))
                    ps = self.psbig()
                    for jj in range(4):
                        self.mm(ps[:, 0:n], wd[jj][:, e * 128:(e + 1) * 128],
                                V(ACTB[:, jj, t0:t0 + n], "ACTB%d_%d" % (jj, t0)), start=(jj == 0), stop=(jj == 3))
                    xv = V(X[:, e, t0:t0 + n], "X%d_%d" % (e, ti))
                    self.tt("dve", xv, xv, ps[:, 0:n], ALU.add)
        self.new_phase()
        self.out_states(l, CSTF, 24, 2, 32, d["o_pfc"][l], d["o_sfc"][l], "cstf")

    def final(self):
        t = self.t
        d = self.d
        X = t["X"]
        self.new_phase(("SCR", "RB32"))
        nsc = ([self.scr(512, BF16) for _ in range(2)], [self.scr(512) for _ in range(2)])
        yf = [self.scr(512) for _ in range(8)]
        yt = [self.scr(1024) for _ in range(2)]
        bi = 0
        for (t0, n) in ALL_TILES:
            def dst(k, t0_, n_):
                return yf[k][:, 0:n_]
            self.rmsnorm_fm([(t0, n)], CP_FIN, dst, scratch=nsc)
            blocks = [(t0 + b * 128, 128) for b in range(4)] if n == 512 else [(t0, 64)]
            for (b0, C) in blocks:
                y = yt[bi % 2]
                bi += 1
                for hb in range(2):
                    ps = self.psbig()
                    for q in range(4):
                        k = hb * 4 + q
                        self.tr(ps[0:C, q * 128:(q + 1) * 128], yf[k][:, b0 - t0:b0 - t0 + C], 128)
                    self.cp("act" if hb == 0 else "dve", y[0:C, hb * 512:(hb + 1) * 512], ps[0:C, 0:512])
                if C == 128:
                    self.dma(OQ, d["o_yp"][b0:b0 + 128, :], y.ap[0:128, :], reads=[y.key])
                else:
                    self.dma(OQ, d["o_ys"][:, :], y.ap[0:64, :], reads=[y.key])

    def build(self):
        t = self.t
        self.dma("sp", t["cp"].ap, self.d["cst"][:, 0:CP_N], writes=["cp"])
        self.dma("sp", t["cm"].ap, self.d["cst"][:, CP_N:CP_N + CM_N], writes=["cm"])
        self.memset("pool", t["onesb"], 1.0)
        self.memset("pool", t["ones32"], 1.0)
        self.memset("pool", t["epsc"], EPS)
        if KSTOP >= -1:
            self.load_x()
        if KSTOP >= 1:
            for l in range(self.depth):
                self.layer(l)
        if KSTOP >= 0:
            self.final()
        self.P.final_wait_all("sp")


def build_program(depth=DEPTH):
    nc = bass.Bass("TRN2", target_bir_lowering=False)
    dram = {}

    def din(name, shape):
        if os.environ.get("KNOIN") and name not in ("cst",):
            return
        dram[name] = nc.dram_tensor(name, list(shape), F32, kind="ExternalInput").ap()

    def dout(name, shape):
        if os.environ.get("KNOOUT") and name not in os.environ.get("KNOOUT").split(","):
            return
        dram[name] = nc.dram_tensor(name, list(shape), F32, kind="ExternalOutput").ap()

    din("xp", (TP, D)); din("xs", (64, D))
    din("sdc", (4, 48, 1536)); din("sd", (4, NS, 4, 128, 128)); din("slc", (4, 48, 512))
    din("slh", (4, NS, 512)); din("sfc", (4, 32, DFF))
    din("w_in", (4, D, INC)); din("lruw", (4, 8, 128, 128))
    din("wbr", (4, 2, 512, D)); din("w_o", (4, D, D))
    din("ffn_w_in", (4, D, 2 * DFF)); din("ffn_w_down", (4, DFF, D))
    din("cst", (128, CP_N + CM_N))
    dout("o_yp", (TP, D)); dout("o_ys", (64, D))
    dout("o_pd", (4, 4, 128, 128)); dout("o_sd", (4, NS, 4, 128, 128))
    dout("o_ps", (4, 12800)); dout("o_ss", (4, 204800))
    if "o_ps" not in dram or "o_ss" not in dram:
        dram["o_ps"] = nc.dram_tensor("o_ps_i", [4, 12800], F32, kind="Internal").ap() if "o_ps" not in dram else dram["o_ps"]
        dram["o_ss"] = nc.dram_tensor("o_ss_i", [4, 204800], F32, kind="Internal").ap() if "o_ss" not in dram else dram["o_ss"]
    ops, oss = dram["o_ps"], dram["o_ss"]
    dram["o_pdc"] = [ops[l, 0:4608].rearrange("(r c) -> r c", c=1536) for l in range(4)]
    dram["o_plc"] = [ops[l, 4608:6144].rearrange("(r c) -> r c", c=512) for l in range(4)]
    dram["o_pl"] = [ops[l, 6144:6656].rearrange("(r c) -> r c", c=512) for l in range(4)]
    dram["o_pfc"] = [ops[l, 6656:12800].rearrange("(r c) -> r c", c=DFF) for l in range(4)]
    dram["o_sdc"] = [oss[l, 0:73728].rearrange("(r c) -> r c", c=1536) for l in range(4)]
    dram["o_slc"] = [oss[l, 73728:98304].rearrange("(r c) -> r c", c=512) for l in range(4)]
    dram["o_sl"] = [oss[l, 98304:106496].rearrange("(r c) -> r c", c=512) for l in range(4)]
    dram["o_sfc"] = [oss[l, 106496:204800].rearrange("(r c) -> r c", c=DFF) for l in range(4)]

    SCRN = 6656
    with contextlib.ExitStack() as es:
        def SB(name, shape, dt=F32):
            return es.enter_context(nc.sbuf_tensor(name, list(shape), dt))

        def PS(name, shape):
            return es.enter_context(nc.psum_tensor(name, list(shape), F32))

        tens = {}
        tens["X"] = SB("X", (128, 8, T))
        Hfull = SB("H", (128, 8, T), BF16)
        tens["H"] = Hfull
        Hflat = Hfull[:, :, :].rearrange("p a b -> p (a b)")
        tens["Hm"] = Hflat[:, 0:8 * HT].rearrange("p (a b) -> p a b", b=HT)
        tens["SCR2"] = Hflat[:, 8 * HT:8 * T].bitcast(F32)
        RB = SB("RB", (128, 2, 8, HT), BF16)
        RBflat = RB[:, :, :, :].rearrange("p c a b -> p (c a b)")
        tens["RB32"] = RBflat.bitcast(F32)
        tens["MG32"] = RB[:, 1, :, :].rearrange("p a b -> p (a b)").bitcast(F32)
        tens["OAB"] = RB[:, 0, :, :]
        tens["MG"] = RB[:, 1, :, :]
        tens["ACTB"] = RBflat[:, 0:4 * T].rearrange("p (a b) -> p a b", b=T)
        tens["stg"] = [SB("stg%d" % i, (128, 1024)) for i in range(3)]
        tens["wbf"] = [SB("wbf%d" % i, (128, 1024), BF16) for i in range(6)]
        tens["SCR"] = SB("SCR", (128, SCRN))[:, :]
        tens["SCRN"] = SCRN
        tens["cp"] = V(SB("cp", (128, CP_N))[:, :], "cp")
        tens["cm"] = V(SB("cm", (128, CM_N))[:, :], "cm")
        tens["onesb"] = V(SB("onesb", (128, 128), BF16)[:, :], "onesb")
        tens["ones32"] = V(SB("ones32", (128, 128))[:, :], "ones32")
        tens["epsc"] = V(SB("epsc", (128, 1))[:, :], "epsc")
        tens["LC"] = SB("LC", (128, 12))
        tens["LW"] = SB("LW", (128, 1024), BF16)
        tens["COLS"] = SB("COLS", (128, 9, 28))
        tens["CSTD"] = SB("CSTD", (128, 12, 51))
        tens["CSTL"] = SB("CSTL", (128, 4, 51))
        tens["CSTF"] = SB("CSTF", (128, 24, 34))
        tens["HL"] = SB("HL", (128, 4, 17))
        tens["SP"] = SB("SP", (128, 4, 128))
        tens["HISTD"] = SB("HISTD", (128, 12, 48))
        tens["HISTL"] = SB("HISTL", (128, 4, 48))
        tens["HISTF"] = SB("HISTF", (128, 24, 32))
        tens["H0"] = SB("H0", (128, 4, 16))
        tens["psb"] = [PS("psb%d" % i, (128, 512)) for i in range(4)]
        tens["pss"] = [PS("pss%d" % i, (128, 512)) for i in range(4)]

        Pd = Prog(nc, dry=True)
        bd = Builder(nc, Pd, dram, tens, None, depth)
        bd.build()
        plan = bd.rec
        P = Prog(nc)
        b = Builder(nc, P, dram, tens, plan, depth)
        b.prep_plan()
        b.build()
        assert b.wpos == len(plan), (b.wpos, len(plan))
        P.run()
        nops = P.nops
    return nc, nops


def make_consts():
    cm = np.zeros((128, CM_N), np.float32)
    i = np.arange(128)
    cm[:, CM_ID:CM_ID + 128] = np.eye(128, dtype=np.float32)
    cm[:, CM_MU:CM_MU + 128] = np.where(i[None, :] >= i[:, None], 0.0, -1e30)
    cm[:, CM_PL:CM_PL + 128] = np.where(i[:, None] > i[None, :], 0.0, 1e30)
    cm[:, CM_UC:CM_UC + 128] = (i[:, None] <= i[None, :]).astype(np.float32)
    s = np.arange(64)
    same = (s[:, None] // 4) == (s[None, :] // 4)
    cm[:64, CM_MUS:CM_MUS + 64] = np.where(same & (s[None, :] >= s[:, None]), 0.0, -1e30)
    cm[:64, CM_PLS:CM_PLS + 64] = np.where(same & (s[:, None] > s[None, :]), 0.0, 1e30)
    cm[:64, CM_UCS:CM_UCS + 64] = (same & (s[:, None] <= s[None, :])).astype(np.float32)
    cm[:64, CM_OBS:CM_OBS + 64] = same.astype(np.float32)
    cm[:64, CM_RM:CM_RM + 16] = ((s[:, None] // 4) == np.arange(16)[None, :]).astype(np.float32)
    cm[:64, CM_SEL:CM_SEL + 16] = (s[:, None] == (4 * np.arange(16)[None, :] + 3)).astype(np.float32)
    return cm


def pack_params(inp):
    cpk = np.zeros((128, CP_N), np.float32)

    def fm(v, n):
        return np.ascontiguousarray(v.reshape(n, 128).T)

    for l in range(4):
        b = l * CPL
        cpk[:, b + 0:b + 8] = fm(inp["norm1_w"][l], 8)
        cpk[:, b + 8:b + 16] = fm(inp["norm2_w"][l], 8)
        for j in range(4):
            cpk[:, b + 16 + j * 12:b + 16 + (j + 1) * 12] = fm(inp["dn_conv_w"][l, j], 12)
            cpk[:, b + 64 + j * 4:b + 64 + (j + 1) * 4] = fm(inp["lru_conv_w"][l, j], 4)
        cpk[:, b + 80:b + 84] = fm(inp["lru_conv_b"][l], 4)
        cpk[:, b + 84:b + 88] = fm(inp["lru_ba"][l], 4)
        cpk[:, b + 88:b + 92] = fm(inp["lru_bx"][l], 4)
        cpk[:, b + 92:b + 96] = fm(inp["lru_lambda"][l], 4)
        for j in range(3):
            cpk[:, b + 96 + j * 24:b + 96 + (j + 1) * 24] = fm(inp["ffn_conv_w"][l, j], 24)
        cpk[:, b + 168] = inp["dn_norm_w"][l]
        cpk[:, CP_AB + l * 8:CP_AB + l * 8 + 4] = inp["dn_A_log"][l][None, :]
        cpk[:, CP_AB + l * 8 + 4:CP_AB + l * 8 + 8] = inp["dn_dt_bias"][l][None, :]
    cpk[:, CP_FIN:CP_FIN + 8] = fm(inp["final_norm_w"], 8)
    return cpk


_CACHE = {}


def kernel(**inp):
    depth = DEPTH
    if "nc" not in _CACHE:
        _CACHE["nc"] = build_program(depth)
    nc, _ = _CACHE["nc"]
    f = lambda a: np.ascontiguousarray(np.asarray(a, dtype=np.float32))
    cpk = pack_params({k: np.asarray(v) for k, v in inp.items()})
    cmk = make_consts()
    shared = {
        "w_in": f(inp["w_in"]),
        "lruw": np.ascontiguousarray(np.concatenate([f(inp["lru_wa"]), f(inp["lru_wx"])], axis=1)),
        "wbr": np.ascontiguousarray(np.stack([f(inp["w_branch_a"]), f(inp["w_branch_b"])], axis=1)),
        "w_o": f(inp["w_o"]),
        "ffn_w_in": f(inp["ffn_w_in"]), "ffn_w_down": f(inp["ffn_w_down"]),
        "cst": np.ascontiguousarray(np.concatenate([cpk, cmk], axis=1)),
    }
    in_maps = []
    for c in range(8):
        s0, s1 = c * NS, (c + 1) * NS
        m = dict(shared)
        m["xp"] = f(inp["x_prompt"][c])
        m["xs"] = f(inp["x_sample"][s0:s1]).reshape(64, D)
        m["sdc"] = f(inp["state_dn_conv"][:, s0:s1]).reshape(4, 48, 1536)
        m["sd"] = f(inp["state_dn"][:, s0:s1])
        m["slc"] = f(inp["state_lru_conv"][:, s0:s1]).reshape(4, 48, 512)
        m["slh"] = f(inp["state_lru"][:, s0:s1])
        m["sfc"] = f(inp["state_ffn_conv"][:, s0:s1]).reshape(4, 32, DFF)
        in_maps.append(m)
    res = run_bass_kernel_spmd(nc, in_maps, core_ids=list(range(8)))
    R = res.results
    y_p = np.stack([R[c]["o_yp"] for c in range(8)], 0)
    y_s = np.concatenate([R[c]["o_ys"].reshape(NS, 4, D) for c in range(8)], 0)
    PS = [R[c]["o_ps"] for c in range(8)]
    SS = [R[c]["o_ss"] for c in range(8)]
    p_dc = np.stack([PS[c][:, 0:4608].reshape(4, 3, 1536) for c in range(8)], 1)
    p_d = np.stack([R[c]["o_pd"] for c in range(8)], 1)
    p_lc = np.stack([PS[c][:, 4608:6144].reshape(4, 3, 512) for c in range(8)], 1)
    p_l = np.stack([PS[c][:, 6144:6656].reshape(4, 512) for c in range(8)], 1)
    p_fc = np.stack([PS[c][:, 6656:12800].reshape(4, 2, DFF) for c in range(8)], 1)
    s_dc = np.concatenate([SS[c][:, 0:73728].reshape(4, NS, 3, 1536) for c in range(8)], 1)
    s_d = np.concatenate([R[c]["o_sd"] for c in range(8)], 1)
    s_lc = np.concatenate([SS[c][:, 73728:98304].reshape(4, NS, 3, 512) for c in range(8)], 1)
    s_l = np.concatenate([SS[c][:, 98304:106496].reshape(4, NS, 512) for c in range(8)], 1)
    s_fc = np.concatenate([SS[c][:, 106496:204800].reshape(4, NS, 2, DFF) for c in range(8)], 1)
    outs = (y_p, y_s, p_dc, p_d, p_lc, p_l, p_fc, s_dc, s_d, s_lc, s_l, s_fc)
    return tuple(np.ascontiguousarray(o, dtype=np.float32) for o in outs)
```

```python
import contextlib
import os
import numpy as np
import concourse.bass as bass
import concourse.mybir as mybir
from concourse.bass_utils import run_bass_kernel_spmd

F32 = mybir.dt.float32
BF16 = mybir.dt.bfloat16
AF = mybir.ActivationFunctionType
ALU = mybir.AluOpType

ENGS = ("pe", "act", "dve", "pool", "sp")

DEPTH = 4
D = 1024
T = 2112
TP = 2048
NS = 16
INC = 5128
DFF = 3072
EPS = 1e-6
import os
KSTOP = int(os.environ.get("KSTOP", "99"))
OQ = os.environ.get("KOQ", "sp")
CPL = 169
CP_FIN = 4 * CPL
CP_AB = CP_FIN + 8
CP_N = CP_AB + 32
CM_ID = 0
CM_MU = 128
CM_PL = 256
CM_UC = 384
CM_MUS = 512
CM_PLS = 576
CM_UCS = 640
CM_OBS = 704
CM_RM = 768
CM_SEL = 784
CM_N = 800

PASS_TILES = [[(0, 512), (512, 512)], [(1024, 512), (1536, 512), (2048, 64)]]
PASS_BASE = [0, 1024]
ALL_TILES = [(0, 512), (512, 512), (1024, 512), (1536, 512), (2048, 64)]
HT = 1088


class Op:
    __slots__ = ("eng", "fn", "waits", "flag", "idx", "dma", "dsem", "dval")

    def __init__(self, eng, fn):
        self.eng = eng
        self.fn = fn
        self.waits = []
        self.flag = False
        self.idx = -1
        self.dma = False
        self.dsem = None
        self.dval = 0


class Prog:
    def __init__(self, nc, n_dma_sems=int(os.environ.get("KNS", "32")), dry=False):
        self.nc = nc
        self.dry = dry
        self.streams = {e: [] for e in ENGS}
        if not dry:
            self.esem = {e: nc.alloc_semaphore(name="s_" + e) for e in ENGS}
            self.dsems = [nc.alloc_semaphore(name="d%d" % i) for i in range(n_dma_sems)]
        else:
            self.esem = {e: None for e in ENGS}
            self.dsems = [i for i in range(n_dma_sems)]
        self.dlast = [None] * n_dma_sems
        self.dvals = [0] * n_dma_sems
        self.dnext = 0
        self.last_w = {}
        self.readers = {}
        self.waited = {}
        self.dwaited = {}
        self.nops = 0

    def _add_wait(self, op, dep):
        f = op.eng
        if dep.dma:
            k = (f, id(dep.dsem) if not self.dry else dep.dsem)
            if self.dwaited.get(k, 0) >= dep.dval:
                return
            self.dwaited[k] = dep.dval
            op.waits.append(dep)
        else:
            if dep.eng == f and False:
                return
            k = (f, dep.eng)
            if self.waited.get(k, -1) >= dep.idx:
                return
            self.waited[k] = dep.idx
            dep.flag = True
            op.waits.append(dep)

    def _deps(self, op, reads, writes):
        for k in reads:
            w = self.last_w.get(k)
            if w is not None and (w.dma or w.eng != op.eng or op.eng != "pe"):
                self._add_wait(op, w)
            if k.startswith("ps"):
                for r in self.readers.get(k, ()):
                    if r.eng != op.eng:
                        self._add_wait(op, r)
        for k in writes:
            w = self.last_w.get(k)
            if w is not None and (w.dma or w.eng != op.eng or op.eng != "pe"):
                self._add_wait(op, w)
            for r in self.readers.get(k, ()):
                if r is not op and (r.dma or r.eng != op.eng or op.eng != "pe"):
                    self._add_wait(op, r)
        for k in reads:
            self.readers.setdefault(k, []).append(op)
        for k in writes:
            self.last_w[k] = op
            self.readers[k] = []

    def op(self, eng, fn, reads=(), writes=()):
        o = Op(eng, fn)
        o.idx = len(self.streams[eng])
        self._deps(o, reads, writes)
        self.streams[eng].append(o)
        self.nops += 1
        return o

    def dma(self, eng, out, in_, reads=(), writes=()):
        o = Op(eng, None)
        o.dma = True
        o.idx = len(self.streams[eng])
        si = self.dnext
        self.dnext = (self.dnext + 1) % len(self.dsems)
        prev = self.dlast[si]
        if prev is not None:
            self._add_wait(o, prev)
        self._deps(o, reads, writes)
        o.dsem = self.dsems[si]
        self.dvals[si] += 16
        o.dval = self.dvals[si]
        self.dlast[si] = o
        o.fn = (out, in_)
        self.streams[eng].append(o)
        self.nops += 1
        return o

    def fence(self):
        lasts = {}
        for en in ENGS:
            for o in reversed(self.streams[en]):
                if not o.dma and o.fn is not None:
                    lasts[en] = o
                    break
        dl = [d for d in self.dlast if d is not None]
        for en in ENGS:
            o = Op(en, None)
            o.idx = len(self.streams[en])
            for en2, l in lasts.items():
                if en2 != en:
                    self._add_wait(o, l)
            for d in dl:
                self._add_wait(o, d)
            self.streams[en].append(o)
        self.last_w = {}
        self.readers = {}

    def final_wait_all(self, ename="sp"):
        o = Op(ename, None)
        o.idx = len(self.streams[ename])
        for prev in self.dlast:
            if prev is not None:
                self._add_wait(o, prev)
        for en in ENGS:
            if en != ename:
                for l in reversed(self.streams[en]):
                    if not l.dma and l.fn is not None:
                        self._add_wait(o, l)
                        break
        self.streams[ename].append(o)

    def prepass(self):
        for ename in ENGS:
            cnt = 0
            for o in self.streams[ename]:
                if not o.dma and o.flag:
                    cnt += 1
                    o.dval = cnt

    def emit(self, ename, e):
        for o in self.streams[ename]:
            for d in o.waits:
                if d.dma:
                    e.wait_ge(d.dsem, d.dval)
                else:
                    e.wait_ge(self.esem[d.eng], d.dval)
            if o.dma:
                out, in_ = o.fn
                e.dma_start(out=out, in_=in_).then_inc(o.dsem, 16)
            elif o.fn is not None:
                ins = o.fn(e)
                if o.flag:
                    ins.then_inc(self.esem[ename], 1)
            else:
                assert not o.flag

    def run(self):
        nc = self.nc
        self.prepass()
        with nc.Block() as block:
            @block.tensor
            def _(e):
                self.emit("pe", e)

            @block.scalar
            def _(e):
                self.emit("act", e)

            @block.vector
            def _(e):
                self.emit("dve", e)

            @block.gpsimd
            def _(e):
                self.emit("pool", e)

            @block.sync
            def _(e):
                self.emit("sp", e)


class V:
    __slots__ = ("ap", "key")

    def __init__(self, ap, key):
        self.ap = ap
        self.key = key

    def __getitem__(self, idx):
        return V(self.ap[idx], self.key)

    def k(self, key):
        return V(self.ap, key)


class Builder:
    def __init__(self, nc, P, dram, tens, plan, depth):
        self.nc = nc
        self.P = P
        self.d = dram
        self.t = tens
        self.depth = depth
        self.plan = plan
        self.rec = []
        self.wpos = 0
        self.wissued = 0
        self.dma_next = 0
        self.cast_next = 0
        self.uid = 0
        self.big_i = 0
        self.sm_i = 0
        self.scr_off = 0
        self.scr_phase = 0

    def _rk(self, *vs):
        out = []
        for v in vs:
            if isinstance(v, V):
                if isinstance(v.key, tuple):
                    out.extend(v.key)
                else:
                    out.append(v.key)
        return out

    def _wk(self, *vs):
        return self._rk(*vs)

    def mm(self, out, lhsT, rhs, start=True, stop=True):
        self.P.op("pe", lambda e: e.matmul(out.ap, lhsT=lhsT.ap, rhs=rhs.ap, start=start, stop=stop),
                  reads=self._rk(lhsT, rhs), writes=self._wk(out))

    def tr(self, out, in_, np_):
        idn = self.t["cm"][0:np_, CM_ID:CM_ID + np_]
        shp = tuple(in_.ap.shape)
        if len(shp) == 2 and shp[0] == 128 and shp[1] == 128:
            self.P.op("pe", lambda e: e.transpose(out.ap, in_.ap, idn.ap),
                      reads=self._rk(in_, idn), writes=self._wk(out))
        else:
            self.P.op("pe", lambda e: e.matmul(out.ap, lhsT=in_.ap, rhs=idn.ap, start=True, stop=True),
                      reads=self._rk(in_, idn), writes=self._wk(out))

    def act(self, out, in_, func, bias=None, scale=None, accum=None):
        kw = {}
        rd = [in_]
        if bias is not None:
            kw["bias"] = bias.ap if isinstance(bias, V) else bias
            rd.append(bias)
        if scale is not None:
            kw["scale"] = scale.ap if isinstance(scale, V) else scale
            rd.append(scale)
        wr = self._wk(out)
        if accum is not None:
            kw["accum_out"] = accum.ap
            wr = wr + self._wk(accum)
        self.P.op("act", lambda e: e.activation(out=out.ap, in_=in_.ap, func=func, **kw),
                  reads=self._rk(*rd), writes=wr)

    def ts(self, eng, out, in0, s1, op0, s2=None, op1=None):
        a1 = s1.ap if isinstance(s1, V) else s1
        a2 = s2.ap if isinstance(s2, V) else s2
        if op1 is None:
            fn = lambda e: e.tensor_scalar(out=out.ap, in0=in0.ap, scalar1=a1, scalar2=None, op0=op0)
        else:
            fn = lambda e: e.tensor_scalar(out=out.ap, in0=in0.ap, scalar1=a1, scalar2=a2, op0=op0, op1=op1)
        self.P.op(eng, fn, reads=self._rk(in0, s1, s2), writes=self._wk(out))

    def stt(self, out, in0, scalar, in1, op0, op1):
        a = scalar.ap if isinstance(scalar, V) else scalar
        self.P.op("dve", lambda e: e.scalar_tensor_tensor(out=out.ap, in0=in0.ap, scalar=a, in1=in1.ap,
                                                          op0=op0, op1=op1),
                  reads=self._rk(in0, scalar, in1), writes=self._wk(out))

    def tt(self, eng, out, in0, in1, op):
        self.P.op(eng, lambda e: e.tensor_tensor(out=out.ap, in0=in0.ap, in1=in1.ap, op=op),
                  reads=self._rk(in0, in1), writes=self._wk(out))

    def cp(self, eng, out, in_):
        if eng == "act":
            self.act(out, in_, AF.Copy)
        else:
            self.P.op(eng, lambda e: e.tensor_copy(out=out.ap, in_=in_.ap), reads=self._rk(in_), writes=self._wk(out))

    def memset(self, eng, out, val):
        self.P.op(eng, lambda e: e.memset(out.ap, val), writes=self._wk(out))

    def recip(self, out, in_):
        self.P.op("dve", lambda e: e.reciprocal(out=out.ap, in_=in_.ap), reads=self._rk(in_), writes=self._wk(out))

    def scan(self, out, d0, d1, init):
        a = init.ap if isinstance(init, V) else init
        self.P.op("dve", lambda e: e.tensor_tensor_scan(out=out.ap, data0=d0.ap, data1=d1.ap, initial=a,
                                                        op0=ALU.mult, op1=ALU.add),
                  reads=self._rk(d0, d1, init), writes=self._wk(out))

    def dma(self, eng, out_ap, in_ap, reads=(), writes=()):
        def fl(ks):
            o = []
            for k in ks:
                if isinstance(k, tuple):
                    o.extend(k)
                else:
                    o.append(k)
            return o
        self.P.dma(eng, out_ap, in_ap, reads=fl(reads), writes=fl(writes))

    def psbig(self):
        i = self.big_i
        self.big_i = (i + 1) % 4
        return V(self.t["psb"][i][:, :], "psb%d" % i)

    def pssm(self):
        i = self.sm_i
        self.sm_i = (i + 1) % 16
        q, b = divmod(i, 4)
        return V(self.t["pss"][b][:, q * 128:(q + 1) * 128], "pss%d" % b)

    def new_phase(self, segs=("SCR",)):
        self.P.fence()
        self.scr_off = 0
        self.scr_phase += 1
        self.segs = []
        o = 0
        for nm in segs:
            ap = self.t[nm]
            n = ap.shape[1]
            self.segs.append((o, n, ap))
            o += n
        self.scr_total = o

    def scr(self, cols, dt=F32, parts=128):
        n32 = cols if dt == F32 else (cols + 1) // 2
        n32 = (n32 + 7) // 8 * 8
        off = self.scr_off
        if n32 >= 128:
            n32 = (n32 + 127) // 128 * 128
            off = (off + 127) // 128 * 128
        seg = None
        for (o, n, ap) in self.segs:
            if off < o:
                off = o
            if off >= o and off + n32 <= o + n:
                seg = (o, n, ap)
                break
        assert seg is not None, ("scratch overflow", self.scr_off, n32, self.scr_total)
        self.scr_off = off + n32
        self.scr_peak = max(getattr(self, "scr_peak", 0), self.scr_off)
        o, n, sap = seg
        ap = sap[0:parts, off - o:off - o + n32]
        if dt != F32:
            ap = ap.bitcast(BF16)[:, 0:cols]
        else:
            ap = ap[:, 0:cols]
        keys = tuple("scrB%d" % b for b in range(off // 128, (off + n32 - 1) // 128 + 1))
        return V(ap, keys)

    def piece(self, desc):
        if self.plan is None:
            self.rec.append(desc)
            i = len(self.rec) - 1
            kind = desc[0]
            if kind == "cast":
                return V(self.t["wbf"][i % 6][:, :], "wbf%d" % (i % 6))
            return V(self.t["stg"][i % 3][:, :], "stg%d" % (i % 3))
        i = self.wpos
        self.wpos += 1
        last = len(self.plan) - 1
        progressed = True
        while progressed:
            progressed = False
            if self.cast_next <= min(i + 1, last):
                j = self.cast_next
                if self.plan[j][0] != "cast":
                    self.cast_next += 1
                    progressed = True
                elif self.dma_next > j:
                    self._issue_cast(j)
                    self.cast_next += 1
                    progressed = True
            if self.dma_next <= min(i + 3, last):
                j = self.dma_next
                k = j - 3
                ok = k < 0 or (self.plan[k][0] == "cast" and self.cast_next > k) or \
                    (self.plan[k][0] != "cast" and i >= k + 1)
                if ok:
                    self._issue_dma(j)
                    self.dma_next += 1
                    progressed = True
        assert self.dma_next > i and self.cast_next > i, (i, self.dma_next, self.cast_next)
        kind = self.plan[i][0]
        if kind == "cast":
            return V(self.t["wbf"][self.castidx[i] % 6][:, :], "wbf%d" % (self.castidx[i] % 6))
        return V(self.t["stg"][i % 3][:, :], "stg%d" % (i % 3))

    def prep_plan(self):
        self.castidx = {}
        c = 0
        for i, d in enumerate(self.plan):
            if d[0] == "cast":
                self.castidx[i] = c
                c += 1

    def _issue_dma(self, j):
        kind, srcs, ncols = self.plan[j]
        stg = self.t["stg"][j % 3]
        skey = "stg%d" % (j % 3)
        for (sl, dram_ap) in srcs:
            self.P.dma("sp", sl(stg), dram_ap, writes=[skey])

    def _issue_cast(self, j):
        kind, srcs, ncols = self.plan[j]
        stg = self.t["stg"][j % 3]
        skey = "stg%d" % (j % 3)
        if kind == "cast":
            ci = self.castidx[j] % 6
            wb = self.t["wbf"][ci]
            ceng = os.environ.get("KCAST", "act")
            if ceng == "act":
                self.P.op("act", lambda e: e.activation(out=wb[:, 0:ncols], in_=stg[:, 0:ncols], func=AF.Copy),
                          reads=[skey], writes=["wbf%d" % ci])
            else:
                self.P.op(ceng, lambda e: e.tensor_copy(out=wb[:, 0:ncols], in_=stg[:, 0:ncols]),
                          reads=[skey], writes=["wbf%d" % ci])

    def pc_cols(self, w_ap, c0, nk=8, ncol=128):
        src = w_ap[:, c0:c0 + ncol].rearrange("(k p) c -> p k c", p=128)
        sl = lambda stg: stg[:, 0:nk * ncol].rearrange("p (k c) -> p k c", c=ncol)
        return ("cast", [(sl, src)], nk * ncol)

    def pc_rows(self, w_ap, r0):
        src = w_ap[r0:r0 + 128, :]
        sl = lambda stg: stg[:, 0:1024]
        return ("cast", [(sl, src)], 1024)

    def pc_raw(self, dram_ap, rows, cols):
        sl = lambda stg: stg[0:rows, 0:cols]
        return ("raw", [(sl, dram_ap)], cols)

    def load_x(self):
        t = self.t
        X = t["X"]
        for i in range(17):
            if i >= int(os.environ.get("KNX", "17")) and i < 16:
                continue
            if i == 16 and os.environ.get("KNOXS"):
                continue
            rows = 128 if i < 16 else 64
            src = self.d["xp"][i * 128:(i + 1) * 128, :] if i < 16 else self.d["xs"][:, :]
            st = self.piece(self.pc_raw(src, rows, 1024))
            ti = min(i // 4, 4)
            for b in range(2):
                ps = self.psbig()
                for q in range(4):
                    k = 4 * b + q
                    self.tr(ps[:, q * rows:(q + 1) * rows], st[0:rows, k * 128:(k + 1) * 128], rows)
                for q in range(4):
                    k = 4 * b + q
                    dst = V(X[:, k, i * 128:i * 128 + rows], "X%d_%d" % (k, ti))
                    kcp = os.environ.get("KCP", "bank")
                    if kcp == "dve":
                        eng_ = "dve"
                    elif kcp == "bank":
                        eng_ = "dve" if b == 0 else "act"
                    else:
                        eng_ = "dve" if q % 2 == 0 else "act"
                    self.cp(eng_, dst, ps[:, q * rows:(q + 1) * rows])

    def rmsnorm_fm(self, tiles, wcol0, dst_fn, scratch=None):
        t = self.t
        X = t["X"]
        cp = t["cp"]
        if scratch is None:
            so_ = self.scr_off
            sq = [self.scr(512, BF16) for _ in range(2)]
            rs = [self.scr(512) for _ in range(2)]
            self.scr_off = so_
        else:
            sq, rs = scratch
        for ti_i, (t0, n) in enumerate(tiles):
            ti = ALL_TILES.index((t0, n))
            ps = self.psbig()
            for k in range(8):
                s = sq[k % 2]
                xin = V(X[:, k, t0:t0 + n], "X%d_%d" % (k, ti))
                self.act(s[:, 0:n], xin, AF.Square)
                self.mm(ps[:, 0:n], t["onesb"], s[:, 0:n], start=(k == 0), stop=(k == 7))
            r = rs[ti_i % 2]
            self.act(r[:, 0:n], ps[:, 0:n], AF.Ln, bias=t["epsc"], scale=1.0 / D)
            self.act(r[:, 0:n], r[:, 0:n], AF.Exp, scale=-0.5)
            for k in range(8):
                xin = V(X[:, k, t0:t0 + n], "X%d_%d" % (k, ti))
                self.stt(dst_fn(k, t0, n), xin, cp[:, wcol0 + k:wcol0 + k + 1], r[:, 0:n], ALU.mult, ALU.mult)

    def load_hist(self, l):
        t = self.t
        specs = [
            ("sdc", 48, 1536, t["HISTD"], 12),
            ("slc", 48, 512, t["HISTL"], 4),
            ("slh", 16, 512, t["H0"], 4),
            ("sfc", 32, 3072, t["HISTF"], 24),
        ]
        for name, rows, cols, dst, nch in specs:
            c0 = 0
            while c0 < cols:
                cc = min(1024, cols - c0)
                st = self.piece(self.pc_raw(self.d[name][l, :, c0:c0 + cc], rows, cc))
                nb = cc // 128
                for b0 in range(0, nb, 4):
                    ps = self.psbig()
                    qn = min(4, nb - b0)
                    for q in range(qn):
                        self.tr(ps[:, q * rows:(q + 1) * rows], st[0:rows, (b0 + q) * 128:(b0 + q + 1) * 128], rows)
                    ch0 = c0 // 128 + b0
                    dstv = V(dst[:, ch0:ch0 + qn, :], name + "h")
                    self.cp("dve", dstv, V(ps.ap[:, 0:qn * rows].rearrange("p (q r) -> p q r", r=rows), ps.key))
                c0 += cc

    def out_states(self, l, src, nch, ncolp, ncols, out_p, out_s, name):
        nr = ncolp + ncols
        ost = self.scr(nch * 128, parts=nr)
        for b0 in range(0, nch, 4):
            ps = self.psbig()
            qn = min(4, nch - b0)
            for q in range(qn):
                self.tr(ps[0:nr, q * 128:(q + 1) * 128], V(src[:, b0 + q, :], name), 128)
            self.cp("act", ost[:, b0 * 128:(b0 + qn) * 128], ps[0:nr, 0:qn * 128])
        self.dma(OQ, out_p, ost.ap[0:ncolp, :], reads=[ost.key])
        self.dma(OQ, out_s, ost.ap[ncolp:nr, :], reads=[ost.key])

    def layer(self, l):
        t = self.t
        d = self.d
        cp = t["cp"]
        cm = t["cm"]
        X = t["X"]
        H = t["H"]
        OAB = t["OAB"]
        MG = t["MG"]
        cb = l * CPL
        w_in = d["w_in"][l]

        self.new_phase()
        self.memset("pool", V(t["CSTD"][:, :, :], "cstd"), 0.0)
        self.memset("pool", V(t["CSTL"][:, :, :], "cstl"), 0.0)
        self.memset("pool", V(t["CSTF"][:, :, :], "cstf"), 0.0)
        self.memset("pool", V(t["HL"][:, :, :], "hl"), 0.0)
        self.memset("pool", V(t["SP"][:, :, :], "sprompt"), 0.0)
        self.load_hist(l)
        nea = V(t["LC"][:, 0:4], "lc_nea")
        self.act(nea, cp[:, CP_AB + l * 8:CP_AB + l * 8 + 4], AF.Exp)
        self.ts("dve", nea, nea, -1.0, ALU.mult)
        lc = V(t["LC"][:, 4:8], "lc_c")
        lc2 = V(t["LC"][:, 8:12], "lc_c2")
        self.act(lc, cp[:, cb + 92:cb + 96], AF.Exp, scale=-1.0)
        self.act(lc, lc, AF.Ln, bias=1.0)
        self.ts("dve", lc2, lc, -16.0, ALU.mult)
        self.ts("dve", lc, lc, -8.0, ALU.mult)
        srcs = [(lambda stg: stg[:, 0:1024].rearrange("p (n d) -> p n d", d=128),
                 d["lruw"][l].rearrange("n c d -> c n d"))]
        wv = self.piece(("cast", srcs, 1024))
        LW = V(t["LW"][:, :], "lw")
        self.cp("dve", LW, wv)

        if KSTOP <= 1:
            return
        for p in range(2):
            self.mixer_pass(l, p)
            if KSTOP <= 6:
                return

        self.ffn(l)

    def mixer_pass(self, l, p):
        t = self.t
        d = self.d
        cp = t["cp"]
        cm = t["cm"]
        X = t["X"]
        H = t["H"]
        OAB = t["OAB"]
        MG = t["MG"]
        cb = l * CPL
        w_in = d["w_in"][l]
        tiles = PASS_TILES[p]
        base = PASS_BASE[p]
        H = t["Hm"]
        self.new_phase(("SCR", "SCR2", "MG32"))

        def hv(k, t0, n):
            return V(H[:, k, t0 - base:t0 - base + n], "H%d_%d" % (k, t0))

        self.rmsnorm_fm(tiles, cb + 0, hv)

        chunks = []
        for (t0, n) in tiles:
            if n == 512:
                for c in range(4):
                    chunks.append((t0 + c * 128, 128, t0))
            else:
                chunks.append((t0, 64, t0))
        COLS = t["COLS"]
        wab = self.piece(self.pc_cols(w_in, 2048, 8, 8))
        wabv = V(wab.ap[:, 0:64].rearrange("p (k c) -> p k c", c=8), wab.key)
        tmp4 = [self.scr(4) for _ in range(4)]
        for ci, (c0, C, t0) in enumerate(chunks):
            smp = (C == 64)
            ps = self.pssm()
            for k in range(8):
                self.mm(ps[0:C, 0:8], V(H[:, k, c0 - base:c0 - base + C], "H%d_%d" % (k, t0)), wabv[:, k, :],
                        start=(k == 0), stop=(k == 7))
            col = lambda q: V(COLS[0:C, ci, q * 4:(q + 1) * 4], "cols%d" % ci)
            g = tmp4[ci % 2]
            self.tt("dve", g[0:C, :], ps[0:C, 0:4], cp[0:C, CP_AB + l * 8 + 4:CP_AB + l * 8 + 8], ALU.add)
            self.act(g[0:C, :], g[0:C, :], AF.Exp)
            self.act(g[0:C, :], g[0:C, :], AF.Ln, bias=1.0)
            self.tt("dve", g[0:C, :], g[0:C, :], V(t["LC"][0:C, 0:4], "lc_nea"), ALU.mult)
            self.act(col(1), ps[0:C, 4:8], AF.Sigmoid)
            ps2 = self.pssm()
            uc = cm[0:C, CM_UCS:CM_UCS + C] if smp else cm[0:C, CM_UC:CM_UC + C]
            ob = cm[0:C, CM_OBS:CM_OBS + C] if smp else t["ones32"][0:C, 0:C]
            self.mm(ps2[0:C, 0:4], uc, g[0:C, :])
            self.mm(ps2[0:C, 4:8], ob, g[0:C, :])
            self.cp("dve", col(0), ps2[0:C, 0:4])
            self.ts("dve", col(2), col(1), -1.0, ALU.mult)
            self.act(col(3), ps2[0:C, 0:4], AF.Exp)
            self.tt("dve", col(4), col(1), col(3), ALU.mult)
            d4 = tmp4[2 + ci % 2]
            self.tt("dve", d4[0:C, :], ps2[0:C, 4:8], col(0), ALU.subtract)
            self.act(col(5), d4[0:C, :], AF.Exp)
            self.act(col(6), ps2[0:C, 4:8], AF.Exp)

        if KSTOP <= 2:
            return
        for hd in range(4):
            self.dn_head(l, p, hd, tiles, chunks)
            if KSTOP <= 3:
                return
        if KSTOP <= 4:
            return

        for j in range(4):
            self.lru_chunk(l, p, j, tiles)

        if KSTOP <= 5:
            return
        self.P.fence()
        mt = [self.scr(512) for _ in range(4)]
        mi = 0
        for m in range(8):
            wga = self.piece(self.pc_cols(w_in, 3080 + m * 128))
            wgb = self.piece(self.pc_cols(w_in, 4104 + m * 128))
            wba = self.piece(self.pc_cols(d["wbr"][l, 0], m * 128, 4))
            wbb = self.piece(self.pc_cols(d["wbr"][l, 1], m * 128, 4))
            v3 = lambda w, nk: V(w.ap[:, 0:nk * 128].rearrange("p (k c) -> p k c", c=128), w.key)
            wga3, wgb3, wba3, wbb3 = v3(wga, 8), v3(wgb, 8), v3(wba, 4), v3(wbb, 4)
            for (t0, n) in tiles:
                lo = t0 - base
                pga, pgb, pba, pbb = self.psbig(), self.psbig(), self.psbig(), self.psbig()
                for k in range(8):
                    self.mm(pga[:, 0:n], wga3[:, k, :], hv(k, t0, n), start=(k == 0), stop=(k == 7))
                for k in range(8):
                    self.mm(pgb[:, 0:n], wgb3[:, k, :], hv(k, t0, n), start=(k == 0), stop=(k == 7))
                for k in range(4):
                    self.mm(pba[:, 0:n], wba3[:, k, :], V(OAB[:, k, lo:lo + n], "OAB%d_%d" % (k, t0)),
                            start=(k == 0), stop=(k == 3))
                for k in range(4):
                    self.mm(pbb[:, 0:n], wbb3[:, k, :], V(OAB[:, 4 + k, lo:lo + n], "OAB%d_%d" % (4 + k, t0)),
                            start=(k == 0), stop=(k == 3))
                sa = mt[mi % 4]
                sb = mt[(mi + 1) % 4]
                mi += 2
                self.act(sa[:, 0:n], pga[:, 0:n], AF.Sigmoid)
                self.act(sb[:, 0:n], pgb[:, 0:n], AF.Sigmoid)
                self.tt("dve", sa[:, 0:n], sa[:, 0:n], pba[:, 0:n], ALU.mult)
                self.tt("dve", sb[:, 0:n], sb[:, 0:n], pbb[:, 0:n], ALU.mult)
                self.tt("dve", V(MG[:, m, lo:lo + n], "MG%d_%d" % (m, t0)), sa[:, 0:n], sb[:, 0:n], ALU.add)

        for e in range(8):
            wo = self.piece(self.pc_cols(d["w_o"][l], e * 128))
            wo3 = V(wo.ap.rearrange("p (k c) -> p k c", c=128), wo.key)
            for (t0, n) in tiles:
                lo = t0 - base
                ti = ALL_TILES.index((t0, n))
                ps = self.psbig()
                for k in range(8):
                    self.mm(ps[:, 0:n], wo3[:, k, :], V(MG[:, k, lo:lo + n], "MG%d_%d" % (k, t0)),
                            start=(k == 0), stop=(k == 7))
                xv = V(X[:, e, t0:t0 + n], "X%d_%d" % (e, ti))
                self.tt("dve", xv, xv, ps[:, 0:n], ALU.add)

        if p == 1:
            self.new_phase()
            o = self.d
            self.out_states(l, t["CSTD"], 12, 3, 48, o["o_pdc"][l], o["o_sdc"][l], "cstd")
            self.out_states(l, t["CSTL"], 4, 3, 48, o["o_plc"][l], o["o_slc"][l], "cstl")
            self.out_states(l, t["HL"], 4, 1, 16, o["o_pl"][l], o["o_sl"][l], "hl")
            for hd in range(4):
                self.dma(OQ, o["o_pd"][l, hd], t["SP"][:, hd, :], reads=["sprompt"])

    def dn_head(self, l, p, hd, tiles, chunks):
        t = self.t
        d = self.d
        cp = t["cp"]
        cm = t["cm"]
        H = t["Hm"]
        OAB = t["OAB"]
        COLS = t["COLS"]
        cb = l * CPL
        base = PASS_BASE[p]
        w_in = d["w_in"][l]
        CSTD = t["CSTD"]
        so = self.scr_off
        self.scr_off = so
        wq = self.piece(self.pc_cols(w_in, hd * 128))
        wk = self.piece(self.pc_cols(w_in, 512 + hd * 128))
        wv = self.piece(self.pc_cols(w_in, 1024 + hd * 128))
        wz = self.piece(self.pc_cols(w_in, 1536 + hd * 128))
        v3 = lambda w: V(w.ap.rearrange("p (k c) -> p k c", c=128), w.key)
        w3 = [v3(wq), v3(wk), v3(wv), v3(wz)]
        chs = [hd, 4 + hd, 8 + hd]
        cvs = [[self.scr(512) for _ in range(3)] for _ in range(2)]
        zss = [self.scr(512) for _ in range(2)]
        xp = [self.scr(520) for _ in range(3)]
        sq = [self.scr(512) for _ in range(2)]
        mark = self.scr_off
        S = V(t["SP"][:, hd, :], "sprompt")

        def proj(t0_, n_):
            lo_ = t0_ - base
            pss_ = [self.psbig() for _ in range(4)]

            def mk(i):
                def f():
                    for k in range(8):
                        self.mm(pss_[i][:, 0:n_], w3[i][:, k, :], V(H[:, k, lo_:lo_ + n_], "H%d_%d" % (k, t0_)),
                                start=(k == 0), stop=(k == 7))
                return f
            return pss_, [mk(i) for i in range(4)]

        def pre_steps(t0, n, pss, cv, zs):
            smp = (n == 64)
            steps = []
            steps.append(lambda: self.act(zs[:, 0:n], pss[3][:, 0:n], AF.Silu))
            views = {}

            def evac(i):
                ch = chs[i]
                if not smp:
                    carry = V(CSTD[:, ch, 0:3], "cstd")
                    self.cp("act", xp[i][:, 0:3], carry)
                    self.cp("act", xp[i][:, 3:3 + n], pss[i][:, 0:n])
                    self.cp("act", carry, xp[i][:, n:n + 3])
                else:
                    x3 = V(xp[i].ap[:, 0:112].rearrange("p (s c) -> p s c", c=7), xp[i].key)
                    self.cp("act", x3[:, :, 0:3], V(t["HISTD"][:, ch, :].rearrange("p (s r) -> p s r", r=3), "sdch"))
                    self.cp("act", x3[:, :, 3:7], V(pss[i].ap[:, 0:64].rearrange("p (s c) -> p s c", c=4), pss[i].key))
                    self.cp("act", V(CSTD[:, ch, 3:51].rearrange("p (s r) -> p s r", r=3), "cstd"), x3[:, :, 4:7])

            def conv(i):
                ch = chs[i]
                cwc = cb + 16
                if not smp:
                    src = lambda j: xp[i][:, j:j + n]
                    dst = cv[i][:, 0:n]
                else:
                    x3 = V(xp[i].ap[:, 0:112].rearrange("p (s c) -> p s c", c=7), xp[i].key)
                    src = lambda j: x3[:, :, j:j + 4]
                    dst = V(cv[i].ap[:, 0:64].rearrange("p (s c) -> p s c", c=4), cv[i].key)
                wc = lambda j: cp[:, cwc + j * 12 + ch:cwc + j * 12 + ch + 1]
                self.ts("dve", dst, src(0), wc(0), ALU.mult)
                for j in range(1, 4):
                    self.stt(dst, src(j), wc(j), dst, ALU.mult, ALU.add)
                self.act(cv[i][:, 0:n], cv[i][:, 0:n], AF.Silu)

            def l2a(i):
                sqb = V(sq[i].ap.bitcast(BF16)[:, 0:512], sq[i].key)
                self.act(sqb[:, 0:n], cv[i][:, 0:n], AF.Square)
                ps = self.psbig()
                self.mm(ps[:, 0:n], t["onesb"], sqb[:, 0:n])
                self.act(sq[i][:, 0:n], ps[:, 0:n], AF.Ln, bias=t["epsc"], scale=1.0)

            def l2b(i):
                self.act(sq[i][:, 0:n], sq[i][:, 0:n], AF.Exp, scale=-0.5)
                if i == 0:
                    self.stt(cv[i][:, 0:n], cv[i][:, 0:n], 128.0 ** -0.5, sq[i][:, 0:n], ALU.mult, ALU.mult)
                else:
                    self.tt("dve", cv[i][:, 0:n], cv[i][:, 0:n], sq[i][:, 0:n], ALU.mult)

            for i in range(3):
                steps.append(lambda i=i: evac(i))
            for i in range(3):
                steps.append(lambda i=i: conv(i))
            for i in range(2):
                steps.append(lambda i=i: l2a(i))
            for i in range(2):
                steps.append(lambda i=i: l2b(i))
            return steps

        pss, cl = proj(*tiles[0])
        for f_ in cl:
            f_()
        for st_ in pre_steps(tiles[0][0], tiles[0][1], pss, cvs[0], zss[0]):
            st_()
        ci = 0
        for tidx, (t0, n) in enumerate(tiles):
            lo = t0 - base
            smp = (n == 64)
            cv, zs = cvs[tidx % 2], zss[tidx % 2]
            side = []
            if tidx + 1 < len(tiles):
                nt0, nn = tiles[tidx + 1]
                npss, ncl = proj(nt0, nn)
                side = ncl + pre_steps(nt0, nn, npss, cvs[(tidx + 1) % 2], zss[(tidx + 1) % 2])
            self.scr_off = mark
            if smp:
                for st_ in side:
                    st_()
                self.delta_chunk(l, p, hd, ci, 64, True, cv[0][:, 0:64], cv[1][:, 0:64], cv[2][:, 0:64], zs[:, 0:64], S,
                                 V(OAB[:, hd, lo:lo + 64], "OAB%d_%d" % (hd, t0)))
                ci += 1
            else:
                ctxs = []
                for c in range(4):
                    a = c * 128
                    ctxs.append(dict(ci=ci, qn=cv[0][:, a:a + 128], kn=cv[1][:, a:a + 128], vs=cv[2][:, a:a + 128],
                                     zs=zs[:, a:a + 128],
                                     dst=V(OAB[:, hd, lo + a:lo + a + 128], "OAB%d_%d" % (hd, t0))))
                    ci += 1
                self.delta_group(l, hd, ctxs, S, side)
        self.scr_off = so

    def delta_chunk(self, l, p, hd, ci, C, smp, qn, kn, vs, zs, S, oab_dst):
        t = self.t
        d = self.d
        cp = t["cp"]
        cm = t["cm"]
        COLS = t["COLS"]
        cb = l * CPL
        so = self.scr_off
        col = lambda q: V(COLS[0:C, ci, q * 4 + hd:q * 4 + hd + 1], "cols%d" % ci)
        gc, beta, nbeta, egc, bge, edl, egl = [col(q) for q in range(7)]
        if smp:
            mU = cm[0:C, CM_MUS:CM_MUS + C]
            pL = cm[0:C, CM_PLS:CM_PLS + C]
        else:
            mU = cm[0:C, CM_MU:CM_MU + C]
            pL = cm[0:C, CM_PL:CM_PL + C]
        idn = cm[0:C, CM_ID:CM_ID + C]
        sc = lambda n=128, parts=128: self.scr(n, parts=parts)
        pk = self.pssm()
        pv = self.pssm()
        self.tr(pk[0:C, :], kn, 128)
        self.tr(pv[0:C, :], vs, 128)
        kbg, kd, vb = sc(), sc(), sc()
        self.act(kbg[0:C, :], pk[0:C, :], AF.Copy, scale=bge)
        self.act(kd[0:C, :], pk[0:C, :], AF.Copy, scale=edl)
        self.ts("dve", vb[0:C, :], pv[0:C, :], beta, ALU.mult)
        gcb = sc()
        self.ts("dve", gcb[0:C, :], t["ones32"][0:C, :], gc, ALU.mult)
        pg = self.pssm()
        self.tr(pg[:, 0:C], gcb[0:C, :], C)
        eT, eL = sc(), sc()
        self.stt(eT[0:C, 0:C], pg[0:C, 0:C], gc, mU, ALU.subtract, ALU.add)
        self.act(eT[0:C, 0:C], eT[0:C, 0:C], AF.Exp)
        self.stt(eL[0:C, 0:C], pg[0:C, 0:C], gc, pL, ALU.subtract, ALU.add)
        self.act(eL[0:C, 0:C], eL[0:C, 0:C], AF.Exp, scale=-1.0)
        pkk = self.pssm()
        self.mm(pkk[0:C, 0:C], kn, kn)
        N = sc()
        self.stt(N[0:C, 0:C], pkk[0:C, 0:C], nbeta, eL[0:C, 0:C], ALU.mult, ALU.mult)
        pqk = self.pssm()
        self.mm(pqk[0:C, 0:C], kn, qn)
        attnT = sc()
        self.tt("dve", attnT[0:C, 0:C], pqk[0:C, 0:C], eT[0:C, 0:C], ALU.mult)
        pn = self.pssm()
        self.tr(pn[0:C, 0:C], N[0:C, 0:C], C)
        Nt = sc()
        self.cp("act", Nt[0:C, 0:C], pn[0:C, 0:C])
        Pt = sc()
        self.tt("dve", Pt[0:C, 0:C], pn[0:C, 0:C], idn, ALU.add)
        L = 2 if smp else 7
        Ncur, Ntcur, Ptcur = N, Nt, Pt
        for k in range(1, L):
            p1 = self.pssm()
            self.mm(p1[0:C, 0:C], Ntcur[0:C, 0:C], Ncur[0:C, 0:C])
            Nn = sc()
            self.cp("act", Nn[0:C, 0:C], p1[0:C, 0:C])
            Ntn = None
            if k < L - 1:
                p2 = self.pssm()
                self.mm(p2[0:C, 0:C], Ncur[0:C, 0:C], Ntcur[0:C, 0:C])
                Ntn = sc()
                self.cp("dve", Ntn[0:C, 0:C], p2[0:C, 0:C])
            p3 = self.pssm()
            self.mm(p3[0:C, 0:C], Nn[0:C, 0:C], Ptcur[0:C, 0:C])
            Ptn = sc()
            self.tt("dve", Ptn[0:C, 0:C], p3[0:C, 0:C], Ptcur[0:C, 0:C], ALU.add)
            Ncur, Ntcur, Ptcur = Nn, Ntn, Ptn
        Pt = Ptcur
        pw = self.pssm()
        self.mm(pw[:, 0:C], kbg[0:C, :], Pt[0:C, 0:C])
        nwT = sc()
        self.act(nwT[:, 0:C], pw[:, 0:C], AF.Copy, scale=-1.0)
        o_sb = sc()
        if not smp:
            pvn = self.pssm()
            self.mm(pvn[0:C, :], Pt[0:C, 0:C], vb[0:C, :], start=True, stop=False)
            self.mm(pvn[0:C, :], nwT[:, 0:C], S, start=False, stop=True)
            vn = sc()
            self.cp("act", vn[0:C, :], pvn[0:C, :])
            pqs = self.pssm()
            self.mm(pqs[0:C, :], qn, S)
            tq = sc()
            self.act(tq[0:C, :], pqs[0:C, :], AF.Copy, scale=egc)
            pav = self.pssm()
            self.mm(pav[0:C, :], attnT[0:C, 0:C], vn[0:C, :])
            self.tt("dve", o_sb[0:C, :], tq[0:C, :], pav[0:C, :], ALU.add)
            pds = self.pssm()
            self.mm(pds[:, :], kd[0:C, :], vn[0:C, :])
            self.stt(S, S, egl, pds[:, :], ALU.mult, ALU.add)
        else:
            SS = [self.scr(128) for _ in range(NS)]
            for s in range(NS):
                self.dma("sp", SS[s].ap, d["sd"][l, s, hd], writes=[SS[s].key])
            nwm = self.scr(1088)
            qm = self.scr(1088)
            self.memset("pool", nwm, 0.0)
            self.memset("pool", qm, 0.0)
            dv = lambda b: V(b.ap.rearrange("p (s c) -> p s c", c=68)[:, :, 0:4], b.key)
            sv = lambda b: V(b.ap[:, 0:64].rearrange("p (s c) -> p s c", c=4), b.key)
            self.cp("pool", dv(nwm), sv(nwT))
            self.cp("pool", dv(qm), sv(qn))
            pvn = self.pssm()
            self.mm(pvn[0:C, :], Pt[0:C, 0:C], vb[0:C, :], start=True, stop=False)
            for s in range(NS):
                self.mm(pvn[0:C, :], nwm[:, s * 68 - s * 4:s * 68 - s * 4 + 64], SS[s], start=False, stop=(s == NS - 1))
            vn = sc()
            self.cp("act", vn[0:C, :], pvn[0:C, :])
            pqs = self.pssm()
            for s in range(NS):
                self.mm(pqs[0:C, :], qm[:, s * 64:s * 64 + 64], SS[s], start=(s == 0), stop=(s == NS - 1))
            tq = sc()
            self.act(tq[0:C, :], pqs[0:C, :], AF.Copy, scale=egc)
            pav = self.pssm()
            self.mm(pav[0:C, :], attnT[0:C, 0:C], vn[0:C, :])
            self.tt("dve", o_sb[0:C, :], tq[0:C, :], pav[0:C, :], ALU.add)
            eglb = sc()
            self.ts("dve", eglb[0:C, :], t["ones32"][0:C, :], egl, ALU.mult)
            pe_ = self.pssm()
            self.mm(pe_[:, 0:NS], eglb[0:C, :], cm[0:C, CM_SEL:CM_SEL + NS])
            egs = sc(16)
            self.cp("act", egs[:, 0:NS], pe_[:, 0:NS])
            kdm = [self.scr(128) for _ in range(2)]
            for s in range(NS):
                km = kdm[s % 2]
                self.act(km[0:C, :], kd[0:C, :], AF.Copy, scale=cm[0:C, CM_RM + s:CM_RM + s + 1])
                pds = self.pssm()
                self.mm(pds[:, :], km[0:C, :], vn[0:C, :])
                self.stt(SS[s], SS[s], egs[:, s:s + 1], pds[:, :], ALU.mult, ALU.add)
                self.dma(OQ, d["o_sd"][l, s, hd], SS[s].ap, reads=[SS[s].key])
        junk = sc()
        ss = sc(8)
        self.act(junk[0:C, :], o_sb[0:C, :], AF.Square, accum=ss[0:C, 0:1])
        self.act(ss[0:C, 0:1], ss[0:C, 0:1], AF.Ln, bias=t["epsc"][0:C, :], scale=1.0 / 128)
        self.act(ss[0:C, 0:1], ss[0:C, 0:1], AF.Exp, scale=-0.5)
        self.ts("dve", o_sb[0:C, :], o_sb[0:C, :], ss[0:C, 0:1], ALU.mult)
        po = self.pssm()
        self.tr(po[:, 0:C], o_sb[0:C, :], C)
        self.stt(oab_dst, po[:, 0:C], cp[:, cb + 168:cb + 169], zs, ALU.mult, ALU.mult)
        self.scr_off = so

    def delta_group(self, l, hd, ctxs, S, side=()):
        t = self.t
        cp = t["cp"]
        cm = t["cm"]
        COLS = t["COLS"]
        cb = l * CPL
        so = self.scr_off
        C = 128
        mU = cm[:, CM_MU:CM_MU + C]
        pL = cm[:, CM_PL:CM_PL + C]
        idn = cm[:, CM_ID:CM_ID + C]
        for cx in ctxs:
            ci = cx["ci"]
            col = lambda q, ci=ci: V(COLS[:, ci, q * 4 + hd:q * 4 + hd + 1], "cols%d" % ci)
            cx["c"] = [col(q) for q in range(7)]
            for nm in ("kbg", "kd", "vb", "gcb", "eT", "eL", "Nt", "Pt", "N2", "Nt2", "Pt2"):
                cx[nm] = self.scr(128)
            cx["nwT"] = cx["Nt2"]
            cx["vn"] = cx["N2"]
            cx["osb"] = cx["Pt2"]
            cx["junk"] = cx["Nt"]
        ssall = self.scr(8 * len(ctxs))
        for ii, cx in enumerate(ctxs):
            cx["ss"] = ssall[:, ii * 8:(ii + 1) * 8]
        side = list(side)

        def sidestep(k=1):
            for _ in range(k):
                if side:
                    side.pop(0)()
        sidestep(4)
        for cx in ctxs:
            cx["pk"] = self.pssm(); self.tr(cx["pk"], cx["kn"], 128)
        for cx in ctxs:
            cx["pv"] = self.pssm(); self.tr(cx["pv"], cx["vs"], 128)
        for cx in ctxs:
            gc, beta, nbeta, egc, bge, edl, egl = cx["c"]
            self.act(cx["kbg"], cx["pk"], AF.Copy, scale=bge)
            self.act(cx["kd"], cx["pk"], AF.Copy, scale=edl)
            self.ts("dve", cx["vb"], cx["pv"], beta, ALU.mult)
            self.ts("dve", cx["gcb"], t["ones32"], gc, ALU.mult)
        sidestep()
        for cx in ctxs:
            cx["pg"] = self.pssm(); self.tr(cx["pg"], cx["gcb"], 128)
        for cx in ctxs:
            gc = cx["c"][0]
            self.stt(cx["eT"], cx["pg"], gc, mU, ALU.subtract, ALU.add)
            self.stt(cx["eL"], cx["pg"], gc, pL, ALU.subtract, ALU.add)
        for cx in ctxs:
            self.act(cx["eT"], cx["eT"], AF.Exp)
            self.act(cx["eL"], cx["eL"], AF.Exp, scale=-1.0)
        sidestep()
        for cx in ctxs:
            cx["pkk"] = self.pssm(); self.mm(cx["pkk"], cx["kn"], cx["kn"])
        for cx in ctxs:
            cx["pqk"] = self.pssm(); self.mm(cx["pqk"], cx["kn"], cx["qn"])
        for cx in ctxs:
            self.stt(cx["eL"], cx["pkk"], cx["c"][2], cx["eL"], ALU.mult, ALU.mult)
            cx["N"] = cx["eL"]
        for cx in ctxs:
            self.tt("dve", cx["eT"], cx["pqk"], cx["eT"], ALU.mult)
            cx["attnT"] = cx["eT"]
        sidestep()
        for cx in ctxs:
            cx["pn"] = self.pssm(); self.tr(cx["pn"], cx["N"], 128)
        for cx in ctxs:
            self.cp("act", cx["Nt"], cx["pn"])
            self.tt("dve", cx["Pt"], cx["pn"], idn, ALU.add)
        L = 7
        for cx in ctxs:
            cx["cur"] = (cx["N"], cx["Nt"], cx["Pt"])
            cx["alt"] = (cx["N2"], cx["Nt2"], cx["Pt2"])
        for k in range(1, L):
            sidestep()
            for cx in ctxs:
                Nc, Ntc, Ptc = cx["cur"]
                cx["p1"] = self.pssm(); self.mm(cx["p1"], Ntc, Nc)
            if k < L - 1:
                for cx in ctxs:
                    Nc, Ntc, Ptc = cx["cur"]
                    cx["p2"] = self.pssm(); self.mm(cx["p2"], Nc, Ntc)
            for cx in ctxs:
                self.cp("act", cx["alt"][0], cx["p1"])
            if k < L - 1:
                for cx in ctxs:
                    self.cp("dve", cx["alt"][1], cx["p2"])
            for cx in ctxs:
                Nc, Ntc, Ptc = cx["cur"]
                cx["p3"] = self.pssm(); self.mm(cx["p3"], cx["alt"][0], Ptc)
            for cx in ctxs:
                Nc, Ntc, Ptc = cx["cur"]
                self.tt("dve", cx["alt"][2], cx["p3"], Ptc, ALU.add)
                cx["cur"], cx["alt"] = cx["alt"], cx["cur"]
        for cx in ctxs:
            cx["pw"] = self.pssm(); self.mm(cx["pw"], cx["kbg"], cx["cur"][2])
        for cx in ctxs:
            self.act(cx["nwT"], cx["pw"], AF.Copy, scale=-1.0)
        while side:
            side.pop(0)()
        for cx in ctxs:
            gc, beta, nbeta, egc, bge, edl, egl = cx["c"]
            Pt = cx["cur"][2]
            pvn = self.pssm()
            self.mm(pvn, Pt, cx["vb"], start=True, stop=False)
            self.mm(pvn, cx["nwT"], S, start=False, stop=True)
            pqs = self.pssm()
            self.mm(pqs, cx["qn"], S)
            self.cp("act", cx["vn"], pvn)
            tq = cx["gcb"]
            self.act(tq, pqs, AF.Copy, scale=egc)
            pav = self.pssm()
            self.mm(pav, cx["attnT"], cx["vn"])
            pds = self.pssm()
            self.mm(pds, cx["kd"], cx["vn"])
            self.stt(S, S, egl, pds, ALU.mult, ALU.add)
            self.tt("dve", cx["osb"], tq, pav, ALU.add)
        while side:
            side.pop(0)()
        for cx in ctxs:
            self.act(cx["junk"], cx["osb"], AF.Square, accum=cx["ss"][:, 0:1])
        for cx in ctxs:
            self.act(cx["ss"][:, 0:1], cx["ss"][:, 0:1], AF.Ln, bias=t["epsc"], scale=1.0 / 128)
        for cx in ctxs:
            self.act(cx["ss"][:, 0:1], cx["ss"][:, 0:1], AF.Exp, scale=-0.5)
        for cx in ctxs:
            self.ts("dve", cx["osb"], cx["osb"], cx["ss"][:, 0:1], ALU.mult)
        for cx in ctxs:
            cx["po"] = self.pssm(); self.tr(cx["po"], cx["osb"], 128)
        for cx in ctxs:
            self.stt(cx["dst"], cx["po"], cp[:, cb + 168:cb + 169], cx["zs"], ALU.mult, ALU.mult)
        self.scr_off = so

    def lru_chunk(self, l, p, j, tiles):
        t = self.t
        d = self.d
        cp = t["cp"]
        H = t["Hm"]
        OAB = t["OAB"]
        cb = l * CPL
        base = PASS_BASE[p]
        w_in = d["w_in"][l]
        CSTL = t["CSTL"]
        HL = t["HL"]
        so = self.scr_off
        wx = self.piece(self.pc_cols(w_in, 2056 + j * 128))
        wy = self.piece(self.pc_cols(w_in, 2568 + j * 128))
        v3 = lambda w: V(w.ap.rearrange("p (k c) -> p k c", c=128), w.key)
        wx3, wy3 = v3(wx), v3(wy)
        LW3 = V(t["LW"][:, :].rearrange("p (n d) -> p n d", d=128), "lw")
        xp = self.scr(520)
        xc = self.scr(512)
        xcb = self.scr(512, BF16)
        r_, i_, a_, m_, gy = [self.scr(512) for _ in range(5)]
        lc = V(t["LC"][:, 4 + j:5 + j], "lc_c")
        lc2 = V(t["LC"][:, 8 + j:9 + j], "lc_c2")
        for (t0, n) in tiles:
            lo = t0 - base
            smp = (n == 64)
            px, py = self.psbig(), self.psbig()
            for k in range(8):
                self.mm(px[:, 0:n], wx3[:, k, :], V(H[:, k, lo:lo + n], "H%d_%d" % (k, t0)), start=(k == 0), stop=(k == 7))
            for k in range(8):
                self.mm(py[:, 0:n], wy3[:, k, :], V(H[:, k, lo:lo + n], "H%d_%d" % (k, t0)), start=(k == 0), stop=(k == 7))
            self.act(gy[:, 0:n], py[:, 0:n], AF.Gelu_apprx_tanh)
            if not smp:
                carry = V(CSTL[:, j, 0:3], "cstl")
                self.cp("act", xp[:, 0:3], carry)
                self.cp("act", xp[:, 3:3 + n], px[:, 0:n])
                self.cp("act", carry, xp[:, n:n + 3])
                src = lambda jj: xp[:, jj:jj + n]
                dst = xc[:, 0:n]
            else:
                x3 = V(xp.ap[:, 0:112].rearrange("p (s c) -> p s c", c=7), xp.key)
                self.cp("act", x3[:, :, 0:3], V(t["HISTL"][:, j, :].rearrange("p (s r) -> p s r", r=3), "slch"))
                self.cp("act", x3[:, :, 3:7], V(px.ap[:, 0:64].rearrange("p (s c) -> p s c", c=4), px.key))
                self.cp("act", V(CSTL[:, j, 3:51].rearrange("p (s r) -> p s r", r=3), "cstl"), x3[:, :, 4:7])
                src = lambda jj: x3[:, :, jj:jj + 4]
                dst = V(xc.ap[:, 0:64].rearrange("p (s c) -> p s c", c=4), xc.key)
            wc = lambda jj: cp[:, cb + 64 + jj * 4 + j:cb + 64 + jj * 4 + j + 1]
            self.ts("dve", dst, src(0), wc(0), ALU.mult, cp[:, cb + 80 + j:cb + 81 + j], ALU.add)
            for jj in range(1, 4):
                self.stt(dst, src(jj), wc(jj), dst, ALU.mult, ALU.add)
            self.cp("dve", xcb[:, 0:n], xc[:, 0:n])
            pr, pi = self.psbig(), self.psbig()
            self.mm(pr[:, 0:n], LW3[:, j, :], xcb[:, 0:n])
            self.mm(pi[:, 0:n], LW3[:, 4 + j, :], xcb[:, 0:n])
            self.act(r_[:, 0:n], pr[:, 0:n], AF.Sigmoid, bias=cp[:, cb + 84 + j:cb + 85 + j])
            self.act(i_[:, 0:n], pi[:, 0:n], AF.Sigmoid, bias=cp[:, cb + 88 + j:cb + 89 + j])
            self.act(a_[:, 0:n], r_[:, 0:n], AF.Exp, scale=lc)
            self.act(m_[:, 0:n], r_[:, 0:n], AF.Exp, scale=lc2)
            self.act(m_[:, 0:n], m_[:, 0:n], AF.Ln, bias=1.0, scale=-1.0)
            self.act(m_[:, 0:n], m_[:, 0:n], AF.Exp, scale=0.5)
            if t0 == 0:
                self.memset("dve", m_[:, 0:1], 1.0)
            self.tt("dve", i_[:, 0:n], i_[:, 0:n], xc[:, 0:n], ALU.mult)
            self.tt("dve", i_[:, 0:n], i_[:, 0:n], m_[:, 0:n], ALU.mult)
            hl = V(HL[:, j, 0:1], "hl")
            if not smp:
                self.scan(r_[:, 0:n], a_[:, 0:n], i_[:, 0:n], hl)
                self.cp("dve", hl, r_[:, n - 1:n])
            else:
                for s in range(NS):
                    self.scan(r_[:, s * 4:s * 4 + 4], a_[:, s * 4:s * 4 + 4], i_[:, s * 4:s * 4 + 4],
                              V(t["H0"][:, j, s:s + 1], "slhh"))
                self.cp("dve", V(HL[:, j, 1:17], "hl"),
                        V(r_.ap[:, 0:64].rearrange("p (s c) -> p s c", c=4)[:, :, 3], r_.key))
            self.tt("dve", V(OAB[:, 4 + j, lo:lo + n], "OAB%d_%d" % (4 + j, t0)), r_[:, 0:n], gy[:, 0:n], ALU.mult)
        self.scr_off = so

    def ffn(self, l):
        t = self.t
        d = self.d
        cp = t["cp"]
        X = t["X"]
        H = t["H"]
        ACTB = t["ACTB"]
        CSTF = t["CSTF"]
        cb = l * CPL
        self.new_phase()

        def hv(k, t0, n):
            return V(H[:, k, t0:t0 + n], "H%d_%d" % (k, t0))

        self.rmsnorm_fm(ALL_TILES, cb + 8, hv)
        xp = [self.scr(520) for _ in range(2)]
        gc_ = [self.scr(512) for _ in range(3)]
        gb_ = [self.scr(512, BF16) for _ in range(3)]
        ub_ = [self.scr(512, BF16) for _ in range(3)]
        it = 0
        pend = None
        for g in range(6):
            for jj in range(4):
                f = g * 4 + jj
                wg = self.piece(self.pc_cols(d["ffn_w_in"][l], f * 128))
                wu = self.piece(self.pc_cols(d["ffn_w_in"][l], DFF + f * 128))
                v3 = lambda w: V(w.ap.rearrange("p (k c) -> p k c", c=128), w.key)
                wg3, wu3 = v3(wg), v3(wu)
                for (t0, n) in ALL_TILES:
                    smp = (n == 64)
                    pg, pu = self.psbig(), self.psbig()
                    for k in range(8):
                        self.mm(pg[:, 0:n], wg3[:, k, :], hv(k, t0, n), start=(k == 0), stop=(k == 7))
                    for k in range(8):
                        self.mm(pu[:, 0:n], wu3[:, k, :], hv(k, t0, n), start=(k == 0), stop=(k == 7))
                    x_ = xp[it % 2]
                    c_ = gc_[it % 3]
                    ub = ub_[it % 3]
                    it += 1
                    self.cp("act", ub[:, 0:n], pu[:, 0:n])
                    if not smp:
                        carry = V(CSTF[:, f, 0:2], "cstf")
                        self.cp("act", x_[:, 0:2], carry)
                        self.cp("act", x_[:, 2:2 + n], pg[:, 0:n])
                        self.cp("act", carry, x_[:, n:n + 2])
                        src = lambda j: x_[:, j:j + n]
                        dst = c_[:, 0:n]
                    else:
                        x3 = V(x_.ap[:, 0:96].rearrange("p (s c) -> p s c", c=6), x_.key)
                        self.cp("act", x3[:, :, 0:2], V(t["HISTF"][:, f, :].rearrange("p (s r) -> p s r", r=2), "sfch"))
                        self.cp("act", x3[:, :, 2:6], V(pg.ap[:, 0:64].rearrange("p (s c) -> p s c", c=4), pg.key))
                        self.cp("act", V(CSTF[:, f, 2:34].rearrange("p (s r) -> p s r", r=2), "cstf"), x3[:, :, 4:6])
                        src = lambda j: x3[:, :, j:j + 4]
                        dst = V(c_.ap[:, 0:64].rearrange("p (s c) -> p s c", c=4), c_.key)
                    wc = lambda j: cp[:, cb + 96 + j * 24 + f:cb + 96 + j * 24 + f + 1]
                    self.act(dst, src(0), AF.Copy, scale=wc(0))
                    for j in range(1, 3):
                        self.stt(dst, src(j), wc(j), dst, ALU.mult, ALU.add)
                    if pend is not None:
                        pend()

                    def _fin(c_=c_, ub=ub, n=n, jj=jj, t0=t0, gb=gb_[it % 3]):
                        self.act(gb[:, 0:n], c_[:, 0:n], AF.Gelu_apprx_tanh)
                        self.tt("dve", V(ACTB[:, jj, t0:t0 + n], "ACTB%d_%d" % (jj, t0)), gb[:, 0:n], ub[:, 0:n], ALU.mult)
                    pend = _fin
            if pend is not None:
                pend()
                pend = None
            wd = [self.piece(self.pc_rows(d["ffn_w_down"][l], (g * 4 + jj) * 128)) for jj in range(4)]
            for e in range(8):
                for (t0, n) in ALL_TILES:
                    ti = ALL_TILES.index((t0, n))
                    ps = self.psbig()
                    for jj in range(4):
                        self.mm(ps[:, 0:n], wd[jj][:, e * 128:(e + 1) * 128],
                                V(ACTB[:, jj, t0:t0 + n], "ACTB%d_%d" % (jj, t0)), start=(jj == 0), stop=(jj == 3))
                    xv = V(X[:, e, t0:t0 + n], "X%d_%d" % (e, ti))
                    self.tt("dve", xv, xv, ps[:, 0:n], ALU.add)
        self.new_phase()
        self.out_states(l, CSTF, 24, 2, 32, d["o_pfc"][l], d["o_sfc"][l], "cstf")

    def final(self):
        t = self.t
        d = self.d
        X = t["X"]
        self.new_phase(("SCR", "RB32"))
        nsc = ([self.scr(512, BF16) for _ in range(2)], [self.scr(512) for _ in range(2)])
        yf = [self.scr(512) for _ in range(8)]
        yt = [self.scr(1024) for _ in range(2)]
        bi = 0
        for (t0, n) in ALL_TILES:
            def dst(k, t0_, n_):
                return yf[k][:, 0:n_]
            self.rmsnorm_fm([(t0, n)], CP_FIN, dst, scratch=nsc)
            blocks = [(t0 + b * 128, 128) for b in range(4)] if n == 512 else [(t0, 64)]
            for (b0, C) in blocks:
                y = yt[bi % 2]
                bi += 1
                for hb in range(2):
                    ps = self.psbig()
                    for q in range(4):
                        k = hb * 4 + q
                        self.tr(ps[0:C, q * 128:(q + 1) * 128], yf[k][:, b0 - t0:b0 - t0 + C], 128)
                    self.cp("act" if hb == 0 else "dve", y[0:C, hb * 512:(hb + 1) * 512], ps[0:C, 0:512])
                if C == 128:
                    self.dma(OQ, d["o_yp"][b0:b0 + 128, :], y.ap[0:128, :], reads=[y.key])
                else:
                    self.dma(OQ, d["o_ys"][:, :], y.ap[0:64, :], reads=[y.key])

    def build(self):
        t = self.t
        self.dma("sp", t["cp"].ap, self.d["cst"][:, 0:CP_N], writes=["cp"])
        self.dma("sp", t["cm"].ap, self.d["cst"][:, CP_N:CP_N + CM_N], writes=["cm"])
        self.memset("pool", t["onesb"], 1.0)
        self.memset("pool", t["ones32"], 1.0)
        self.memset("pool", t["epsc"], EPS)
        if KSTOP >= -1:
            self.load_x()
        if KSTOP >= 1:
            for l in range(self.depth):
                self.layer(l)
        if KSTOP >= 0:
            self.final()
        self.P.final_wait_all("sp")


def build_program(depth=DEPTH):
    nc = bass.Bass("TRN2", target_bir_lowering=False)
    dram = {}

    def din(name, shape):
        if os.environ.get("KNOIN") and name not in ("cst",):
            return
        dram[name] = nc.dram_tensor(name, list(shape), F32, kind="ExternalInput").ap()

    def dout(name, shape):
        if os.environ.get("KNOOUT") and name not in os.environ.get("KNOOUT").split(","):
            return
        dram[name] = nc.dram_tensor(name, list(shape), F32, kind="ExternalOutput").ap()

    din("xp", (TP, D)); din("xs", (64, D))
    din("sdc", (4, 48, 1536)); din("sd", (4, NS, 4, 128, 128)); din("slc", (4, 48, 512))
    din("slh", (4, NS, 512)); din("sfc", (4, 32, DFF))
    din("w_in", (4, D, INC)); din("lruw", (4, 8, 128, 128))
    din("wbr", (4, 2, 512, D)); din("w_o", (4, D, D))
    din("ffn_w_in", (4, D, 2 * DFF)); din("ffn_w_down", (4, DFF, D))
    din("cst", (128, CP_N + CM_N))
    dout("o_yp", (TP, D)); dout("o_ys", (64, D))
    dout("o_pd", (4, 4, 128, 128)); dout("o_sd", (4, NS, 4, 128, 128))
    dout("o_ps", (4, 12800)); dout("o_ss", (4, 204800))
    if "o_ps" not in dram or "o_ss" not in dram:
        dram["o_ps"] = nc.dram_tensor("o_ps_i", [4, 12800], F32, kind="Internal").ap() if "o_ps" not in dram else dram["o_ps"]
        dram["o_ss"] = nc.dram_tensor("o_ss_i", [4, 204800], F32, kind="Internal").ap() if "o_ss" not in dram else dram["o_ss"]
    ops, oss = dram["o_ps"], dram["o_ss"]
    dram["o_pdc"] = [ops[l, 0:4608].rearrange("(r c) -> r c", c=1536) for l in range(4)]
    dram["o_plc"] = [ops[l, 4608:6144].rearrange("(r c) -> r c", c=512) for l in range(4)]
    dram["o_pl"] = [ops[l, 6144:6656].rearrange("(r c) -> r c", c=512) for l in range(4)]
    dram["o_pfc"] = [ops[l, 6656:12800].rearrange("(r c) -> r c", c=DFF) for l in range(4)]
    dram["o_sdc"] = [oss[l, 0:73728].rearrange("(r c) -> r c", c=1536) for l in range(4)]
    dram["o_slc"] = [oss[l, 73728:98304].rearrange("(r c) -> r c", c=512) for l in range(4)]
    dram["o_sl"] = [oss[l, 98304:106496].rearrange("(r c) -> r c", c=512) for l in range(4)]
    dram["o_sfc"] = [oss[l, 106496:204800].rearrange("(r c) -> r c", c=DFF) for l in range(4)]

    SCRN = 6656
    with contextlib.ExitStack() as es:
        def SB(name, shape, dt=F32):
            return es.enter_context(nc.sbuf_tensor(name, list(shape), dt))

        def PS(name, shape):
            return es.enter_context(nc.psum_tensor(name, list(shape), F32))

        tens = {}
        tens["X"] = SB("X", (128, 8, T))
        Hfull = SB("H", (128, 8, T), BF16)
        tens["H"] = Hfull
        Hflat = Hfull[:, :, :].rearrange("p a b -> p (a b)")
        tens["Hm"] = Hflat[:, 0:8 * HT].rearrange("p (a b) -> p a b", b=HT)
        tens["SCR2"] = Hflat[:, 8 * HT:8 * T].bitcast(F32)
        RB = SB("RB", (128, 2, 8, HT), BF16)
        RBflat = RB[:, :, :, :].rearrange("p c a b -> p (c a b)")
        tens["RB32"] = RBflat.bitcast(F32)
        tens["MG32"] = RB[:, 1, :, :].rearrange("p a b -> p (a b)").bitcast(F32)
        tens["OAB"] = RB[:, 0, :, :]
        tens["MG"] = RB[:, 1, :, :]
        tens["ACTB"] = RBflat[:, 0:4 * T].rearrange("p (a b) -> p a b", b=T)
        tens["stg"] = [SB("stg%d" % i, (128, 1024)) for i in range(3)]
        tens["wbf"] = [SB("wbf%d" % i, (128, 1024), BF16) for i in range(6)]
        tens["SCR"] = SB("SCR", (128, SCRN))[:, :]
        tens["SCRN"] = SCRN
        tens["cp"] = V(SB("cp", (128, CP_N))[:, :], "cp")
        tens["cm"] = V(SB("cm", (128, CM_N))[:, :], "cm")
        tens["onesb"] = V(SB("onesb", (128, 128), BF16)[:, :], "onesb")
        tens["ones32"] = V(SB("ones32", (128, 128))[:, :], "ones32")
        tens["epsc"] = V(SB("epsc", (128, 1))[:, :], "epsc")
        tens["LC"] = SB("LC", (128, 12))
        tens["LW"] = SB("LW", (128, 1024), BF16)
        tens["COLS"] = SB("COLS", (128, 9, 28))
        tens["CSTD"] = SB("CSTD", (128, 12, 51))
        tens["CSTL"] = SB("CSTL", (128, 4, 51))
        tens["CSTF"] = SB("CSTF", (128, 24, 34))
        tens["HL"] = SB("HL", (128, 4, 17))
        tens["SP"] = SB("SP", (128, 4, 128))
        tens["HISTD"] = SB("HISTD", (128, 12, 48))
        tens["HISTL"] = SB("HISTL", (128, 4, 48))
        tens["HISTF"] = SB("HISTF", (128, 24, 32))
        tens["H0"] = SB("H0", (128, 4, 16))
        tens["psb"] = [PS("psb%d" % i, (128, 512)) for i in range(4)]
        tens["pss"] = [PS("pss%d" % i, (128, 512)) for i in range(4)]

        Pd = Prog(nc, dry=True)
        bd = Builder(nc, Pd, dram, tens, None, depth)
        bd.build()
        plan = bd.rec
        P = Prog(nc)
        b = Builder(nc, P, dram, tens, plan, depth)
        b.prep_plan()
        b.build()
        assert b.wpos == len(plan), (b.wpos, len(plan))
        P.run()
        nops = P.nops
    return nc, nops


def make_consts():
    cm = np.zeros((128, CM_N), np.float32)
    i = np.arange(128)
    cm[:, CM_ID:CM_ID + 128] = np.eye(128, dtype=np.float32)
    cm[:, CM_MU:CM_MU + 128] = np.where(i[None, :] >= i[:, None], 0.0, -1e30)
    cm[:, CM_PL:CM_PL + 128] = np.where(i[:, None] > i[None, :], 0.0, 1e30)
    cm[:, CM_UC:CM_UC + 128] = (i[:, None] <= i[None, :]).astype(np.float32)
    s = np.arange(64)
    same = (s[:, None] // 4) == (s[None, :] // 4)
    cm[:64, CM_MUS:CM_MUS + 64] = np.where(same & (s[None, :] >= s[:, None]), 0.0, -1e30)
    cm[:64, CM_PLS:CM_PLS + 64] = np.where(same & (s[:, None] > s[None, :]), 0.0, 1e30)
    cm[:64, CM_UCS:CM_UCS + 64] = (same & (s[:, None] <= s[None, :])).astype(np.float32)
    cm[:64, CM_OBS:CM_OBS + 64] = same.astype(np.float32)
    cm[:64, CM_RM:CM_RM + 16] = ((s[:, None] // 4) == np.arange(16)[None, :]).astype(np.float32)
    cm[:64, CM_SEL:CM_SEL + 16] = (s[:, None] == (4 * np.arange(16)[None, :] + 3)).astype(np.float32)
    return cm


def pack_params(inp):
    cpk = np.zeros((128, CP_N), np.float32)

    def fm(v, n):
        return np.ascontiguousarray(v.reshape(n, 128).T)

    for l in range(4):
        b = l * CPL
        cpk[:, b + 0:b + 8] = fm(inp["norm1_w"][l], 8)
        cpk[:, b + 8:b + 16] = fm(inp["norm2_w"][l], 8)
        for j in range(4):
            cpk[:, b + 16 + j * 12:b + 16 + (j + 1) * 12] = fm(inp["dn_conv_w"][l, j], 12)
            cpk[:, b + 64 + j * 4:b + 64 + (j + 1) * 4] = fm(inp["lru_conv_w"][l, j], 4)
        cpk[:, b + 80:b + 84] = fm(inp["lru_conv_b"][l], 4)
        cpk[:, b + 84:b + 88] = fm(inp["lru_ba"][l], 4)
        cpk[:, b + 88:b + 92] = fm(inp["lru_bx"][l], 4)
        cpk[:, b + 92:b + 96] = fm(inp["lru_lambda"][l], 4)
        for j in range(3):
            cpk[:, b + 96 + j * 24:b + 96 + (j + 1) * 24] = fm(inp["ffn_conv_w"][l, j], 24)
        cpk[:, b + 168] = inp["dn_norm_w"][l]
        cpk[:, CP_AB + l * 8:CP_AB + l * 8 + 4] = inp["dn_A_log"][l][None, :]
        cpk[:, CP_AB + l * 8 + 4:CP_AB + l * 8 + 8] = inp["dn_dt_bias"][l][None, :]
    cpk[:, CP_FIN:CP_FIN + 8] = fm(inp["final_norm_w"], 8)
    return cpk


_CACHE = {}


def kernel(**inp):
    depth = DEPTH
    if "nc" not in _CACHE:
        _CACHE["nc"] = build_program(depth)
    nc, _ = _CACHE["nc"]
    f = lambda a: np.ascontiguousarray(np.asarray(a, dtype=np.float32))
    cpk = pack_params({k: np.asarray(v) for k, v in inp.items()})
    cmk = make_consts()
    shared = {
        "w_in": f(inp["w_in"]),
        "lruw": np.ascontiguousarray(np.concatenate([f(inp["lru_wa"]), f(inp["lru_wx"])], axis=1)),
        "wbr": np.ascontiguousarray(np.stack([f(inp["w_branch_a"]), f(inp["w_branch_b"])], axis=1)),
        "w_o": f(inp["w_o"]),
        "ffn_w_in": f(inp["ffn_w_in"]), "ffn_w_down": f(inp["ffn_w_down"]),
        "cst": np.ascontiguousarray(np.concatenate([cpk, cmk], axis=1)),
    }
    in_maps = []
    for c in range(8):
        s0, s1 = c * NS, (c + 1) * NS
        m = dict(shared)
        m["xp"] = f(inp["x_prompt"][c])
        m["xs"] = f(inp["x_sample"][s0:s1]).reshape(64, D)
        m["sdc"] = f(inp["state_dn_conv"][:, s0:s1]).reshape(4, 48, 1536)
        m["sd"] = f(inp["state_dn"][:, s0:s1])
        m["slc"] = f(inp["state_lru_conv"][:, s0:s1]).reshape(4, 48, 512)
        m["slh"] = f(inp["state_lru"][:, s0:s1])
        m["sfc"] = f(inp["state_ffn_conv"][:, s0:s1]).reshape(4, 32, DFF)
        in_maps.append(m)
    res = run_bass_kernel_spmd(nc, in_maps, core_ids=list(range(8)))
    R = res.results
    y_p = np.stack([R[c]["o_yp"] for c in range(8)], 0)
    y_s = np.concatenate([R[c]["o_ys"].reshape(NS, 4, D) for c in range(8)], 0)
    PS = [R[c]["o_ps"] for c in range(8)]
    SS = [R[c]["o_ss"] for c in range(8)]
    p_dc = np.stack([PS[c][:, 0:4608].reshape(4, 3, 1536) for c in range(8)], 1)
    p_d = np.stack([R[c]["o_pd"] for c in range(8)], 1)
    p_lc = np.stack([PS[c][:, 4608:6144].reshape(4, 3, 512) for c in range(8)], 1)
    p_l = np.stack([PS[c][:, 6144:6656].reshape(4, 512) for c in range(8)], 1)
    p_fc = np.stack([PS[c][:, 6656:12800].reshape(4, 2, DFF) for c in range(8)], 1)
    s_dc = np.concatenate([SS[c][:, 0:73728].reshape(4, NS, 3, 1536) for c in range(8)], 1)
    s_d = np.concatenate([R[c]["o_sd"] for c in range(8)], 1)
    s_lc = np.concatenate([SS[c][:, 73728:98304].reshape(4, NS, 3, 512) for c in range(8)], 1)
    s_l = np.concatenate([SS[c][:, 98304:106496].reshape(4, NS, 512) for c in range(8)], 1)
    s_fc = np.concatenate([SS[c][:, 106496:204800].reshape(4, NS, 2, DFF) for c in range(8)], 1)
    outs = (y_p, y_s, p_dc, p_d, p_lc, p_l, p_fc, s_dc, s_d, s_lc, s_l, s_fc)
    return tuple(np.ascontiguousarray(o, dtype=np.float32) for o in outs)
```

```python
import contextlib
import os
import numpy as np
import concourse.bass as bass
import concourse.mybir as mybir
from concourse.bass_utils import run_bass_kernel_spmd

F32 = mybir.dt.float32
BF16 = mybir.dt.bfloat16
AF = mybir.ActivationFunctionType
ALU = mybir.AluOpType

ENGS = ("pe", "act", "dve", "pool", "sp")

DEPTH = 4
D = 1024
T = 2112
TP = 2048
NS = 16
INC = 5128
DFF = 3072
EPS = 1e-6
import os
KSTOP = int(os.environ.get("KSTOP", "99"))
OQ = os.environ.get("KOQ", "sp")
CPL = 169
CP_FIN = 4 * CPL
CP_AB = CP_FIN + 8
CP_N = CP_AB + 32
CM_ID = 0
CM_MU = 128
CM_PL = 256
CM_UC = 384
CM_MUS = 512
CM_PLS = 576
CM_UCS = 640
CM_OBS = 704
CM_RM = 768
CM_SEL = 784
CM_N = 800

PASS_TILES = [[(0, 512), (512, 512)], [(1024, 512), (1536, 512), (2048, 64)]]
PASS_BASE = [0, 1024]
ALL_TILES = [(0, 512), (512, 512), (1024, 512), (1536, 512), (2048, 64)]
HT = 1088


class Op:
    __slots__ = ("eng", "fn", "waits", "flag", "idx", "dma", "dsem", "dval")

    def __init__(self, eng, fn):
        self.eng = eng
        self.fn = fn
        self.waits = []
        self.flag = False
        self.idx = -1
        self.dma = False
        self.dsem = None
        self.dval = 0


class Prog:
    def __init__(self, nc, n_dma_sems=int(os.environ.get("KNS", "32")), dry=False):
        self.nc = nc
        self.dry = dry
        self.streams = {e: [] for e in ENGS}
        if not dry:
            self.esem = {e: nc.alloc_semaphore(name="s_" + e) for e in ENGS}
            self.dsems = [nc.alloc_semaphore(name="d%d" % i) for i in range(n_dma_sems)]
        else:
            self.esem = {e: None for e in ENGS}
            self.dsems = [i for i in range(n_dma_sems)]
        self.dlast = [None] * n_dma_sems
        self.dvals = [0] * n_dma_sems
        self.dnext = 0
        self.last_w = {}
        self.readers = {}
        self.waited = {}
        self.dwaited = {}
        self.nops = 0

    def _add_wait(self, op, dep):
        f = op.eng
        if dep.dma:
            k = (f, id(dep.dsem) if not self.dry else dep.dsem)
            if self.dwaited.get(k, 0) >= dep.dval:
                return
            self.dwaited[k] = dep.dval
            op.waits.append(dep)
        else:
            if dep.eng == f and False:
                return
            k = (f, dep.eng)
            if self.waited.get(k, -1) >= dep.idx:
                return
            self.waited[k] = dep.idx
            dep.flag = True
            op.waits.append(dep)

    def _deps(self, op, reads, writes):
        for k in reads:
            w = self.last_w.get(k)
            if w is not None and (w.dma or w.eng != op.eng or op.eng != "pe"):
                self._add_wait(op, w)
            if k.startswith("ps"):
                for r in self.readers.get(k, ()):
                    if r.eng != op.eng:
                        self._add_wait(op, r)
        for k in writes:
            w = self.last_w.get(k)
            if w is not None and (w.dma or w.eng != op.eng or op.eng != "pe"):
                self._add_wait(op, w)
            for r in self.readers.get(k, ()):
                if r is not op and (r.dma or r.eng != op.eng or op.eng != "pe"):
                    self._add_wait(op, r)
        for k in reads:
            self.readers.setdefault(k, []).append(op)
        for k in writes:
            self.last_w[k] = op
            self.readers[k] = []

    def op(self, eng, fn, reads=(), writes=()):
        o = Op(eng, fn)
        o.idx = len(self.streams[eng])
        self._deps(o, reads, writes)
        self.streams[eng].append(o)
        self.nops += 1
        return o

    def dma(self, eng, out, in_, reads=(), writes=()):
        o = Op(eng, None)
        o.dma = True
        o.idx = len(self.streams[eng])
        si = self.dnext
        self.dnext = (self.dnext + 1) % len(self.dsems)
        prev = self.dlast[si]
        if prev is not None:
            self._add_wait(o, prev)
        self._deps(o, reads, writes)
        o.dsem = self.dsems[si]
        self.dvals[si] += 16
        o.dval = self.dvals[si]
        self.dlast[si] = o
        o.fn = (out, in_)
        self.streams[eng].append(o)
        self.nops += 1
        return o

    def fence(self):
        lasts = {}
        for en in ENGS:
            for o in reversed(self.streams[en]):
                if not o.dma and o.fn is not None:
                    lasts[en] = o
                    break
        dl = [d for d in self.dlast if d is not None]
        for en in ENGS:
            o = Op(en, None)
            o.idx = len(self.streams[en])
            for en2, l in lasts.items():
                if en2 != en:
                    self._add_wait(o, l)
            for d in dl:
                self._add_wait(o, d)
            self.streams[en].append(o)
        self.last_w = {}
        self.readers = {}

    def final_wait_all(self, ename="sp"):
        o = Op(ename, None)
        o.idx = len(self.streams[ename])
        for prev in self.dlast:
            if prev is not None:
                self._add_wait(o, prev)
        for en in ENGS:
            if en != ename:
                for l in reversed(self.streams[en]):
                    if not l.dma and l.fn is not None:
                        self._add_wait(o, l)
                        break
        self.streams[ename].append(o)

    def prepass(self):
        for ename in ENGS:
            cnt = 0
            for o in self.streams[ename]:
                if not o.dma and o.flag:
                    cnt += 1
                    o.dval = cnt

    def emit(self, ename, e):
        for o in self.streams[ename]:
            for d in o.waits:
                if d.dma:
                    e.wait_ge(d.dsem, d.dval)
                else:
                    e.wait_ge(self.esem[d.eng], d.dval)
            if o.dma:
                out, in_ = o.fn
                e.dma_start(out=out, in_=in_).then_inc(o.dsem, 16)
            elif o.fn is not None:
                ins = o.fn(e)
                if o.flag:
                    ins.then_inc(self.esem[ename], 1)
            else:
                assert not o.flag

    def run(self):
        nc = self.nc
        self.prepass()
        with nc.Block() as block:
            @block.tensor
            def _(e):
                self.emit("pe", e)

            @block.scalar
            def _(e):
                self.emit("act", e)

            @block.vector
            def _(e):
                self.emit("dve", e)

            @block.gpsimd
            def _(e):
                self.emit("pool", e)

            @block.sync
            def _(e):
                self.emit("sp", e)


class V:
    __slots__ = ("ap", "key")

    def __init__(self, ap, key):
        self.ap = ap
        self.key = key

    def __getitem__(self, idx):
        return V(self.ap[idx], self.key)

    def k(self, key):
        return V(self.ap, key)


class Builder:
    def __init__(self, nc, P, dram, tens, plan, depth):
        self.nc = nc
        self.P = P
        self.d = dram
        self.t = tens
        self.depth = depth
        self.plan = plan
        self.rec = []
        self.wpos = 0
        self.wissued = 0
        self.dma_next = 0
        self.cast_next = 0
        self.uid = 0
        self.big_i = 0
        self.sm_i = 0
        self.scr_off = 0
        self.scr_phase = 0

    def _rk(self, *vs):
        out = []
        for v in vs:
            if isinstance(v, V):
                if isinstance(v.key, tuple):
                    out.extend(v.key)
                else:
                    out.append(v.key)
        return out

    def _wk(self, *vs):
        return self._rk(*vs)

    def mm(self, out, lhsT, rhs, start=True, stop=True):
        self.P.op("pe", lambda e: e.matmul(out.ap, lhsT=lhsT.ap, rhs=rhs.ap, start=start, stop=stop),
                  reads=self._rk(lhsT, rhs), writes=self._wk(out))

    def tr(self, out, in_, np_):
        idn = self.t["cm"][0:np_, CM_ID:CM_ID + np_]
        shp = tuple(in_.ap.shape)
        if len(shp) == 2 and shp[0] == 128 and shp[1] == 128:
            self.P.op("pe", lambda e: e.transpose(out.ap, in_.ap, idn.ap),
                      reads=self._rk(in_, idn), writes=self._wk(out))
        else:
            self.P.op("pe", lambda e: e.matmul(out.ap, lhsT=in_.ap, rhs=idn.ap, start=True, stop=True),
                      reads=self._rk(in_, idn), writes=self._wk(out))

    def act(self, out, in_, func, bias=None, scale=None, accum=None):
        kw = {}
        rd = [in_]
        if bias is not None:
            kw["bias"] = bias.ap if isinstance(bias, V) else bias
            rd.append(bias)
        if scale is not None:
            kw["scale"] = scale.ap if isinstance(scale, V) else scale
            rd.append(scale)
        wr = self._wk(out)
        if accum is not None:
            kw["accum_out"] = accum.ap
            wr = wr + self._wk(accum)
        self.P.op("act", lambda e: e.activation(out=out.ap, in_=in_.ap, func=func, **kw),
                  reads=self._rk(*rd), writes=wr)

    def ts(self, eng, out, in0, s1, op0, s2=None, op1=None):
        a1 = s1.ap if isinstance(s1, V) else s1
        a2 = s2.ap if isinstance(s2, V) else s2
        if op1 is None:
            fn = lambda e: e.tensor_scalar(out=out.ap, in0=in0.ap, scalar1=a1, scalar2=None, op0=op0)
        else:
            fn = lambda e: e.tensor_scalar(out=out.ap, in0=in0.ap, scalar1=a1, scalar2=a2, op0=op0, op1=op1)
        self.P.op(eng, fn, reads=self._rk(in0, s1, s2), writes=self._wk(out))

    def stt(self, out, in0, scalar, in1, op0, op1):
        a = scalar.ap if isinstance(scalar, V) else scalar
        self.P.op("dve", lambda e: e.scalar_tensor_tensor(out=out.ap, in0=in0.ap, scalar=a, in1=in1.ap,
                                                          op0=op0, op1=op1),
                  reads=self._rk(in0, scalar, in1), writes=self._wk(out))

    def tt(self, eng, out, in0, in1, op):
        self.P.op(eng, lambda e: e.tensor_tensor(out=out.ap, in0=in0.ap, in1=in1.ap, op=op),
                  reads=self._rk(in0, in1), writes=self._wk(out))

    def cp(self, eng, out, in_):
        if eng == "act":
            self.act(out, in_, AF.Copy)
        else:
            self.P.op(eng, lambda e: e.tensor_copy(out=out.ap, in_=in_.ap), reads=self._rk(in_), writes=self._wk(out))

    def memset(self, eng, out, val):
        self.P.op(eng, lambda e: e.memset(out.ap, val), writes=self._wk(out))

    def recip(self, out, in_):
        self.P.op("dve", lambda e: e.reciprocal(out=out.ap, in_=in_.ap), reads=self._rk(in_), writes=self._wk(out))

    def scan(self, out, d0, d1, init):
        a = init.ap if isinstance(init, V) else init
        self.P.op("dve", lambda e: e.tensor_tensor_scan(out=out.ap, data0=d0.ap, data1=d1.ap, initial=a,
                                                        op0=ALU.mult, op1=ALU.add),
                  reads=self._rk(d0, d1, init), writes=self._wk(out))

    def dma(self, eng, out_ap, in_ap, reads=(), writes=()):
        def fl(ks):
            o = []
            for k in ks:
                if isinstance(k, tuple):
                    o.extend(k)
                else:
                    o.append(k)
            return o
        self.P.dma(eng, out_ap, in_ap, reads=fl(reads), writes=fl(writes))

    def psbig(self):
        if getattr(self, "use8", False):
            i = self.big8_i = (getattr(self, "big8_i", -1) + 1) % 8
            if i < 4:
                return V(self.t["psb"][i][:, :], "psb%d" % i)
            return V(self.t["pss"][i - 4][:, :], "pss%d" % (i - 4))
        i = self.big_i
        self.big_i = (i + 1) % 4
        return V(self.t["psb"][i][:, :], "psb%d" % i)

    def pssm(self):
        i = self.sm_i
        self.sm_i = (i + 1) % 16
        q, b = divmod(i, 4)
        return V(self.t["pss"][b][:, q * 128:(q + 1) * 128], "pss%d" % b)

    def new_phase(self, segs=("SCR",)):
        self.P.fence()
        self.scr_off = 0
        self.scr_phase += 1
        self.segs = []
        o = 0
        for nm in segs:
            ap = self.t[nm]
            n = ap.shape[1]
            self.segs.append((o, n, ap))
            o += n
        self.scr_total = o

    def scr(self, cols, dt=F32, parts=128):
        n32 = cols if dt == F32 else (cols + 1) // 2
        n32 = (n32 + 7) // 8 * 8
        off = self.scr_off
        if n32 >= 128:
            n32 = (n32 + 127) // 128 * 128
            off = (off + 127) // 128 * 128
        seg = None
        for (o, n, ap) in self.segs:
            if off < o:
                off = o
            if off >= o and off + n32 <= o + n:
                seg = (o, n, ap)
                break
        assert seg is not None, ("scratch overflow", self.scr_off, n32, self.scr_total)
        self.scr_off = off + n32
        self.scr_peak = max(getattr(self, "scr_peak", 0), self.scr_off)
        o, n, sap = seg
        ap = sap[0:parts, off - o:off - o + n32]
        if dt != F32:
            ap = ap.bitcast(BF16)[:, 0:cols]
        else:
            ap = ap[:, 0:cols]
        keys = tuple("scrB%d" % b for b in range(off // 128, (off + n32 - 1) // 128 + 1))
        return V(ap, keys)

    def piece(self, desc):
        if self.plan is None:
            self.rec.append(desc)
            i = len(self.rec) - 1
            kind = desc[0]
            if kind == "cast":
                return V(self.t["wbf"][i % 6][:, :], "wbf%d" % (i % 6))
            return V(self.t["stg"][i % 3][:, :], "stg%d" % (i % 3))
        i = self.wpos
        self.wpos += 1
        last = len(self.plan) - 1
        progressed = True
        while progressed:
            progressed = False
            if self.cast_next <= min(i + 1, last):
                j = self.cast_next
                if self.plan[j][0] != "cast":
                    self.cast_next += 1
                    progressed = True
                elif self.dma_next > j:
                    self._issue_cast(j)
                    self.cast_next += 1
                    progressed = True
            if self.dma_next <= min(i + 3, last):
                j = self.dma_next
                k = j - 3
                ok = k < 0 or (self.plan[k][0] == "cast" and self.cast_next > k) or \
                    (self.plan[k][0] != "cast" and i >= k + 1)
                if ok:
                    self._issue_dma(j)
                    self.dma_next += 1
                    progressed = True
        assert self.dma_next > i and self.cast_next > i, (i, self.dma_next, self.cast_next)
        kind = self.plan[i][0]
        if kind == "cast":
            return V(self.t["wbf"][self.castidx[i] % 6][:, :], "wbf%d" % (self.castidx[i] % 6))
        return V(self.t["stg"][i % 3][:, :], "stg%d" % (i % 3))

    def prep_plan(self):
        self.castidx = {}
        c = 0
        for i, d in enumerate(self.plan):
            if d[0] == "cast":
                self.castidx[i] = c
                c += 1

    def _issue_dma(self, j):
        kind, srcs, ncols = self.plan[j]
        stg = self.t["stg"][j % 3]
        skey = "stg%d" % (j % 3)
        for (sl, dram_ap) in srcs:
            self.P.dma("sp", sl(stg), dram_ap, writes=[skey])

    def _issue_cast(self, j):
        kind, srcs, ncols = self.plan[j]
        stg = self.t["stg"][j % 3]
        skey = "stg%d" % (j % 3)
        if kind == "cast":
            ci = self.castidx[j] % 6
            wb = self.t["wbf"][ci]
            ceng = os.environ.get("KCAST", "act")
            if ceng == "act":
                self.P.op("act", lambda e: e.activation(out=wb[:, 0:ncols], in_=stg[:, 0:ncols], func=AF.Copy),
                          reads=[skey], writes=["wbf%d" % ci])
            else:
                self.P.op(ceng, lambda e: e.tensor_copy(out=wb[:, 0:ncols], in_=stg[:, 0:ncols]),
                          reads=[skey], writes=["wbf%d" % ci])

    def pc_cols(self, w_ap, c0, nk=8, ncol=128):
        src = w_ap[:, c0:c0 + ncol].rearrange("(k p) c -> p k c", p=128)
        sl = lambda stg: stg[:, 0:nk * ncol].rearrange("p (k c) -> p k c", c=ncol)
        return ("cast", [(sl, src)], nk * ncol)

    def pc_rows(self, w_ap, r0):
        src = w_ap[r0:r0 + 128, :]
        sl = lambda stg: stg[:, 0:1024]
        return ("cast", [(sl, src)], 1024)

    def pc_raw(self, dram_ap, rows, cols):
        sl = lambda stg: stg[0:rows, 0:cols]
        return ("raw", [(sl, dram_ap)], cols)

    def load_x(self):
        t = self.t
        X = t["X"]
        for i in range(17):
            if i >= int(os.environ.get("KNX", "17")) and i < 16:
                continue
            if i == 16 and os.environ.get("KNOXS"):
                continue
            rows = 128 if i < 16 else 64
            src = self.d["xp"][i * 128:(i + 1) * 128, :] if i < 16 else self.d["xs"][:, :]
            st = self.piece(self.pc_raw(src, rows, 1024))
            ti = min(i // 4, 4)
            for b in range(2):
                ps = self.psbig()
                for q in range(4):
                    k = 4 * b + q
                    self.tr(ps[:, q * rows:(q + 1) * rows], st[0:rows, k * 128:(k + 1) * 128], rows)
                for q in range(4):
                    k = 4 * b + q
                    dst = V(X[:, k, i * 128:i * 128 + rows], "X%d_%d" % (k, ti))
                    kcp = os.environ.get("KCP", "bank")
                    if kcp == "dve":
                        eng_ = "dve"
                    elif kcp == "bank":
                        eng_ = "dve" if b == 0 else "act"
                    else:
                        eng_ = "dve" if q % 2 == 0 else "act"
                    self.cp(eng_, dst, ps[:, q * rows:(q + 1) * rows])

    def rmsnorm_fm(self, tiles, wcol0, dst_fn, scratch=None):
        t = self.t
        X = t["X"]
        cp = t["cp"]
        if scratch is None:
            so_ = self.scr_off
            sq = [self.scr(512, BF16) for _ in range(2)]
            rs = [self.scr(512) for _ in range(2)]
            self.scr_off = so_
        else:
            sq, rs = scratch
        for ti_i, (t0, n) in enumerate(tiles):
            ti = ALL_TILES.index((t0, n))
            ps = self.psbig()
            for k in range(8):
                s = sq[k % 2]
                xin = V(X[:, k, t0:t0 + n], "X%d_%d" % (k, ti))
                self.act(s[:, 0:n], xin, AF.Square)
                self.mm(ps[:, 0:n], t["onesb"], s[:, 0:n], start=(k == 0), stop=(k == 7))
            r = rs[ti_i % 2]
            self.act(r[:, 0:n], ps[:, 0:n], AF.Ln, bias=t["epsc"], scale=1.0 / D)
            self.act(r[:, 0:n], r[:, 0:n], AF.Exp, scale=-0.5)
            for k in range(8):
                xin = V(X[:, k, t0:t0 + n], "X%d_%d" % (k, ti))
                self.stt(dst_fn(k, t0, n), xin, cp[:, wcol0 + k:wcol0 + k + 1], r[:, 0:n], ALU.mult, ALU.mult)

    def load_hist(self, l):
        t = self.t
        specs = [
            ("sdc", 48, 1536, t["HISTD"], 12),
            ("slc", 48, 512, t["HISTL"], 4),
            ("slh", 16, 512, t["H0"], 4),
            ("sfc", 32, 3072, t["HISTF"], 24),
        ]
        for name, rows, cols, dst, nch in specs:
            c0 = 0
            while c0 < cols:
                cc = min(1024, cols - c0)
                st = self.piece(self.pc_raw(self.d[name][l, :, c0:c0 + cc], rows, cc))
                nb = cc // 128
                for b0 in range(0, nb, 4):
                    ps = self.psbig()
                    qn = min(4, nb - b0)
                    for q in range(qn):
                        self.tr(ps[:, q * rows:(q + 1) * rows], st[0:rows, (b0 + q) * 128:(b0 + q + 1) * 128], rows)
                    ch0 = c0 // 128 + b0
                    dstv = V(dst[:, ch0:ch0 + qn, :], name + "h")
                    self.cp("dve", dstv, V(ps.ap[:, 0:qn * rows].rearrange("p (q r) -> p q r", r=rows), ps.key))
                c0 += cc

    def out_states(self, l, src, nch, ncolp, ncols, out_p, out_s, name):
        nr = ncolp + ncols
        ost = self.scr(nch * 128, parts=nr)
        for b0 in range(0, nch, 4):
            ps = self.psbig()
            qn = min(4, nch - b0)
            for q in range(qn):
                self.tr(ps[0:nr, q * 128:(q + 1) * 128], V(src[:, b0 + q, :], name), 128)
            self.cp("act", ost[:, b0 * 128:(b0 + qn) * 128], ps[0:nr, 0:qn * 128])
        self.dma(OQ, out_p, ost.ap[0:ncolp, :], reads=[ost.key])
        self.dma(OQ, out_s, ost.ap[ncolp:nr, :], reads=[ost.key])

    def layer(self, l):
        t = self.t
        d = self.d
        cp = t["cp"]
        cm = t["cm"]
        X = t["X"]
        H = t["H"]
        OAB = t["OAB"]
        MG = t["MG"]
        cb = l * CPL
        w_in = d["w_in"][l]

        self.new_phase()
        self.memset("pool", V(t["CSTD"][:, :, :], "cstd"), 0.0)
        self.memset("pool", V(t["CSTL"][:, :, :], "cstl"), 0.0)
        self.memset("pool", V(t["CSTF"][:, :, :], "cstf"), 0.0)
        self.memset("pool", V(t["HL"][:, :, :], "hl"), 0.0)
        self.memset("pool", V(t["SP"][:, :, :], "sprompt"), 0.0)
        self.load_hist(l)
        nea = V(t["LC"][:, 0:4], "lc_nea")
        self.act(nea, cp[:, CP_AB + l * 8:CP_AB + l * 8 + 4], AF.Exp)
        self.ts("dve", nea, nea, -1.0, ALU.mult)
        lc = V(t["LC"][:, 4:8], "lc_c")
        lc2 = V(t["LC"][:, 8:12], "lc_c2")
        self.act(lc, cp[:, cb + 92:cb + 96], AF.Exp, scale=-1.0)
        self.act(lc, lc, AF.Ln, bias=1.0)
        self.ts("dve", lc2, lc, -16.0, ALU.mult)
        self.ts("dve", lc, lc, -8.0, ALU.mult)
        srcs = [(lambda stg: stg[:, 0:1024].rearrange("p (n d) -> p n d", d=128),
                 d["lruw"][l].rearrange("n c d -> c n d"))]
        wv = self.piece(("cast", srcs, 1024))
        LW = V(t["LW"][:, :], "lw")
        self.cp("dve", LW, wv)

        if KSTOP <= 1:
            return
        for p in range(2):
            self.mixer_pass(l, p)
            if KSTOP <= 6:
                return

        self.ffn(l)

    def mixer_pass(self, l, p):
        t = self.t
        d = self.d
        cp = t["cp"]
        cm = t["cm"]
        X = t["X"]
        H = t["H"]
        OAB = t["OAB"]
        MG = t["MG"]
        cb = l * CPL
        w_in = d["w_in"][l]
        tiles = PASS_TILES[p]
        base = PASS_BASE[p]
        H = t["Hm"]
        self.use8 = False
        self.new_phase(("SCR", "SCR2", "MG32"))

        def hv(k, t0, n):
            return V(H[:, k, t0 - base:t0 - base + n], "H%d_%d" % (k, t0))

        self.rmsnorm_fm(tiles, cb + 0, hv)

        chunks = []
        for (t0, n) in tiles:
            if n == 512:
                for c in range(4):
                    chunks.append((t0 + c * 128, 128, t0))
            else:
                chunks.append((t0, 64, t0))
        COLS = t["COLS"]
        wab = self.piece(self.pc_cols(w_in, 2048, 8, 8))
        wabv = V(wab.ap[:, 0:64].rearrange("p (k c) -> p k c", c=8), wab.key)
        tmp4 = [self.scr(4) for _ in range(4)]
        for ci, (c0, C, t0) in enumerate(chunks):
            smp = (C == 64)
            ps = self.pssm()
            for k in range(8):
                self.mm(ps[0:C, 0:8], V(H[:, k, c0 - base:c0 - base + C], "H%d_%d" % (k, t0)), wabv[:, k, :],
                        start=(k == 0), stop=(k == 7))
            col = lambda q: V(COLS[0:C, ci, q * 4:(q + 1) * 4], "cols%d" % ci)
            g = tmp4[ci % 2]
            self.tt("dve", g[0:C, :], ps[0:C, 0:4], cp[0:C, CP_AB + l * 8 + 4:CP_AB + l * 8 + 8], ALU.add)
            self.act(g[0:C, :], g[0:C, :], AF.Exp)
            self.act(g[0:C, :], g[0:C, :], AF.Ln, bias=1.0)
            self.tt("dve", g[0:C, :], g[0:C, :], V(t["LC"][0:C, 0:4], "lc_nea"), ALU.mult)
            self.act(col(1), ps[0:C, 4:8], AF.Sigmoid)
            ps2 = self.pssm()
            uc = cm[0:C, CM_UCS:CM_UCS + C] if smp else cm[0:C, CM_UC:CM_UC + C]
            ob = cm[0:C, CM_OBS:CM_OBS + C] if smp else t["ones32"][0:C, 0:C]
            self.mm(ps2[0:C, 0:4], uc, g[0:C, :])
            self.mm(ps2[0:C, 4:8], ob, g[0:C, :])
            self.cp("dve", col(0), ps2[0:C, 0:4])
            self.ts("dve", col(2), col(1), -1.0, ALU.mult)
            self.act(col(3), ps2[0:C, 0:4], AF.Exp)
            self.tt("dve", col(4), col(1), col(3), ALU.mult)
            d4 = tmp4[2 + ci % 2]
            self.tt("dve", d4[0:C, :], ps2[0:C, 4:8], col(0), ALU.subtract)
            self.act(col(5), d4[0:C, :], AF.Exp)
            self.act(col(6), ps2[0:C, 4:8], AF.Exp)

        if KSTOP <= 2:
            return
        for hd in range(4):
            self.dn_head(l, p, hd, tiles, chunks)
            if KSTOP <= 3:
                return
        if KSTOP <= 4:
            return

        self.use8 = True
        for j in range(4):
            self.lru_chunk(l, p, j, tiles)

        if KSTOP <= 5:
            return
        self.P.fence()
        mt = [self.scr(512) for _ in range(4)]
        mi = 0
        for m in range(8):
            wga = self.piece(self.pc_cols(w_in, 3080 + m * 128))
            wgb = self.piece(self.pc_cols(w_in, 4104 + m * 128))
            wba = self.piece(self.pc_cols(d["wbr"][l, 0], m * 128, 4))
            wbb = self.piece(self.pc_cols(d["wbr"][l, 1], m * 128, 4))
            v3 = lambda w, nk: V(w.ap[:, 0:nk * 128].rearrange("p (k c) -> p k c", c=128), w.key)
            wga3, wgb3, wba3, wbb3 = v3(wga, 8), v3(wgb, 8), v3(wba, 4), v3(wbb, 4)
            for (t0, n) in tiles:
                lo = t0 - base
                pga, pgb, pba, pbb = self.psbig(), self.psbig(), self.psbig(), self.psbig()
                for k in range(8):
                    self.mm(pga[:, 0:n], wga3[:, k, :], hv(k, t0, n), start=(k == 0), stop=(k == 7))
                for k in range(8):
                    self.mm(pgb[:, 0:n], wgb3[:, k, :], hv(k, t0, n), start=(k == 0), stop=(k == 7))
                for k in range(4):
                    self.mm(pba[:, 0:n], wba3[:, k, :], V(OAB[:, k, lo:lo + n], "OAB%d_%d" % (k, t0)),
                            start=(k == 0), stop=(k == 3))
                for k in range(4):
                    self.mm(pbb[:, 0:n], wbb3[:, k, :], V(OAB[:, 4 + k, lo:lo + n], "OAB%d_%d" % (4 + k, t0)),
                            start=(k == 0), stop=(k == 3))
                sa = mt[mi % 4]
                sb = mt[(mi + 1) % 4]
                mi += 2
                self.act(sa[:, 0:n], pga[:, 0:n], AF.Sigmoid)
                self.act(sb[:, 0:n], pgb[:, 0:n], AF.Sigmoid)
                self.tt("dve", sa[:, 0:n], sa[:, 0:n], pba[:, 0:n], ALU.mult)
                self.tt("dve", sb[:, 0:n], sb[:, 0:n], pbb[:, 0:n], ALU.mult)
                self.tt("dve", V(MG[:, m, lo:lo + n], "MG%d_%d" % (m, t0)), sa[:, 0:n], sb[:, 0:n], ALU.add)

        for e in range(8):
            wo = self.piece(self.pc_cols(d["w_o"][l], e * 128))
            wo3 = V(wo.ap.rearrange("p (k c) -> p k c", c=128), wo.key)
            for (t0, n) in tiles:
                lo = t0 - base
                ti = ALL_TILES.index((t0, n))
                ps = self.psbig()
                for k in range(8):
                    self.mm(ps[:, 0:n], wo3[:, k, :], V(MG[:, k, lo:lo + n], "MG%d_%d" % (k, t0)),
                            start=(k == 0), stop=(k == 7))
                xv = V(X[:, e, t0:t0 + n], "X%d_%d" % (e, ti))
                self.tt("dve", xv, xv, ps[:, 0:n], ALU.add)

        if p == 1:
            self.new_phase()
            o = self.d
            self.out_states(l, t["CSTD"], 12, 3, 48, o["o_pdc"][l], o["o_sdc"][l], "cstd")
            self.out_states(l, t["CSTL"], 4, 3, 48, o["o_plc"][l], o["o_slc"][l], "cstl")
            self.out_states(l, t["HL"], 4, 1, 16, o["o_pl"][l], o["o_sl"][l], "hl")
            for hd in range(4):
                self.dma(OQ, o["o_pd"][l, hd], t["SP"][:, hd, :], reads=["sprompt"])

    def dn_head(self, l, p, hd, tiles, chunks):
        t = self.t
        d = self.d
        cp = t["cp"]
        cm = t["cm"]
        H = t["Hm"]
        OAB = t["OAB"]
        COLS = t["COLS"]
        cb = l * CPL
        base = PASS_BASE[p]
        w_in = d["w_in"][l]
        CSTD = t["CSTD"]
        so = self.scr_off
        self.scr_off = so
        wq = self.piece(self.pc_cols(w_in, hd * 128))
        wk = self.piece(self.pc_cols(w_in, 512 + hd * 128))
        wv = self.piece(self.pc_cols(w_in, 1024 + hd * 128))
        wz = self.piece(self.pc_cols(w_in, 1536 + hd * 128))
        v3 = lambda w: V(w.ap.rearrange("p (k c) -> p k c", c=128), w.key)
        w3 = [v3(wq), v3(wk), v3(wv), v3(wz)]
        chs = [hd, 4 + hd, 8 + hd]
        cvs = [[self.scr(512) for _ in range(3)] for _ in range(2)]
        zss = [self.scr(512) for _ in range(2)]
        xp = [self.scr(520) for _ in range(3)]
        sq = [self.scr(512) for _ in range(2)]
        mark = self.scr_off
        S = V(t["SP"][:, hd, :], "sprompt")

        def proj(t0_, n_):
            lo_ = t0_ - base
            pss_ = [self.psbig() for _ in range(4)]

            def mk(i):
                def f():
                    for k in range(8):
                        self.mm(pss_[i][:, 0:n_], w3[i][:, k, :], V(H[:, k, lo_:lo_ + n_], "H%d_%d" % (k, t0_)),
                                start=(k == 0), stop=(k == 7))
                return f
            return pss_, [mk(i) for i in range(4)]

        def pre_steps(t0, n, pss, cv, zs):
            smp = (n == 64)
            steps = []
            steps.append(lambda: self.act(zs[:, 0:n], pss[3][:, 0:n], AF.Silu))
            views = {}

            def evac(i):
                ch = chs[i]
                if not smp:
                    carry = V(CSTD[:, ch, 0:3], "cstd")
                    self.cp("act", xp[i][:, 0:3], carry)
                    self.cp("act", xp[i][:, 3:3 + n], pss[i][:, 0:n])
                    self.cp("act", carry, xp[i][:, n:n + 3])
                else:
                    x3 = V(xp[i].ap[:, 0:112].rearrange("p (s c) -> p s c", c=7), xp[i].key)
                    self.cp("act", x3[:, :, 0:3], V(t["HISTD"][:, ch, :].rearrange("p (s r) -> p s r", r=3), "sdch"))
                    self.cp("act", x3[:, :, 3:7], V(pss[i].ap[:, 0:64].rearrange("p (s c) -> p s c", c=4), pss[i].key))
                    self.cp("act", V(CSTD[:, ch, 3:51].rearrange("p (s r) -> p s r", r=3), "cstd"), x3[:, :, 4:7])

            def conv(i):
                ch = chs[i]
                cwc = cb + 16
                if not smp:
                    src = lambda j: xp[i][:, j:j + n]
                    dst = cv[i][:, 0:n]
                else:
                    x3 = V(xp[i].ap[:, 0:112].rearrange("p (s c) -> p s c", c=7), xp[i].key)
                    src = lambda j: x3[:, :, j:j + 4]
                    dst = V(cv[i].ap[:, 0:64].rearrange("p (s c) -> p s c", c=4), cv[i].key)
                wc = lambda j: cp[:, cwc + j * 12 + ch:cwc + j * 12 + ch + 1]
                self.ts("dve", dst, src(0), wc(0), ALU.mult)
                for j in range(1, 4):
                    self.stt(dst, src(j), wc(j), dst, ALU.mult, ALU.add)
                self.act(cv[i][:, 0:n], cv[i][:, 0:n], AF.Silu)

            def l2a(i):
                sqb = V(sq[i].ap.bitcast(BF16)[:, 0:512], sq[i].key)
                self.act(sqb[:, 0:n], cv[i][:, 0:n], AF.Square)
                ps = self.psbig()
                self.mm(ps[:, 0:n], t["onesb"], sqb[:, 0:n])
                self.act(sq[i][:, 0:n], ps[:, 0:n], AF.Ln, bias=t["epsc"], scale=1.0)

            def l2b(i):
                self.act(sq[i][:, 0:n], sq[i][:, 0:n], AF.Exp, scale=-0.5)
                if i == 0:
                    self.stt(cv[i][:, 0:n], cv[i][:, 0:n], 128.0 ** -0.5, sq[i][:, 0:n], ALU.mult, ALU.mult)
                else:
                    self.tt("dve", cv[i][:, 0:n], cv[i][:, 0:n], sq[i][:, 0:n], ALU.mult)

            for i in range(3):
                steps.append(lambda i=i: evac(i))
            for i in range(3):
                steps.append(lambda i=i: conv(i))
            for i in range(2):
                steps.append(lambda i=i: l2a(i))
            for i in range(2):
                steps.append(lambda i=i: l2b(i))
            return steps

        pss, cl = proj(*tiles[0])
        for f_ in cl:
            f_()
        for st_ in pre_steps(tiles[0][0], tiles[0][1], pss, cvs[0], zss[0]):
            st_()
        ci = 0
        for tidx, (t0, n) in enumerate(tiles):
            lo = t0 - base
            smp = (n == 64)
            cv, zs = cvs[tidx % 2], zss[tidx % 2]
            side = []
            if tidx + 1 < len(tiles):
                nt0, nn = tiles[tidx + 1]
                npss, ncl = proj(nt0, nn)
                side = ncl + pre_steps(nt0, nn, npss, cvs[(tidx + 1) % 2], zss[(tidx + 1) % 2])
            self.scr_off = mark
            if smp:
                for st_ in side:
                    st_()
                self.delta_chunk(l, p, hd, ci, 64, True, cv[0][:, 0:64], cv[1][:, 0:64], cv[2][:, 0:64], zs[:, 0:64], S,
                                 V(OAB[:, hd, lo:lo + 64], "OAB%d_%d" % (hd, t0)))
                ci += 1
            else:
                ctxs = []
                for c in range(4):
                    a = c * 128
                    ctxs.append(dict(ci=ci, qn=cv[0][:, a:a + 128], kn=cv[1][:, a:a + 128], vs=cv[2][:, a:a + 128],
                                     zs=zs[:, a:a + 128],
                                     dst=V(OAB[:, hd, lo + a:lo + a + 128], "OAB%d_%d" % (hd, t0))))
                    ci += 1
                self.delta_group(l, hd, ctxs, S, side)
        self.scr_off = so

    def delta_chunk(self, l, p, hd, ci, C, smp, qn, kn, vs, zs, S, oab_dst):
        t = self.t
        d = self.d
        cp = t["cp"]
        cm = t["cm"]
        COLS = t["COLS"]
        cb = l * CPL
        so = self.scr_off
        col = lambda q: V(COLS[0:C, ci, q * 4 + hd:q * 4 + hd + 1], "cols%d" % ci)
        gc, beta, nbeta, egc, bge, edl, egl = [col(q) for q in range(7)]
        if smp:
            mU = cm[0:C, CM_MUS:CM_MUS + C]
            pL = cm[0:C, CM_PLS:CM_PLS + C]
        else:
            mU = cm[0:C, CM_MU:CM_MU + C]
            pL = cm[0:C, CM_PL:CM_PL + C]
        idn = cm[0:C, CM_ID:CM_ID + C]
        sc = lambda n=128, parts=128: self.scr(n, parts=parts)
        pk = self.pssm()
        pv = self.pssm()
        self.tr(pk[0:C, :], kn, 128)
        self.tr(pv[0:C, :], vs, 128)
        kbg, kd, vb = sc(), sc(), sc()
        self.act(kbg[0:C, :], pk[0:C, :], AF.Copy, scale=bge)
        self.act(kd[0:C, :], pk[0:C, :], AF.Copy, scale=edl)
        self.ts("dve", vb[0:C, :], pv[0:C, :], beta, ALU.mult)
        gcb = sc()
        self.ts("dve", gcb[0:C, :], t["ones32"][0:C, :], gc, ALU.mult)
        pg = self.pssm()
        self.tr(pg[:, 0:C], gcb[0:C, :], C)
        eT, eL = sc(), sc()
        self.stt(eT[0:C, 0:C], pg[0:C, 0:C], gc, mU, ALU.subtract, ALU.add)
        self.act(eT[0:C, 0:C], eT[0:C, 0:C], AF.Exp)
        self.stt(eL[0:C, 0:C], pg[0:C, 0:C], gc, pL, ALU.subtract, ALU.add)
        self.act(eL[0:C, 0:C], eL[0:C, 0:C], AF.Exp, scale=-1.0)
        pkk = self.pssm()
        self.mm(pkk[0:C, 0:C], kn, kn)
        N = sc()
        self.stt(N[0:C, 0:C], pkk[0:C, 0:C], nbeta, eL[0:C, 0:C], ALU.mult, ALU.mult)
        pqk = self.pssm()
        self.mm(pqk[0:C, 0:C], kn, qn)
        attnT = sc()
        self.tt("dve", attnT[0:C, 0:C], pqk[0:C, 0:C], eT[0:C, 0:C], ALU.mult)
        pn = self.pssm()
        self.tr(pn[0:C, 0:C], N[0:C, 0:C], C)
        Nt = sc()
        self.cp("act", Nt[0:C, 0:C], pn[0:C, 0:C])
        Pt = sc()
        self.tt("dve", Pt[0:C, 0:C], pn[0:C, 0:C], idn, ALU.add)
        L = 2 if smp else 7
        Ncur, Ntcur, Ptcur = N, Nt, Pt
        for k in range(1, L):
            p1 = self.pssm()
            self.mm(p1[0:C, 0:C], Ntcur[0:C, 0:C], Ncur[0:C, 0:C])
            Nn = sc()
            self.cp("act", Nn[0:C, 0:C], p1[0:C, 0:C])
            Ntn = None
            if k < L - 1:
                p2 = self.pssm()
                self.mm(p2[0:C, 0:C], Ncur[0:C, 0:C], Ntcur[0:C, 0:C])
                Ntn = sc()
                self.cp("dve", Ntn[0:C, 0:C], p2[0:C, 0:C])
            p3 = self.pssm()
            self.mm(p3[0:C, 0:C], Nn[0:C, 0:C], Ptcur[0:C, 0:C])
            Ptn = sc()
            self.tt("dve", Ptn[0:C, 0:C], p3[0:C, 0:C], Ptcur[0:C, 0:C], ALU.add)
            Ncur, Ntcur, Ptcur = Nn, Ntn, Ptn
        Pt = Ptcur
        pw = self.pssm()
        self.mm(pw[:, 0:C], kbg[0:C, :], Pt[0:C, 0:C])
        nwT = sc()
        self.act(nwT[:, 0:C], pw[:, 0:C], AF.Copy, scale=-1.0)
        o_sb = sc()
        if not smp:
            pvn = self.pssm()
            self.mm(pvn[0:C, :], Pt[0:C, 0:C], vb[0:C, :], start=True, stop=False)
            self.mm(pvn[0:C, :], nwT[:, 0:C], S, start=False, stop=True)
            vn = sc()
            self.cp("act", vn[0:C, :], pvn[0:C, :])
            pqs = self.pssm()
            self.mm(pqs[0:C, :], qn, S)
            tq = sc()
            self.act(tq[0:C, :], pqs[0:C, :], AF.Copy, scale=egc)
            pav = self.pssm()
            self.mm(pav[0:C, :], attnT[0:C, 0:C], vn[0:C, :])
            self.tt("dve", o_sb[0:C, :], tq[0:C, :], pav[0:C, :], ALU.add)
            pds = self.pssm()
            self.mm(pds[:, :], kd[0:C, :], vn[0:C, :])
            self.stt(S, S, egl, pds[:, :], ALU.mult, ALU.add)
        else:
            SS = [self.scr(128) for _ in range(NS)]
            for s in range(NS):
                self.dma("sp", SS[s].ap, d["sd"][l, s, hd], writes=[SS[s].key])
            nwm = self.scr(1088)
            qm = self.scr(1088)
            self.memset("pool", nwm, 0.0)
            self.memset("pool", qm, 0.0)
            dv = lambda b: V(b.ap.rearrange("p (s c) -> p s c", c=68)[:, :, 0:4], b.key)
            sv = lambda b: V(b.ap[:, 0:64].rearrange("p (s c) -> p s c", c=4), b.key)
            self.cp("pool", dv(nwm), sv(nwT))
            self.cp("pool", dv(qm), sv(qn))
            pvn = self.pssm()
            self.mm(pvn[0:C, :], Pt[0:C, 0:C], vb[0:C, :], start=True, stop=False)
            for s in range(NS):
                self.mm(pvn[0:C, :], nwm[:, s * 68 - s * 4:s * 68 - s * 4 + 64], SS[s], start=False, stop=(s == NS - 1))
            vn = sc()
            self.cp("act", vn[0:C, :], pvn[0:C, :])
            pqs = self.pssm()
            for s in range(NS):
                self.mm(pqs[0:C, :], qm[:, s * 64:s * 64 + 64], SS[s], start=(s == 0), stop=(s == NS - 1))
            tq = sc()
            self.act(tq[0:C, :], pqs[0:C, :], AF.Copy, scale=egc)
            pav = self.pssm()
            self.mm(pav[0:C, :], attnT[0:C, 0:C], vn[0:C, :])
            self.tt("dve", o_sb[0:C, :], tq[0:C, :], pav[0:C, :], ALU.add)
            eglb = sc()
            self.ts("dve", eglb[0:C, :], t["ones32"][0:C, :], egl, ALU.mult)
            pe_ = self.pssm()
            self.mm(pe_[:, 0:NS], eglb[0:C, :], cm[0:C, CM_SEL:CM_SEL + NS])
            egs = sc(16)
            self.cp("act", egs[:, 0:NS], pe_[:, 0:NS])
            kdm = [self.scr(128) for _ in range(2)]
            for s in range(NS):
                km = kdm[s % 2]
                self.act(km[0:C, :], kd[0:C, :], AF.Copy, scale=cm[0:C, CM_RM + s:CM_RM + s + 1])
                pds = self.pssm()
                self.mm(pds[:, :], km[0:C, :], vn[0:C, :])
                self.stt(SS[s], SS[s], egs[:, s:s + 1], pds[:, :], ALU.mult, ALU.add)
                self.dma(OQ, d["o_sd"][l, s, hd], SS[s].ap, reads=[SS[s].key])
        junk = sc()
        ss = sc(8)
        self.act(junk[0:C, :], o_sb[0:C, :], AF.Square, accum=ss[0:C, 0:1])
        self.act(ss[0:C, 0:1], ss[0:C, 0:1], AF.Ln, bias=t["epsc"][0:C, :], scale=1.0 / 128)
        self.act(ss[0:C, 0:1], ss[0:C, 0:1], AF.Exp, scale=-0.5)
        self.ts("dve", o_sb[0:C, :], o_sb[0:C, :], ss[0:C, 0:1], ALU.mult)
        po = self.pssm()
        self.tr(po[:, 0:C], o_sb[0:C, :], C)
        self.stt(oab_dst, po[:, 0:C], cp[:, cb + 168:cb + 169], zs, ALU.mult, ALU.mult)
        self.scr_off = so

    def delta_group(self, l, hd, ctxs, S, side=()):
        t = self.t
        cp = t["cp"]
        cm = t["cm"]
        COLS = t["COLS"]
        cb = l * CPL
        so = self.scr_off
        C = 128
        mU = cm[:, CM_MU:CM_MU + C]
        pL = cm[:, CM_PL:CM_PL + C]
        idn = cm[:, CM_ID:CM_ID + C]
        for cx in ctxs:
            ci = cx["ci"]
            col = lambda q, ci=ci: V(COLS[:, ci, q * 4 + hd:q * 4 + hd + 1], "cols%d" % ci)
            cx["c"] = [col(q) for q in range(7)]
            for nm in ("kbg", "kd", "vb", "gcb", "eT", "eL", "Nt", "Pt", "N2", "Nt2", "Pt2"):
                cx[nm] = self.scr(128)
            cx["nwT"] = cx["Nt2"]
            cx["vn"] = cx["N2"]
            cx["osb"] = cx["Pt2"]
            cx["junk"] = cx["Nt"]
        ssall = self.scr(8 * len(ctxs))
        for ii, cx in enumerate(ctxs):
            cx["ss"] = ssall[:, ii * 8:(ii + 1) * 8]
        side = list(side)

        def sidestep(k=1):
            for _ in range(k):
                if side:
                    side.pop(0)()
        sidestep(4)
        for cx in ctxs:
            cx["pk"] = self.pssm(); self.tr(cx["pk"], cx["kn"], 128)
        for cx in ctxs:
            cx["pv"] = self.pssm(); self.tr(cx["pv"], cx["vs"], 128)
        for cx in ctxs:
            gc, beta, nbeta, egc, bge, edl, egl = cx["c"]
            self.act(cx["kbg"], cx["pk"], AF.Copy, scale=bge)
            self.act(cx["kd"], cx["pk"], AF.Copy, scale=edl)
            self.ts("dve", cx["vb"], cx["pv"], beta, ALU.mult)
            self.ts("dve", cx["gcb"], t["ones32"], gc, ALU.mult)
        sidestep()
        for cx in ctxs:
            cx["pg"] = self.pssm(); self.tr(cx["pg"], cx["gcb"], 128)
        for cx in ctxs:
            gc = cx["c"][0]
            self.stt(cx["eT"], cx["pg"], gc, mU, ALU.subtract, ALU.add)
            self.stt(cx["eL"], cx["pg"], gc, pL, ALU.subtract, ALU.add)
        for cx in ctxs:
            self.act(cx["eT"], cx["eT"], AF.Exp)
            self.act(cx["eL"], cx["eL"], AF.Exp, scale=-1.0)
        sidestep()
        for cx in ctxs:
            cx["pkk"] = self.pssm(); self.mm(cx["pkk"], cx["kn"], cx["kn"])
        for cx in ctxs:
            cx["pqk"] = self.pssm(); self.mm(cx["pqk"], cx["kn"], cx["qn"])
        for cx in ctxs:
            self.stt(cx["eL"], cx["pkk"], cx["c"][2], cx["eL"], ALU.mult, ALU.mult)
            cx["N"] = cx["eL"]
        for cx in ctxs:
            self.tt("dve", cx["eT"], cx["pqk"], cx["eT"], ALU.mult)
            cx["attnT"] = cx["eT"]
        sidestep()
        for cx in ctxs:
            cx["pn"] = self.pssm(); self.tr(cx["pn"], cx["N"], 128)
        for cx in ctxs:
            self.cp("act", cx["Nt"], cx["pn"])
            self.tt("dve", cx["Pt"], cx["pn"], idn, ALU.add)
        L = 7
        for cx in ctxs:
            cx["cur"] = (cx["N"], cx["Nt"], cx["Pt"])
            cx["alt"] = (cx["N2"], cx["Nt2"], cx["Pt2"])
        for k in range(1, L):
            sidestep()
            for cx in ctxs:
                Nc, Ntc, Ptc = cx["cur"]
                cx["p1"] = self.pssm(); self.mm(cx["p1"], Ntc, Nc)
            if k < L - 1:
                for cx in ctxs:
                    Nc, Ntc, Ptc = cx["cur"]
                    cx["p2"] = self.pssm(); self.mm(cx["p2"], Nc, Ntc)
            for cx in ctxs:
                self.cp("act", cx["alt"][0], cx["p1"])
            if k < L - 1:
                for cx in ctxs:
                    self.cp("dve", cx["alt"][1], cx["p2"])
            for cx in ctxs:
                Nc, Ntc, Ptc = cx["cur"]
                cx["p3"] = self.pssm(); self.mm(cx["p3"], cx["alt"][0], Ptc)
            for cx in ctxs:
                Nc, Ntc, Ptc = cx["cur"]
                self.tt("dve", cx["alt"][2], cx["p3"], Ptc, ALU.add)
                cx["cur"], cx["alt"] = cx["alt"], cx["cur"]
        for cx in ctxs:
            cx["pw"] = self.pssm(); self.mm(cx["pw"], cx["kbg"], cx["cur"][2])
        for cx in ctxs:
            self.act(cx["nwT"], cx["pw"], AF.Copy, scale=-1.0)
        while side:
            side.pop(0)()
        for cx in ctxs:
            gc, beta, nbeta, egc, bge, edl, egl = cx["c"]
            Pt = cx["cur"][2]
            pvn = self.pssm()
            self.mm(pvn, Pt, cx["vb"], start=True, stop=False)
            self.mm(pvn, cx["nwT"], S, start=False, stop=True)
            pqs = self.pssm()
            self.mm(pqs, cx["qn"], S)
            self.cp("act", cx["vn"], pvn)
            tq = cx["gcb"]
            self.act(tq, pqs, AF.Copy, scale=egc)
            pav = self.pssm()
            self.mm(pav, cx["attnT"], cx["vn"])
            pds = self.pssm()
            self.mm(pds, cx["kd"], cx["vn"])
            self.stt(S, S, egl, pds, ALU.mult, ALU.add)
            self.tt("dve", cx["osb"], tq, pav, ALU.add)
        while side:
            side.pop(0)()
        for cx in ctxs:
            self.act(cx["junk"], cx["osb"], AF.Square, accum=cx["ss"][:, 0:1])
        for cx in ctxs:
            self.act(cx["ss"][:, 0:1], cx["ss"][:, 0:1], AF.Ln, bias=t["epsc"], scale=1.0 / 128)
        for cx in ctxs:
            self.act(cx["ss"][:, 0:1], cx["ss"][:, 0:1], AF.Exp, scale=-0.5)
        for cx in ctxs:
            self.ts("dve", cx["osb"], cx["osb"], cx["ss"][:, 0:1], ALU.mult)
        for cx in ctxs:
            cx["po"] = self.pssm(); self.tr(cx["po"], cx["osb"], 128)
        for cx in ctxs:
            self.stt(cx["dst"], cx["po"], cp[:, cb + 168:cb + 169], cx["zs"], ALU.mult, ALU.mult)
        self.scr_off = so

    def lru_chunk(self, l, p, j, tiles):
        t = self.t
        d = self.d
        cp = t["cp"]
        H = t["Hm"]
        OAB = t["OAB"]
        cb = l * CPL
        base = PASS_BASE[p]
        w_in = d["w_in"][l]
        CSTL = t["CSTL"]
        HL = t["HL"]
        so = self.scr_off
        wx = self.piece(self.pc_cols(w_in, 2056 + j * 128))
        wy = self.piece(self.pc_cols(w_in, 2568 + j * 128))
        v3 = lambda w: V(w.ap.rearrange("p (k c) -> p k c", c=128), w.key)
        wx3, wy3 = v3(wx), v3(wy)
        LW3 = V(t["LW"][:, :].rearrange("p (n d) -> p n d", d=128), "lw")
        xp = self.scr(520)
        xc = self.scr(512)
        xcb = self.scr(512, BF16)
        r_, i_, a_, m_, gy = [self.scr(512) for _ in range(5)]
        lc = V(t["LC"][:, 4 + j:5 + j], "lc_c")
        lc2 = V(t["LC"][:, 8 + j:9 + j], "lc_c2")
        for (t0, n) in tiles:
            lo = t0 - base
            smp = (n == 64)
            px, py = self.psbig(), self.psbig()
            for k in range(8):
                self.mm(px[:, 0:n], wx3[:, k, :], V(H[:, k, lo:lo + n], "H%d_%d" % (k, t0)), start=(k == 0), stop=(k == 7))
            for k in range(8):
                self.mm(py[:, 0:n], wy3[:, k, :], V(H[:, k, lo:lo + n], "H%d_%d" % (k, t0)), start=(k == 0), stop=(k == 7))
            self.act(gy[:, 0:n], py[:, 0:n], AF.Gelu_apprx_tanh)
            if not smp:
                carry = V(CSTL[:, j, 0:3], "cstl")
                self.cp("act", xp[:, 0:3], carry)
                self.cp("act", xp[:, 3:3 + n], px[:, 0:n])
                self.cp("act", carry, xp[:, n:n + 3])
                src = lambda jj: xp[:, jj:jj + n]
                dst = xc[:, 0:n]
            else:
                x3 = V(xp.ap[:, 0:112].rearrange("p (s c) -> p s c", c=7), xp.key)
                self.cp("act", x3[:, :, 0:3], V(t["HISTL"][:, j, :].rearrange("p (s r) -> p s r", r=3), "slch"))
                self.cp("act", x3[:, :, 3:7], V(px.ap[:, 0:64].rearrange("p (s c) -> p s c", c=4), px.key))
                self.cp("act", V(CSTL[:, j, 3:51].rearrange("p (s r) -> p s r", r=3), "cstl"), x3[:, :, 4:7])
                src = lambda jj: x3[:, :, jj:jj + 4]
                dst = V(xc.ap[:, 0:64].rearrange("p (s c) -> p s c", c=4), xc.key)
            wc = lambda jj: cp[:, cb + 64 + jj * 4 + j:cb + 64 + jj * 4 + j + 1]
            self.ts("dve", dst, src(0), wc(0), ALU.mult, cp[:, cb + 80 + j:cb + 81 + j], ALU.add)
            for jj in range(1, 4):
                self.stt(dst, src(jj), wc(jj), dst, ALU.mult, ALU.add)
            self.cp("dve", xcb[:, 0:n], xc[:, 0:n])
            pr, pi = self.psbig(), self.psbig()
            self.mm(pr[:, 0:n], LW3[:, j, :], xcb[:, 0:n])
            self.mm(pi[:, 0:n], LW3[:, 4 + j, :], xcb[:, 0:n])
            self.act(r_[:, 0:n], pr[:, 0:n], AF.Sigmoid, bias=cp[:, cb + 84 + j:cb + 85 + j])
            self.act(i_[:, 0:n], pi[:, 0:n], AF.Sigmoid, bias=cp[:, cb + 88 + j:cb + 89 + j])
            self.act(a_[:, 0:n], r_[:, 0:n], AF.Exp, scale=lc)
            self.act(m_[:, 0:n], r_[:, 0:n], AF.Exp, scale=lc2)
            self.act(m_[:, 0:n], m_[:, 0:n], AF.Ln, bias=1.0, scale=-1.0)
            self.act(m_[:, 0:n], m_[:, 0:n], AF.Exp, scale=0.5)
            if t0 == 0:
                self.memset("dve", m_[:, 0:1], 1.0)
            self.tt("dve", i_[:, 0:n], i_[:, 0:n], xc[:, 0:n], ALU.mult)
            self.tt("dve", i_[:, 0:n], i_[:, 0:n], m_[:, 0:n], ALU.mult)
            hl = V(HL[:, j, 0:1], "hl")
            if not smp:
                self.scan(r_[:, 0:n], a_[:, 0:n], i_[:, 0:n], hl)
                self.cp("dve", hl, r_[:, n - 1:n])
            else:
                for s in range(NS):
                    self.scan(r_[:, s * 4:s * 4 + 4], a_[:, s * 4:s * 4 + 4], i_[:, s * 4:s * 4 + 4],
                              V(t["H0"][:, j, s:s + 1], "slhh"))
                self.cp("dve", V(HL[:, j, 1:17], "hl"),
                        V(r_.ap[:, 0:64].rearrange("p (s c) -> p s c", c=4)[:, :, 3], r_.key))
            self.tt("dve", V(OAB[:, 4 + j, lo:lo + n], "OAB%d_%d" % (4 + j, t0)), r_[:, 0:n], gy[:, 0:n], ALU.mult)
        self.scr_off = so

    def ffn(self, l):
        t = self.t
        d = self.d
        cp = t["cp"]
        X = t["X"]
        H = t["H"]
        ACTB = t["ACTB"]
        CSTF = t["CSTF"]
        cb = l * CPL
        self.new_phase()
        self.use8 = True

        def hv(k, t0, n):
            return V(H[:, k, t0:t0 + n], "H%d_%d" % (k, t0))

        self.rmsnorm_fm(ALL_TILES, cb + 8, hv)
        xp = [self.scr(520) for _ in range(2)]
        gc_ = [self.scr(512) for _ in range(3)]
        ub_ = [self.scr(512, BF16) for _ in range(3)]
        it = 0
        pend = None
        for g in range(6):
            for jj in range(4):
                f = g * 4 + jj
                wg = self.piece(self.pc_cols(d["ffn_w_in"][l], f * 128))
                wu = self.piece(self.pc_cols(d["ffn_w_in"][l], DFF + f * 128))
                v3 = lambda w: V(w.ap.rearrange("p (k c) -> p k c", c=128), w.key)
                wg3, wu3 = v3(wg), v3(wu)
                for (t0, n) in ALL_TILES:
                    smp = (n == 64)
                    pg, pu = self.psbig(), self.psbig()
                    for k in range(8):
                        self.mm(pg[:, 0:n], wg3[:, k, :], hv(k, t0, n), start=(k == 0), stop=(k == 7))
                    for k in range(8):
                        self.mm(pu[:, 0:n], wu3[:, k, :], hv(k, t0, n), start=(k == 0), stop=(k == 7))
                    x_ = xp[it % 2]
                    c_ = gc_[it % 3]
                    ub = ub_[it % 3]
                    it += 1
                    self.cp("act", ub[:, 0:n], pu[:, 0:n])
                    if not smp:
                        carry = V(CSTF[:, f, 0:2], "cstf")
                        self.cp("act", x_[:, 0:2], carry)
                        self.cp("act", x_[:, 2:2 + n], pg[:, 0:n])
                        self.cp("act", carry, x_[:, n:n + 2])
                        src = lambda j: x_[:, j:j + n]
                        dst = c_[:, 0:n]
                    else:
                        x3 = V(x_.ap[:, 0:96].rearrange("p (s c) -> p s c", c=6), x_.key)
                        self.cp("act", x3[:, :, 0:2], V(t["HISTF"][:, f, :].rearrange("p (s r) -> p s r", r=2), "sfch"))
                        self.cp("act", x3[:, :, 2:6], V(pg.ap[:, 0:64].rearrange("p (s c) -> p s c", c=4), pg.key))
                        self.cp("act", V(CSTF[:, f, 2:34].rearrange("p (s r) -> p s r", r=2), "cstf"), x3[:, :, 4:6])
                        src = lambda j: x3[:, :, j:j + 4]
                        dst = V(c_.ap[:, 0:64].rearrange("p (s c) -> p s c", c=4), c_.key)
                    wc = lambda j: cp[:, cb + 96 + j * 24 + f:cb + 96 + j * 24 + f + 1]
                    self.ts("dve", dst, src(0), wc(0), ALU.mult)
                    for j in range(1, 3):
                        self.stt(dst, src(j), wc(j), dst, ALU.mult, ALU.add)
                    if pend is not None:
                        pend()

                    def _fin(c_=c_, ub=ub, n=n, jj=jj, t0=t0):
                        self.act(c_[:, 0:n], c_[:, 0:n], AF.Gelu_apprx_tanh)
                        self.tt("dve", V(ACTB[:, jj, t0:t0 + n], "ACTB%d_%d" % (jj, t0)), c_[:, 0:n], ub[:, 0:n], ALU.mult)
                    pend = _fin
            if pend is not None:
                pend()
                pend = None
            wd = [self.piece(self.pc_rows(d["ffn_w_down"][l], (g * 4 + jj) * 128)) for jj in range(4)]
            for e in range(8):
                for (t0, n) in ALL_TILES:
                    ti = ALL_TILES.index((t0, n))
                    ps = self.psbig()
                    for jj in range(4):
                        self.mm(ps[:, 0:n], wd[jj][:, e * 128:(e + 1) * 128],
                                V(ACTB[:, jj, t0:t0 + n], "ACTB%d_%d" % (jj, t0)), start=(jj == 0), stop=(jj == 3))
                    xv = V(X[:, e, t0:t0 + n], "X%d_%d" % (e, ti))
                    self.tt("dve", xv, xv, ps[:, 0:n], ALU.add)
        self.new_phase()
        self.out_states(l, CSTF, 24, 2, 32, d["o_pfc"][l], d["o_sfc"][l], "cstf")

    def final(self):
        t = self.t
        d = self.d
        X = t["X"]
        self.new_phase(("SCR", "RB32"))
        nsc = ([self.scr(512, BF16) for _ in range(2)], [self.scr(512) for _ in range(2)])
        yf = [self.scr(512) for _ in range(8)]
        yt = [self.scr(1024) for _ in range(2)]
        bi = 0
        for (t0, n) in ALL_TILES:
            def dst(k, t0_, n_):
                return yf[k][:, 0:n_]
            self.rmsnorm_fm([(t0, n)], CP_FIN, dst, scratch=nsc)
            blocks = [(t0 + b * 128, 128) for b in range(4)] if n == 512 else [(t0, 64)]
            for (b0, C) in blocks:
                y = yt[bi % 2]
                bi += 1
                for hb in range(2):
                    ps = self.psbig()
                    for q in range(4):
                        k = hb * 4 + q
                        self.tr(ps[0:C, q * 128:(q + 1) * 128], yf[k][:, b0 - t0:b0 - t0 + C], 128)
                    self.cp("act" if hb == 0 else "dve", y[0:C, hb * 512:(hb + 1) * 512], ps[0:C, 0:512])
                if C == 128:
                    self.dma(OQ, d["o_yp"][b0:b0 + 128, :], y.ap[0:128, :], reads=[y.key])
                else:
                    self.dma(OQ, d["o_ys"][:, :], y.ap[0:64, :], reads=[y.key])

    def build(self):
        t = self.t
        self.dma("sp", t["cp"].ap, self.d["cst"][:, 0:CP_N], writes=["cp"])
        self.dma("sp", t["cm"].ap, self.d["cst"][:, CP_N:CP_N + CM_N], writes=["cm"])
        self.memset("pool", t["onesb"], 1.0)
        self.memset("pool", t["ones32"], 1.0)
        self.memset("pool", t["epsc"], EPS)
        if KSTOP >= -1:
            self.load_x()
        if KSTOP >= 1:
            for l in range(self.depth):
                self.layer(l)
        if KSTOP >= 0:
            self.final()
        self.P.final_wait_all("sp")


def build_program(depth=DEPTH):
    nc = bass.Bass("TRN2", target_bir_lowering=False)
    dram = {}

    def din(name, shape):
        if os.environ.get("KNOIN") and name not in ("cst",):
            return
        dram[name] = nc.dram_tensor(name, list(shape), F32, kind="ExternalInput").ap()

    def dout(name, shape):
        if os.environ.get("KNOOUT") and name not in os.environ.get("KNOOUT").split(","):
            return
        dram[name] = nc.dram_tensor(name, list(shape), F32, kind="ExternalOutput").ap()

    din("xp", (TP, D)); din("xs", (64, D))
    din("sdc", (4, 48, 1536)); din("sd", (4, NS, 4, 128, 128)); din("slc", (4, 48, 512))
    din("slh", (4, NS, 512)); din("sfc", (4, 32, DFF))
    din("w_in", (4, D, INC)); din("lruw", (4, 8, 128, 128))
    din("wbr", (4, 2, 512, D)); din("w_o", (4, D, D))
    din("ffn_w_in", (4, D, 2 * DFF)); din("ffn_w_down", (4, DFF, D))
    din("cst", (128, CP_N + CM_N))
    dout("o_yp", (TP, D)); dout("o_ys", (64, D))
    dout("o_pd", (4, 4, 128, 128)); dout("o_sd", (4, NS, 4, 128, 128))
    dout("o_ps", (4, 12800)); dout("o_ss", (4, 204800))
    if "o_ps" not in dram or "o_ss" not in dram:
        dram["o_ps"] = nc.dram_tensor("o_ps_i", [4, 12800], F32, kind="Internal").ap() if "o_ps" not in dram else dram["o_ps"]
        dram["o_ss"] = nc.dram_tensor("o_ss_i", [4, 204800], F32, kind="Internal").ap() if "o_ss" not in dram else dram["o_ss"]
    ops, oss = dram["o_ps"], dram["o_ss"]
    dram["o_pdc"] = [ops[l, 0:4608].rearrange("(r c) -> r c", c=1536) for l in range(4)]
    dram["o_plc"] = [ops[l, 4608:6144].rearrange("(r c) -> r c", c=512) for l in range(4)]
    dram["o_pl"] = [ops[l, 6144:6656].rearrange("(r c) -> r c", c=512) for l in range(4)]
    dram["o_pfc"] = [ops[l, 6656:12800].rearrange("(r c) -> r c", c=DFF) for l in range(4)]
    dram["o_sdc"] = [oss[l, 0:73728].rearrange("(r c) -> r c", c=1536) for l in range(4)]
    dram["o_slc"] = [oss[l, 73728:98304].rearrange("(r c) -> r c", c=512) for l in range(4)]
    dram["o_sl"] = [oss[l, 98304:106496].rearrange("(r c) -> r c", c=512) for l in range(4)]
    dram["o_sfc"] = [oss[l, 106496:204800].rearrange("(r c) -> r c", c=DFF) for l in range(4)]

    SCRN = 6656
    with contextlib.ExitStack() as es:
        def SB(name, shape, dt=F32):
            return es.enter_context(nc.sbuf_tensor(name, list(shape), dt))

        def PS(name, shape):
            return es.enter_context(nc.psum_tensor(name, list(shape), F32))

        tens = {}
        tens["X"] = SB("X", (128, 8, T))
        Hfull = SB("H", (128, 8, T), BF16)
        tens["H"] = Hfull
        Hflat = Hfull[:, :, :].rearrange("p a b -> p (a b)")
        tens["Hm"] = Hflat[:, 0:8 * HT].rearrange("p (a b) -> p a b", b=HT)
        tens["SCR2"] = Hflat[:, 8 * HT:8 * T].bitcast(F32)
        RB = SB("RB", (128, 2, 8, HT), BF16)
        RBflat = RB[:, :, :, :].rearrange("p c a b -> p (c a b)")
        tens["RB32"] = RBflat.bitcast(F32)
        tens["MG32"] = RB[:, 1, :, :].rearrange("p a b -> p (a b)").bitcast(F32)
        tens["OAB"] = RB[:, 0, :, :]
        tens["MG"] = RB[:, 1, :, :]
        tens["ACTB"] = RBflat[:, 0:4 * T].rearrange("p (a b) -> p a b", b=T)
        tens["stg"] = [SB("stg%d" % i, (128, 1024)) for i in range(3)]
        tens["wbf"] = [SB("wbf%d" % i, (128, 1024), BF16) for i in range(6)]
        tens["SCR"] = SB("SCR", (128, SCRN))[:, :]
        tens["SCRN"] = SCRN
        tens["cp"] = V(SB("cp", (128, CP_N))[:, :], "cp")
        tens["cm"] = V(SB("cm", (128, CM_N))[:, :], "cm")
        tens["onesb"] = V(SB("onesb", (128, 128), BF16)[:, :], "onesb")
        tens["ones32"] = V(SB("ones32", (128, 128))[:, :], "ones32")
        tens["epsc"] = V(SB("epsc", (128, 1))[:, :], "epsc")
        tens["LC"] = SB("LC", (128, 12))
        tens["LW"] = SB("LW", (128, 1024), BF16)
        tens["COLS"] = SB("COLS", (128, 9, 28))
        tens["CSTD"] = SB("CSTD", (128, 12, 51))
        tens["CSTL"] = SB("CSTL", (128, 4, 51))
        tens["CSTF"] = SB("CSTF", (128, 24, 34))
        tens["HL"] = SB("HL", (128, 4, 17))
        tens["SP"] = SB("SP", (128, 4, 128))
        tens["HISTD"] = SB("HISTD", (128, 12, 48))
        tens["HISTL"] = SB("HISTL", (128, 4, 48))
        tens["HISTF"] = SB("HISTF", (128, 24, 32))
        tens["H0"] = SB("H0", (128, 4, 16))
        tens["psb"] = [PS("psb%d" % i, (128, 512)) for i in range(4)]
        tens["pss"] = [PS("pss%d" % i, (128, 512)) for i in range(4)]

        Pd = Prog(nc, dry=True)
        bd = Builder(nc, Pd, dram, tens, None, depth)
        bd.build()
        plan = bd.rec
        P = Prog(nc)
        b = Builder(nc, P, dram, tens, plan, depth)
        b.prep_plan()
        b.build()
        assert b.wpos == len(plan), (b.wpos, len(plan))
        P.run()
        nops = P.nops
    return nc, nops


def make_consts():
    cm = np.zeros((128, CM_N), np.float32)
    i = np.arange(128)
    cm[:, CM_ID:CM_ID + 128] = np.eye(128, dtype=np.float32)
    cm[:, CM_MU:CM_MU + 128] = np.where(i[None, :] >= i[:, None], 0.0, -1e30)
    cm[:, CM_PL:CM_PL + 128] = np.where(i[:, None] > i[None, :], 0.0, 1e30)
    cm[:, CM_UC:CM_UC + 128] = (i[:, None] <= i[None, :]).astype(np.float32)
    s = np.arange(64)
    same = (s[:, None] // 4) == (s[None, :] // 4)
    cm[:64, CM_MUS:CM_MUS + 64] = np.where(same & (s[None, :] >= s[:, None]), 0.0, -1e30)
    cm[:64, CM_PLS:CM_PLS + 64] = np.where(same & (s[:, None] > s[None, :]), 0.0, 1e30)
    cm[:64, CM_UCS:CM_UCS + 64] = (same & (s[:, None] <= s[None, :])).astype(np.float32)
    cm[:64, CM_OBS:CM_OBS + 64] = same.astype(np.float32)
    cm[:64, CM_RM:CM_RM + 16] = ((s[:, None] // 4) == np.arange(16)[None, :]).astype(np.float32)
    cm[:64, CM_SEL:CM_SEL + 16] = (s[:, None] == (4 * np.arange(16)[None, :] + 3)).astype(np.float32)
    return cm


def pack_params(inp):
    cpk = np.zeros((128, CP_N), np.float32)

    def fm(v, n):
        return np.ascontiguousarray(v.reshape(n, 128).T)

    for l in range(4):
        b = l * CPL
        cpk[:, b + 0:b + 8] = fm(inp["norm1_w"][l], 8)
        cpk[:, b + 8:b + 16] = fm(inp["norm2_w"][l], 8)
        for j in range(4):
            cpk[:, b + 16 + j * 12:b + 16 + (j + 1) * 12] = fm(inp["dn_conv_w"][l, j], 12)
            cpk[:, b + 64 + j * 4:b + 64 + (j + 1) * 4] = fm(inp["lru_conv_w"][l, j], 4)
        cpk[:, b + 80:b + 84] = fm(inp["lru_conv_b"][l], 4)
        cpk[:, b + 84:b + 88] = fm(inp["lru_ba"][l], 4)
        cpk[:, b + 88:b + 92] = fm(inp["lru_bx"][l], 4)
        cpk[:, b + 92:b + 96] = fm(inp["lru_lambda"][l], 4)
        for j in range(3):
            cpk[:, b + 96 + j * 24:b + 96 + (j + 1) * 24] = fm(inp["ffn_conv_w"][l, j], 24)
        cpk[:, b + 168] = inp["dn_norm_w"][l]
        cpk[:, CP_AB + l * 8:CP_AB + l * 8 + 4] = inp["dn_A_log"][l][None, :]
        cpk[:, CP_AB + l * 8 + 4:CP_AB + l * 8 + 8] = inp["dn_dt_bias"][l][None, :]
    cpk[:, CP_FIN:CP_FIN + 8] = fm(inp["final_norm_w"], 8)
    return cpk


_CACHE = {}


def kernel(**inp):
    depth = DEPTH
    if "nc" not in _CACHE:
        _CACHE["nc"] = build_program(depth)
    nc, _ = _CACHE["nc"]
    f = lambda a: np.ascontiguousarray(np.asarray(a, dtype=np.float32))
    cpk = pack_params({k: np.asarray(v) for k, v in inp.items()})
    cmk = make_consts()
    shared = {
        "w_in": f(inp["w_in"]),
        "lruw": np.ascontiguousarray(np.concatenate([f(inp["lru_wa"]), f(inp["lru_wx"])], axis=1)),
        "wbr": np.ascontiguousarray(np.stack([f(inp["w_branch_a"]), f(inp["w_branch_b"])], axis=1)),
        "w_o": f(inp["w_o"]),
        "ffn_w_in": f(inp["ffn_w_in"]), "ffn_w_down": f(inp["ffn_w_down"]),
        "cst": np.ascontiguousarray(np.concatenate([cpk, cmk], axis=1)),
    }
    in_maps = []
    for c in range(8):
        s0, s1 = c * NS, (c + 1) * NS
        m = dict(shared)
        m["xp"] = f(inp["x_prompt"][c])
        m["xs"] = f(inp["x_sample"][s0:s1]).reshape(64, D)
        m["sdc"] = f(inp["state_dn_conv"][:, s0:s1]).reshape(4, 48, 1536)
        m["sd"] = f(inp["state_dn"][:, s0:s1])
        m["slc"] = f(inp["state_lru_conv"][:, s0:s1]).reshape(4, 48, 512)
        m["slh"] = f(inp["state_lru"][:, s0:s1])
        m["sfc"] = f(inp["state_ffn_conv"][:, s0:s1]).reshape(4, 32, DFF)
        in_maps.append(m)
    res = run_bass_kernel_spmd(nc, in_maps, core_ids=list(range(8)))
    R = res.results
    y_p = np.stack([R[c]["o_yp"] for c in range(8)], 0)
    y_s = np.concatenate([R[c]["o_ys"].reshape(NS, 4, D) for c in range(8)], 0)
    PS = [R[c]["o_ps"] for c in range(8)]
    SS = [R[c]["o_ss"] for c in range(8)]
    p_dc = np.stack([PS[c][:, 0:4608].reshape(4, 3, 1536) for c in range(8)], 1)
    p_d = np.stack([R[c]["o_pd"] for c in range(8)], 1)
    p_lc = np.stack([PS[c][:, 4608:6144].reshape(4, 3, 512) for c in range(8)], 1)
    p_l = np.stack([PS[c][:, 6144:6656].reshape(4, 512) for c in range(8)], 1)
    p_fc = np.stack([PS[c][:, 6656:12800].reshape(4, 2, DFF) for c in range(8)], 1)
    s_dc = np.concatenate([SS[c][:, 0:73728].reshape(4, NS, 3, 1536) for c in range(8)], 1)
    s_d = np.concatenate([R[c]["o_sd"] for c in range(8)], 1)
    s_lc = np.concatenate([SS[c][:, 73728:98304].reshape(4, NS, 3, 512) for c in range(8)], 1)
    s_l = np.concatenate([SS[c][:, 98304:106496].reshape(4, NS, 512) for c in range(8)], 1)
    s_fc = np.concatenate([SS[c][:, 106496:204800].reshape(4, NS, 2, DFF) for c in range(8)], 1)
    outs = (y_p, y_s, p_dc, p_d, p_lc, p_l, p_fc, s_dc, s_d, s_lc, s_l, s_fc)
    return tuple(np.ascontiguousarray(o, dtype=np.float32) for o in outs)
```
